# Optimizing a Trainium2 kernel written in Bass

```python
import math
import jax
import jax.numpy as jnp
from jax import lax
import numpy as np


D_MODEL = 1024
BATCH = 8
SEQ = 4096
DEPTH = 2

HEAD_DIM = 64
ROPE_THETA = 10000.0
NORM_EPS = 1e-6
QBLK = 128

DSW_GROUPS = ((128, 1), (512, 4), (2048, 16))
DSW_HEADS_PER_GROUP = 4
DSW_HEADS = DSW_HEADS_PER_GROUP * len(DSW_GROUPS)
DSW_BLK = 128

MLA_HEADS = (3 * D_MODEL) // (4 * HEAD_DIM)
MLA_Q_RANK = D_MODEL // 4
MLA_KV_RANK = D_MODEL // 8
MLA_NOPE = HEAD_DIM
MLA_ROPE = HEAD_DIM // 2
MLA_V = HEAD_DIM

SSM_INNER = D_MODEL
SSM_HEADDIM = 64
SSM_HEADS = SSM_INNER // SSM_HEADDIM
SSM_STATE = 128
SSM_GROUPS = 2
SSM_CONV = 4
SSM_CHUNK = 128
SSM_CONV_DIM = SSM_INNER + 2 * SSM_GROUPS * SSM_STATE

SB_HEADS = 8
SB_WIDTH = SB_HEADS * HEAD_DIM

FFN_DIM = 2816
FFN_CONV = 3

IN0 = 3 * DSW_HEADS * HEAD_DIM + MLA_Q_RANK + MLA_KV_RANK + MLA_ROPE
OUT0 = DSW_HEADS_PER_GROUP * HEAD_DIM + MLA_HEADS * MLA_V
IN1 = SSM_INNER + SSM_CONV_DIM + SSM_HEADS + 3 * SB_WIDTH
OUT1 = SSM_INNER + SB_WIDTH

kernel_name = 'hybrid_dilated_mla_ssd_stickbreak'


def rmsnorm(x, g):
    xf = x.astype(jnp.float32)
    y = xf * lax.rsqrt(jnp.mean(xf * xf, axis=-1, keepdims=True) + NORM_EPS)
    return (y * g.astype(jnp.float32)).astype(x.dtype)


def group_rmsnorm(x, g, groups):
    shp = x.shape
    xf = x.astype(jnp.float32).reshape(shp[:-1] + (groups, shp[-1] // groups))
    y = xf * lax.rsqrt(jnp.mean(xf * xf, axis=-1, keepdims=True) + NORM_EPS)
    return y.reshape(shp) * g.astype(jnp.float32)


def rope(x, positions):
    d = x.shape[-1]
    half = d // 2
    inv_freq = 1.0 / (ROPE_THETA ** (jnp.arange(half, dtype=jnp.float32) * (2.0 / d)))
    ang = positions.astype(jnp.float32)[..., None] * inv_freq
    cos = jnp.cos(ang)[:, :, None, :]
    sin = jnp.sin(ang)[:, :, None, :]
    xf = x.astype(jnp.float32)
    x1, x2 = xf[..., :half], xf[..., half:]
    return jnp.concatenate([x1 * cos - x2 * sin, x2 * cos + x1 * sin], axis=-1).astype(x.dtype)


def causal_dwconv(x, w, b):
    k = w.shape[0]
    s = x.shape[1]
    xp = jnp.pad(x, ((0, 0), (k - 1, 0), (0, 0)))
    y = b
    for j in range(k):
        y = y + xp[:, j:j + s, :] * w[j]
    return y


def dilated_window_attention(q, k, v, window, dilation):
    bsz, s, h, dh = q.shape
    span = window // dilation
    lsub = s // dilation
    nb = -(-lsub // DSW_BLK)
    lp = nb * DSW_BLK

    def to_blocks(t):
        t = t.reshape(bsz, lsub, dilation, h, dh)
        t = jnp.pad(t, ((0, 0), (0, lp - lsub), (0, 0), (0, 0), (0, 0)))
        return t.reshape(bsz, nb, DSW_BLK, dilation, h, dh)

    def with_prev(t):
        prev = jnp.pad(t, ((0, 0), (1, 0), (0, 0), (0, 0), (0, 0), (0, 0)))[:, :-1]
        return jnp.concatenate([prev, t], axis=2)

    qb, kb, vb = to_blocks(q), to_blocks(k), to_blocks(v)
    kk, vv = with_prev(kb), with_prev(vb)
    scores = jnp.einsum('bnqrhd,bnkrhd->bnqrhk', qb, kk).astype(jnp.float32) * (dh ** -0.5)
    qi = jnp.arange(DSW_BLK)[:, None]
    kj = jnp.arange(2 * DSW_BLK)[None, :]
    rel = qi + DSW_BLK - kj
    key_sub = jnp.arange(nb)[:, None, None] * DSW_BLK + kj[None] - DSW_BLK
    valid = (rel >= 0) & (rel <= span) & (key_sub >= 0)
    scores = jnp.where(valid[None, :, :, None, None, :], scores, -jnp.inf)
    m = jnp.max(scores, axis=-1, keepdims=True)
    p = jnp.exp(scores - m)
    l = jnp.sum(p, axis=-1, keepdims=True)
    out = jnp.einsum('bnqrhk,bnkrhd->bnqrhd', (p / l).astype(v.dtype), vv)
    lse = (m + jnp.log(l))[..., 0]
    out = out.reshape(bsz, lp, dilation, h, dh)[:, :lsub].reshape(bsz, s, h, dh)
    lse = lse.reshape(bsz, lp, dilation, h)[:, :lsub].reshape(bsz, s, h)
    return out, lse


def causal_softmax_attention(q, k, v, scale):
    bsz, s, h, dq = q.shape
    nb = s // QBLK
    qb = q.reshape(bsz, nb, QBLK, h, dq).swapaxes(0, 1)
    key_idx = jnp.arange(s)

    def one_block(args):
        q_blk, i = args
        scores = jnp.einsum('bqhd,bkhd->bhqk', q_blk, k).astype(jnp.float32) * scale
        q_idx = i * QBLK + jnp.arange(QBLK)
        causal = key_idx[None, :] <= q_idx[:, None]
        probs = jax.nn.softmax(jnp.where(causal, scores, -jnp.inf), axis=-1)
        return jnp.einsum('bhqk,bkhd->bqhd', probs.astype(v.dtype), v)

    out = lax.map(one_block, (qb, jnp.arange(nb)))
    return out.swapaxes(0, 1).reshape(bsz, s, h, v.shape[-1])


def stick_breaking_attention(q, k, v):
    bsz, s, h, dh = q.shape
    nb = s // QBLK
    qb = q.reshape(bsz, nb, QBLK, h, dh).swapaxes(0, 1)
    key_idx = jnp.arange(s)

    def one_block(args):
        q_blk, i = args
        z = jnp.einsum('bqhd,bkhd->bhqk', q_blk, k).astype(jnp.float32) * (dh ** -0.5)
        q_idx = i * QBLK + jnp.arange(QBLK)
        strict = key_idx[None, :] < q_idx[:, None]
        log_stay = jnp.where(strict, jax.nn.log_sigmoid(-z), 0.0)
        after = lax.cumsum(log_stay, axis=3, reverse=True) - log_stay
        log_w = jnp.where(strict, jax.nn.log_sigmoid(z) + after, -jnp.inf)
        return jnp.einsum('bhqk,bkhd->bqhd', jnp.exp(log_w).astype(v.dtype), v)

    out = lax.map(one_block, (qb, jnp.arange(nb)))
    return out.swapaxes(0, 1).reshape(bsz, s, h, dh)


def ssd_chunked(x, dt, a, b_mat, c_mat):
    bsz, s, nh, p = x.shape
    g, n = b_mat.shape[2], b_mat.shape[3]
    hg = nh // g
    cl = SSM_CHUNK
    nc = s // cl
    f32 = jnp.float32
    xdt = (x.astype(f32) * dt[..., None]).reshape(bsz, nc, cl, g, hg, p)
    da = (dt * a).reshape(bsz, nc, cl, g, hg)
    bc = b_mat.astype(f32).reshape(bsz, nc, cl, g, n)
    cc = c_mat.astype(f32).reshape(bsz, nc, cl, g, n)
    cs = jnp.cumsum(da, axis=2)
    causal = jnp.tril(jnp.ones((cl, cl), dtype=bool))
    seg = cs[:, :, :, None] - cs[:, :, None, :]
    decay = jnp.exp(jnp.where(causal[None, None, :, :, None, None], seg, -jnp.inf))
    cb = jnp.einsum('bclgn,bcsgn->bclsg', cc, bc)
    y_diag = jnp.einsum('bclsgh,bcsghp->bclghp', cb[..., None] * decay, xdt)
    decay_end = jnp.exp(cs[:, :, -1:] - cs)
    states = jnp.einsum('bclgn,bclgh,bclghp->bcghpn', bc, decay_end, xdt)
    chunk_decay = jnp.exp(cs[:, :, -1])

    def step(hstate, inp):
        st, dec = inp
        return hstate * dec[..., None, None] + st, hstate

    h0 = jnp.zeros((bsz, g, hg, p, n), f32)
    _, prev = lax.scan(step, h0, (jnp.moveaxis(states, 1, 0), jnp.moveaxis(chunk_decay, 1, 0)))
    prev = jnp.moveaxis(prev, 0, 1)
    y_off = jnp.einsum('bclgn,bcghpn,bclgh->bclghp', cc, prev, jnp.exp(cs))
    return (y_diag + y_off).reshape(bsz, s, nh, p)


def even_mixer(h, positions, w_in, q_norm, w_uq, kv_norm, w_ukv, w_out):
    bsz, s, _ = h.shape
    proj = h @ w_in
    nd = DSW_HEADS * HEAD_DIM
    q = proj[..., :nd].reshape(bsz, s, DSW_HEADS, HEAD_DIM)
    k = proj[..., nd:2 * nd].reshape(bsz, s, DSW_HEADS, HEAD_DIM)
    v = proj[..., 2 * nd:3 * nd].reshape(bsz, s, DSW_HEADS, HEAD_DIM)
    o = 3 * nd
    c_q = proj[..., o:o + MLA_Q_RANK]
    o += MLA_Q_RANK
    c_kv = proj[..., o:o + MLA_KV_RANK]
    o += MLA_KV_RANK
    k_pe = proj[..., o:o + MLA_ROPE]

    q = rope(q, positions)
    k = rope(k, positions)
    outs, lses = [], []
    for gi, (window, dilation) in enumerate(DSW_GROUPS):
        sl = slice(gi * DSW_HEADS_PER_GROUP, (gi + 1) * DSW_HEADS_PER_GROUP)
        og, lg = dilated_window_attention(q[:, :, sl], k[:, :, sl], v[:, :, sl], window, dilation)
        outs.append(og)
        lses.append(lg)
    wts = jax.nn.softmax(jnp.stack(lses, axis=0), axis=0)
    y_a = jnp.sum(wts[..., None] * jnp.stack(outs, axis=0).astype(jnp.float32), axis=0).astype(h.dtype)

    qm = (rmsnorm(c_q, q_norm) @ w_uq).reshape(bsz, s, MLA_HEADS, MLA_NOPE + MLA_ROPE)
    q_nope, q_pe = qm[..., :MLA_NOPE], rope(qm[..., MLA_NOPE:], positions)
    kv = (rmsnorm(c_kv, kv_norm) @ w_ukv).reshape(bsz, s, MLA_HEADS, MLA_NOPE + MLA_V)
    k_nope, v_m = kv[..., :MLA_NOPE], kv[..., MLA_NOPE:]
    k_pe = rope(k_pe[:, :, None, :], positions)
    q_full = jnp.concatenate([q_nope, q_pe], axis=-1)
    k_full = jnp.concatenate([k_nope, jnp.broadcast_to(k_pe, (bsz, s, MLA_HEADS, MLA_ROPE))], axis=-1)
    y_b = causal_softmax_attention(q_full, k_full, v_m, (MLA_NOPE + MLA_ROPE) ** -0.5)

    y = jnp.concatenate([y_a.reshape(bsz, s, -1), y_b.reshape(bsz, s, -1)], axis=-1)
    return y @ w_out


def odd_mixer(h, w_in, conv_w, conv_b, dt_bias, a_log, d_skip, ssm_norm, w_out):
    bsz, s, _ = h.shape
    proj = h @ w_in
    z = proj[..., :SSM_INNER]
    o = SSM_INNER
    xbc = proj[..., o:o + SSM_CONV_DIM]
    o += SSM_CONV_DIM
    dt_raw = proj[..., o:o + SSM_HEADS]
    o += SSM_HEADS
    qkv = proj[..., o:]

    xbc = jax.nn.silu(causal_dwconv(xbc, conv_w, conv_b))
    gn = SSM_GROUPS * SSM_STATE
    xs = xbc[..., :SSM_INNER].reshape(bsz, s, SSM_HEADS, SSM_HEADDIM)
    bm = xbc[..., SSM_INNER:SSM_INNER + gn].reshape(bsz, s, SSM_GROUPS, SSM_STATE)
    cm = xbc[..., SSM_INNER + gn:].reshape(bsz, s, SSM_GROUPS, SSM_STATE)
    dt = jax.nn.softplus(dt_raw.astype(jnp.float32) + dt_bias.astype(jnp.float32))
    a = -jnp.exp(a_log.astype(jnp.float32))
    y = ssd_chunked(xs, dt, a, bm, cm)
    y = y + xs.astype(jnp.float32) * d_skip.astype(jnp.float32)[:, None]
    y = y.reshape(bsz, s, SSM_INNER) * jax.nn.silu(z.astype(jnp.float32))
    y_c = group_rmsnorm(y, ssm_norm, SSM_GROUPS).astype(h.dtype)

    q = qkv[..., :SB_WIDTH].reshape(bsz, s, SB_HEADS, HEAD_DIM)
    k = qkv[..., SB_WIDTH:2 * SB_WIDTH].reshape(bsz, s, SB_HEADS, HEAD_DIM)
    v = qkv[..., 2 * SB_WIDTH:].reshape(bsz, s, SB_HEADS, HEAD_DIM)
    y_d = stick_breaking_attention(q, k, v).reshape(bsz, s, SB_WIDTH)

    return jnp.concatenate([y_c, y_d], axis=-1) @ w_out


def conv_ffn(h, w_gate, w_up, conv_w, conv_b, w_down):
    gate = causal_dwconv(h @ w_gate, conv_w, conv_b)
    return (jax.nn.silu(gate) * (h @ w_up)) @ w_down


def setup_inputs(seed: int = 0) -> dict:
    key = jax.random.key(seed)
    ks = iter(jax.random.split(key, 32))
    f32 = jnp.float32

    def dense(fi, fo):
        return jax.random.normal(next(ks), (fi, fo), f32) * fi ** -0.5

    def gain(n):
        return 1.0 + 0.02 * jax.random.normal(next(ks), (n,), f32)

    def bias(n):
        return 0.02 * jax.random.normal(next(ks), (n,), f32)

    def dwconv(kw, c):
        return jax.random.normal(next(ks), (kw, c), f32) * kw ** -0.5

    def dt_bias_init(n):
        dt = jnp.exp(jax.random.uniform(next(ks), (n,), f32, math.log(1e-3), math.log(1e-1)))
        return dt + jnp.log(-jnp.expm1(-dt))

    def a_log_init(n):
        return jnp.log(jax.random.uniform(next(ks), (n,), f32, 1.0, 16.0))

    x = jax.random.normal(next(ks), (BATCH, SEQ, D_MODEL), f32)
    offset = jax.random.randint(next(ks), (BATCH, 1), 0, 1024, dtype=jnp.int32)
    positions = jnp.arange(SEQ, dtype=jnp.int32)[None, :] + offset
    return {
        'x': x,
        'positions': positions,
        'l0_norm_mix': gain(D_MODEL),
        'l0_w_in': dense(D_MODEL, IN0),
        'l0_mla_q_norm': gain(MLA_Q_RANK),
        'l0_mla_w_uq': dense(MLA_Q_RANK, MLA_HEADS * (MLA_NOPE + MLA_ROPE)),
        'l0_mla_kv_norm': gain(MLA_KV_RANK),
        'l0_mla_w_ukv': dense(MLA_KV_RANK, MLA_HEADS * (MLA_NOPE + MLA_V)),
        'l0_w_out': dense(OUT0, D_MODEL),
        'l0_norm_ffn': gain(D_MODEL),
        'l0_ffn_w_gate': dense(D_MODEL, FFN_DIM),
        'l0_ffn_w_up': dense(D_MODEL, FFN_DIM),
        'l0_ffn_conv_w': dwconv(FFN_CONV, FFN_DIM),
        'l0_ffn_conv_b': bias(FFN_DIM),
        'l0_ffn_w_down': dense(FFN_DIM, D_MODEL),
        'l1_norm_mix': gain(D_MODEL),
        'l1_w_in': dense(D_MODEL, IN1),
        'l1_ssm_conv_w': dwconv(SSM_CONV, SSM_CONV_DIM),
        'l1_ssm_conv_b': bias(SSM_CONV_DIM),
        'l1_ssm_dt_bias': dt_bias_init(SSM_HEADS),
        'l1_ssm_a_log': a_log_init(SSM_HEADS),
        'l1_ssm_d': gain(SSM_HEADS),
        'l1_ssm_norm': gain(SSM_INNER),
        'l1_w_out': dense(OUT1, D_MODEL),
        'l1_norm_ffn': gain(D_MODEL),
        'l1_ffn_w_gate': dense(D_MODEL, FFN_DIM),
        'l1_ffn_w_up': dense(D_MODEL, FFN_DIM),
        'l1_ffn_conv_w': dwconv(FFN_CONV, FFN_DIM),
        'l1_ffn_conv_b': bias(FFN_DIM),
        'l1_ffn_w_down': dense(FFN_DIM, D_MODEL),
        'final_norm': gain(D_MODEL),
    }


def reference(x, positions,
              l0_norm_mix, l0_w_in, l0_mla_q_norm, l0_mla_w_uq, l0_mla_kv_norm, l0_mla_w_ukv, l0_w_out,
              l0_norm_ffn, l0_ffn_w_gate, l0_ffn_w_up, l0_ffn_conv_w, l0_ffn_conv_b, l0_ffn_w_down,
              l1_norm_mix, l1_w_in, l1_ssm_conv_w, l1_ssm_conv_b, l1_ssm_dt_bias, l1_ssm_a_log, l1_ssm_d,
              l1_ssm_norm, l1_w_out,
              l1_norm_ffn, l1_ffn_w_gate, l1_ffn_w_up, l1_ffn_conv_w, l1_ffn_conv_b, l1_ffn_w_down,
              final_norm):
    mix_norms = [l0_norm_mix, l1_norm_mix]
    mixers = [
        (l0_w_in, l0_mla_q_norm, l0_mla_w_uq, l0_mla_kv_norm, l0_mla_w_ukv, l0_w_out),
        (l1_w_in, l1_ssm_conv_w, l1_ssm_conv_b, l1_ssm_dt_bias, l1_ssm_a_log, l1_ssm_d, l1_ssm_norm, l1_w_out),
    ]
    ffn_norms = [l0_norm_ffn, l1_norm_ffn]
    ffns = [
        (l0_ffn_w_gate, l0_ffn_w_up, l0_ffn_conv_w, l0_ffn_conv_b, l0_ffn_w_down),
        (l1_ffn_w_gate, l1_ffn_w_up, l1_ffn_conv_w, l1_ffn_conv_b, l1_ffn_w_down),
    ]
    for i in range(DEPTH):
        h = rmsnorm(x, mix_norms[i])
        if i % 2 == 0:
            x = x + even_mixer(h, positions, *mixers[i])
        else:
            x = x + odd_mixer(h, *mixers[i])
        x = x + conv_ffn(rmsnorm(x, ffn_norms[i]), *ffns[i])
    return rmsnorm(x, final_norm)
```

```python
import math
import os
from contextlib import ExitStack

import numpy as np
import concourse.bass as bass
import concourse.mybir as mybir
from concourse.bass_utils import run_bass_kernel_spmd

F32 = mybir.dt.float32
BF16 = mybir.dt.bfloat16
I32 = mybir.dt.int32
AF = mybir.ActivationFunctionType
ALU = mybir.AluOpType

S = 4096
DM = 1024
NT = 8
FF = 2816
NFC = 22
NEG = -30000.0
PI = math.pi


class Buf:
    __slots__ = ("name", "w", "rs", "excl")

    def __init__(self, name):
        self.name = name
        self.w = None
        self.rs = {}
        self.excl = False


class TB:
    def __init__(self, t, name):
        self.t = t
        self.b = Buf(name)
        self.name = name

    def __getitem__(self, k):
        return self.t[k]


class Prog:
    ENGS = ("sync", "scalar", "vector", "gpsimd", "tensor")

    def __init__(self, nc, es, npool=28, npool_sw=20):
        self.nc = nc
        self.lists = {e: [] for e in self.ENGS}
        self.cnt = {e: 0 for e in self.ENGS}
        self.seen = {e: {} for e in self.ENGS}
        self.sems = {e: es.enter_context(nc.semaphore("s_" + e)) for e in self.ENGS}
        self.pool = [[es.enter_context(nc.semaphore("d%d" % i)), 0, ("d", i)] for i in range(npool + npool_sw)]
        self.npool = npool
        self.keymap = {}
        self.pool_next = 0
        self.pool_next_sw = npool
        self.ninstr = 0

    def new_phase(self):
        self.keymap = {}
        self.pool_next = 0
        self.pool_next_sw = self.npool

    def _ent(self, key, eng):
        if key not in self.keymap:
            if eng == "gpsimd":
                assert self.pool_next_sw < len(self.pool), "out of sw dma sems"
                self.keymap[key] = self.pool[self.pool_next_sw]
                self.pool_next_sw += 1
            else:
                assert self.pool_next < self.npool, "out of dma sems"
                self.keymap[key] = self.pool[self.pool_next]
                self.pool_next += 1
        return self.keymap[key]

    def _wait(self, eng, key, sem, val):
        s = self.seen[eng]
        if s.get(key, 0) >= val:
            return
        s[key] = val
        self.lists[eng].append(lambda e, sem=sem, val=val: e.wait_ge(sem, val))

    def _dep(self, eng, dep, same_ok=False):
        if dep is None:
            return
        key, seq = dep
        if isinstance(key, str):
            if key == eng and same_ok:
                return
            self._wait(eng, key, self.sems[key], seq)
        else:
            self._wait(eng, key, self.pool[key[1]][0], seq)

    def op(self, eng, fn, reads=(), writes=(), inc=True):
        for b in reads:
            if b.w is not None:
                self._dep(eng, b.w, same_ok=(eng == "tensor"))
            if b.excl:
                for re_, rs_ in b.rs.items():
                    self._dep(eng, (re_, rs_), same_ok=True)
        pe = (eng == "tensor")
        for b in writes:
            if b.w is not None:
                self._dep(eng, b.w, same_ok=pe)
            for re_, rs_ in b.rs.items():
                self._dep(eng, (re_, rs_), same_ok=pe)
        if inc:
            self.cnt[eng] += 1
            seq = self.cnt[eng]
            sem = self.sems[eng]
            self.lists[eng].append(lambda e, fn=fn, sem=sem: fn(e).then_inc(sem, 1))
        else:
            seq = self.cnt[eng] + 1
            self.lists[eng].append(fn)
        self.ninstr += 1
        for b in reads:
            if seq > b.rs.get(eng, 0):
                b.rs[eng] = seq
        for b in writes:
            b.w = (eng, seq)
            b.rs = {}
        return seq

    def dma(self, eng, out, in_, reads=(), writes=(), key=None, group=False, slow=False):
        ent = self._ent(key, eng)
        k = ent[2]
        for b in reads:
            if b.w is not None:
                self._dep(eng, b.w)
        for b in writes:
            if b.w is not None and not (group and b.w[0] == k):
                self._dep(eng, b.w)
            for re_, rs_ in b.rs.items():
                self._dep(eng, (re_, rs_))
        ent[1] += 16
        seq = ent[1]
        sem = ent[0]
        self.lists[eng].append(
            lambda e, out=out, in_=in_, sem=sem, slow=slow: (
                e.dma_start(out=out, in_=in_, allow_slow_non_contiguous=True) if slow
                else e.dma_start(out=out, in_=in_)).then_inc(sem, 16))
        self.ninstr += 1
        for b in reads:
            if seq > b.rs.get(k, 0):
                b.rs[k] = seq
        for b in writes:
            b.w = (k, seq)
            b.rs = {}
        return seq

    def barrier_all(self):
        for eng in self.ENGS:
            for other in self.ENGS:
                if other != eng and self.cnt[other] > 0:
                    self._wait(eng, other, self.sems[other], self.cnt[other])
            for sem, cnt, k in self.pool:
                if cnt > 0:
                    self._wait(eng, k, sem, cnt)

    def emit(self):
        nc = self.nc
        lists = self.lists
        with nc.Block() as block:
            @block.sync
            def _(e):
                for f in lists["sync"]:
                    f(e)

            @block.scalar
            def _(e):
                for f in lists["scalar"]:
                    f(e)

            @block.vector
            def _(e):
                for f in lists["vector"]:
                    f(e)

            @block.gpsimd
            def _(e):
                for f in lists["gpsimd"]:
                    f(e)

            @block.tensor
            def _(e):
                for f in lists["tensor"]:
                    f(e)
        self.lists = {e: [] for e in self.ENGS}


def host_consts():
    c = {}
    kk = np.arange(128)[:, None]

    def mb(width, window, dil, strict=False):
        u = np.arange(width)[None, :]
        d = u - kk
        ok = (d >= (1 if strict else 0)) & (d <= window) & (d % dil == 0)
        return np.where(ok, 0.0, NEG).astype(np.float32)

    c["c_ident"] = np.eye(128, dtype=np.float32)
    c["c_mb0"] = mb(128 + 512, 128, 1)
    c["c_mb1"] = mb(512 + 512, 512, 4)
    c["c_mb2"] = mb(2048 + 512, 2048, 16)
    c["c_mbc"] = mb(512, 1 << 30, 1)
    c["c_mbs"] = mb(512, 1 << 30, 1, strict=True)
    u = np.arange(128)[None, :]
    c["c_sm"] = (u - kk > 0).astype(np.float32)
    c["c_tri"] = (kk <= u).astype(np.float32)
    c["c_negtri"] = -(kk >= u).astype(np.float32)
    cols = np.zeros((128, 8), np.float32)
    p = np.arange(128)
    cols[:, 0] = 10000.0 ** (-(p % 32) / 32.0)
    sg = np.where((p % 64) < 32, -1.0, 1.0)
    cols[:, 1] = sg
    cols[:, 2] = sg * 0.125
    pm = p - 64
    inm = (pm >= 0) & (pm < 32)
    cols[:, 3] = np.where(inm, 10000.0 ** (-(pm % 16) / 16.0), 0.0)
    sgm = np.where(pm < 16, -1.0, 1.0)
    cols[:, 4] = np.where(inm, sgm, 0.0)
    cols[:, 5] = np.where(inm, sgm * 96 ** -0.5, 0.0)
    c["c_cols"] = cols.astype(np.float32)
    return c


W_NAMES = [
    ("l0_norm_mix", [1024]), ("l0_w_in", [1024, 2720]), ("l0_mla_q_norm", [256]), ("l0_mla_w_uq", [256, 1152]),
    ("l0_mla_kv_norm", [128]), ("l0_mla_w_ukv", [128, 1536]), ("l0_w_out", [1024, 1024]), ("l0_norm_ffn", [1024]),
    ("l0_ffn_w_gate", [1024, 2816]), ("l0_ffn_w_up", [1024, 2816]), ("l0_ffn_conv_w", [3, 2816]),
    ("l0_ffn_conv_b", [2816]), ("l0_ffn_w_down", [2816, 1024]),
    ("l1_norm_mix", [1024]), ("l1_w_in", [1024, 4112]), ("l1_ssm_conv_w", [4, 1536]), ("l1_ssm_conv_b", [1536]),
    ("l1_ssm_dt_bias", [16]), ("l1_ssm_a_log", [16]), ("l1_ssm_d", [16]), ("l1_ssm_norm", [1024]),
    ("l1_w_out", [1536, 1024]), ("l1_norm_ffn", [1024]), ("l1_ffn_w_gate", [1024, 2816]),
    ("l1_ffn_w_up", [1024, 2816]), ("l1_ffn_conv_w", [3, 2816]), ("l1_ffn_conv_b", [2816]),
    ("l1_ffn_w_down", [2816, 1024]), ("final_norm", [1024]),
]


def build(stop_after=None, dbg=False):
    nc = bass.Bass("TRN2", target_bir_lowering=False)
    skind = "ExternalOutput" if dbg else "Internal"
    x_in = nc.dram_tensor("x", [S, DM], F32, kind="ExternalInput").ap()
    pos_t = nc.dram_tensor("positions", [S], I32, kind="ExternalInput")
    W = {n: nc.dram_tensor(n, shp, F32, kind="ExternalInput").ap() for n, shp in W_NAMES}
    C = {n: nc.dram_tensor(n, list(a.shape), F32, kind="ExternalInput").ap() for n, a in host_consts().items()}
    out_t = nc.dram_tensor("out", [S, DM], F32, kind="ExternalOutput").ap()

    def scratch(name, shape, dt):
        return nc.dram_tensor(name, shape, dt, kind=skind).ap()

    qt_dsw = scratch("qt_dsw", [6, 128, S], BF16)
    kt_dsw = scratch("kt_dsw", [6, 128, S], BF16)
    v_dsw = scratch("v_dsw", [12, 32, 128, 128], BF16)
    qt_mla = scratch("qt_mla", [12, 96, S], BF16)
    kt_mla = scratch("kt_mla", [12, 96, S], BF16)
    v_mla = scratch("v_mla", [12, 32, 128, 128], BF16)
    y0 = scratch("y0", [1024, S], BF16)
    x1 = scratch("x1", [S, DM], F32)
    act0 = scratch("act0", [NFC, 128, S], BF16)
    x2 = scratch("x2", [S, DM], F32)
    y1 = scratch("y1", [1536, S], BF16)
    qt_sb = scratch("qt_sb", [4, 128, S], BF16)
    kt_sb = scratch("kt_sb", [4, 128, S], BF16)
    v_sb = scratch("v_sb", [32, 128, 512], BF16)
    x3 = scratch("x3", [S, DM], F32)
    act1 = scratch("act1", [NFC, 128, S], BF16)

    es = ExitStack()
    P = Prog(nc, es)

    def I(eng, meth, *args, r=(), w=(), inc=True, **kw):
        return P.op(eng, lambda e: getattr(e, meth)(*args, **kw), r, w, inc=inc)

    def cp(eng, out, in_, r, w):
        if eng == "scalar":
            return I(eng, "copy", out, in_, r=r, w=w)
        return I(eng, "tensor_copy", out, in_, r=r, w=w)

    def mm(out, lhsT, rhs, start, stop, r, w, inc=None):
        return I("tensor", "matmul", out, lhsT, rhs, start=start, stop=stop, r=r, w=w,
                 inc=(stop if inc is None else inc))

    def bcast_row(ap1d, n):
        return ap1d.partition_broadcast(128)

    def colvec(ap1d, nchunk):
        t = ap1d.tensor
        off = ap1d.offset
        return bass.AP(t, off, [[1, 128], [128, nchunk], [1, 1]])

    class Phase:
        def __init__(self, name):
            self.name = name
            self.st = ExitStack()
            P.new_phase()
            self.pp = []
            self.pi = 0

        def sb(self, n, s, d):
            return TB(self.st.enter_context(nc.sbuf_tensor(self.name + "_" + n, s, d)), self.name + "_" + n)

        def ps(self, n, s, d):
            t = TB(self.st.enter_context(nc.psum_tensor(self.name + "_" + n, s, d)), self.name + "_" + n)
            t.b.excl = True
            return t

        def pspool(self, k):
            self.pp = [self.ps("pp%d" % i, [128, 512], F32) for i in range(k)]

        def nxt(self):
            t = self.pp[self.pi % len(self.pp)]
            self.pi += 1
            return t

        def load_w(self, tb, out, in_, group=False):
            P.dma("gpsimd", out, in_, writes=[tb.b], key=tb.name, group=group)

        def load(self, tb, out, in_, group=False, slow=False):
            P.dma("sync", out, in_, writes=[tb.b], key=tb.name, group=group, slow=slow)

        def store(self, tb, out, in_):
            P.dma("sync", out, in_, reads=[tb.b], key=tb.name + "_st")

        def close(self):
            print("phase", self.name, "instr", P.ninstr, dict(P.cnt))
            P.barrier_all()
            P.emit()
            self.st.close()

    def rms_rows(x, h, gb, junk, ss, J, width=1024, eps=1e-6):
        I("vector", "memset", ss[:, 0:J], 0.0, w=[ss.b])
        for j in range(J):
            I("scalar", "activation", junk[:, 0:width], x[:, j, 0:width], AF.Square, accum_out=ss[:, j:j + 1],
              r=[x.b, ss.b], w=[junk.b, ss.b])
        I("vector", "tensor_scalar", ss[:, 0:J], ss[:, 0:J], 1.0 / width, eps, ALU.mult, ALU.add, r=[ss.b], w=[ss.b])
        I("scalar", "activation", ss[:, 0:J], ss[:, 0:J], AF.Ln, r=[ss.b], w=[ss.b])
        I("scalar", "activation", ss[:, 0:J], ss[:, 0:J], AF.Exp, scale=-0.5, r=[ss.b], w=[ss.b])
        for j in range(J):
            I("vector", "scalar_tensor_tensor", h[:, j, 0:width], x[:, j, 0:width], ss[:, j:j + 1], gb[:, 0:width],
              ALU.mult, ALU.mult, r=[x.b, ss.b, gb.b], w=[h.b])

    def transpose_rows(h, hT, nchunk, J, tps, ident):
        for c in range(nchunk):
            tp = tps[c % 2]
            for j in range(J):
                I("tensor", "transpose", tp[:, j * 128:(j + 1) * 128], h[:, j, c * 128:(c + 1) * 128], ident[:],
                  r=[h.b, ident.b], w=[tp.b], inc=(j == J - 1))
            cp("vector" if c % 2 == 0 else "scalar", hT[:, c, 0:J * 128], tp[:, 0:J * 128], r=[tp.b], w=[hT.b])

    def proj_fm(pt, Wt, col0, M, hT, nK, N=512):
        for c in range(nK):
            mm(pt[0:M, 0:N], Wt[:, c, col0:col0 + M], hT[:, c, 0:N], c == 0, c == nK - 1, r=[Wt.b, hT.b], w=[pt.b])

    def phase_A():
        ph = Phase("A")
        sb, ps = ph.sb, ph.ps
        ident = sb("ident", [128, 128], BF16)
        ph.load_w(ident, ident[:], C["c_ident"])
        cols = sb("cols", [128, 8], F32)
        ph.load(cols, cols[:], C["c_cols"])
        ones32 = sb("ones32", [128, 128], F32)
        I("vector", "memset", ones32[:], 1.0, w=[ones32.b])
        gb = sb("gb", [128, 1024], F32)
        ph.load(gb, gb[:], bcast_row(W["l0_norm_mix"], 1024))
        gq = sb("gq", [128, 2, 1], F32)
        ph.load(gq, gq[:], colvec(W["l0_mla_q_norm"], 2), slow=True)
        gkv = sb("gkv", [128, 1, 1], F32)
        ph.load(gkv, gkv[:], colvec(W["l0_mla_kv_norm"], 1), slow=True)
        win = W["l0_w_in"].rearrange("(c p) n -> p c n", p=128)
        Wqk = sb("Wqk", [128, 8, 1536], BF16)
        for c in range(8):
            ph.load_w(Wqk, Wqk[:, c, :], win[:, c, 0:1536], group=True)
        Wqkr = sb("Wqkr", [128, 8, 1536], BF16)
        v_src = Wqk.t[:].rearrange("p c (h d) -> p (c h) d", d=64)
        v_dst = Wqkr.t[:].rearrange("p c (h d) -> p (c h) d", d=64)
        cp("vector", v_dst[:, :, 0:32], v_src[:, :, 32:64], r=[Wqk.b], w=[Wqkr.b])
        cp("gpsimd", v_dst[:, :, 32:64], v_src[:, :, 0:32], r=[Wqk.b], w=[Wqkr.b])
        Wv = sb("Wv", [128, 8, 768], BF16)
        ph.load_w(Wv, Wv[:], win[:, :, 1536:2304])
        Wc = sb("Wc", [128, 8, 384], BF16)
        ph.load_w(Wc, Wc[:], win[:, :, 2304:2688])
        Wkpe = sb("Wkpe", [128, 8, 96], BF16)
        I("vector", "memset", Wkpe[:], 0.0, w=[Wkpe.b])
        ph.load_w(Wkpe, Wkpe[:, :, 64:96], win[:, :, 2688:2720])
        Wkper = sb("Wkper", [128, 8, 96], BF16)
        I("vector", "memset", Wkper[:], 0.0, w=[Wkper.b])
        cp("vector", Wkper[:, :, 64:80], Wkpe[:, :, 80:96], r=[Wkpe.b], w=[Wkper.b])
        cp("vector", Wkper[:, :, 80:96], Wkpe[:, :, 64:80], r=[Wkpe.b], w=[Wkper.b])
        Wuq = sb("Wuq", [128, 2, 1152], BF16)
        ph.load_w(Wuq, Wuq[:], W["l0_mla_w_uq"].rearrange("(c p) n -> p c n", p=128))
        Wuqr = sb("Wuqr", [128, 2, 1152], BF16)
        u_src = Wuq.t[:].rearrange("p c (h d) -> p (c h) d", d=96)
        u_dst = Wuqr.t[:].rearrange("p c (h d) -> p (c h) d", d=96)
        cp("vector", u_dst[:, :, 0:64], u_src[:, :, 0:64], r=[Wuq.b], w=[Wuqr.b])
        cp("vector", u_dst[:, :, 64:80], u_src[:, :, 80:96], r=[Wuq.b], w=[Wuqr.b])
        cp("vector", u_dst[:, :, 80:96], u_src[:, :, 64:80], r=[Wuq.b], w=[Wuqr.b])
        Wukv = sb("Wukv", [128, 1536], BF16)
        ph.load_w(Wukv, Wukv[:], W["l0_mla_w_ukv"])
        Wukv4 = Wukv.t[:].rearrange("p (h d) -> p h d", d=128)
        Wvm = sb("Wvm", [128, 768], BF16)
        cp("gpsimd", Wvm.t[:].rearrange("p (h d) -> p h d", d=64), Wukv4[:, :, 64:128], r=[Wukv.b], w=[Wvm.b])

        x32 = sb("x32", [128, 4, 1024], F32)
        h = sb("h", [128, 4, 1024], BF16)
        hT = sb("hT", [128, 8, 512], BF16)
        junk = sb("junk", [128, 1024], F32)
        ss = sb("ss", [128, 4], F32)
        pos_i = sb("pos_i", [128, 512], I32)
        pos_f = sb("pos_f", [128, 512], F32)
        cosk = sb("cosk", [128, 512], F32)
        sink = sb("sink", [128, 512], F32)
        cosq = sb("cosq", [128, 512], F32)
        sinq = sb("sinq", [128, 512], F32)
        cosmk = sb("cosmk", [128, 512], F32)
        sinmk = sb("sinmk", [128, 512], F32)
        cosmq = sb("cosmq", [128, 512], F32)
        sinmq = sb("sinmq", [128, 512], F32)
        t1 = [sb("t1_%d" % i, [128, 512], F32) for i in range(2)]
        t2 = [sb("t2_%d" % i, [128, 512], F32) for i in range(2)]
        ang, r1, r2, sinr = t1[0], t1[1], t2[0], t2[1]
        qk_st = [sb("qkst%d" % i, [128, 512], BF16) for i in range(2)]
        Vst = sb("Vst", [128, 4, 1536], BF16)
        Vmst = Vst
        Vst5 = Vst.t[:].rearrange("p j (h d) -> p j h d", d=128)
        Vmst5 = Vmst.t[:].rearrange("p j (h d) -> p j h d", d=128)
        I("vector", "memset", Vst[:], 1.0, w=[Vst.b])
        cT32 = sb("cT32", [128, 3, 512], F32)
        sq32 = sb("sq32", [128, 3, 512], F32)
        rstd = sb("rstd", [128, 2, 512], F32)
        cqn = sb("cqn", [128, 2, 512], BF16)
        ckvn = sb("ckvn", [128, 512], BF16)
        kpeT = sb("kpeT", [128, 512], BF16)
        Qst = [sb("Qst%d" % i, [128, 512], BF16) for i in range(2)]
        Kst = [sb("Kst%d" % i, [128, 512], BF16) for i in range(2)]
        tps = [ps("tp%d" % i, [128, 1024], BF16) for i in range(2)]
        ph.pspool(6)
        SC96 = 96 ** -0.5
        SSC = 1.0 - 2e-4

        hTs = [hT, sb("hT1", [128, 8, 512], BF16)]

        def a_loads(tt):
            tsl = slice(tt * 512, (tt + 1) * 512)
            hTc = hTs[tt % 2]
            ph.load(x32, x32[:], x_in[tsl, :].rearrange("(j p) d -> p j d", p=128))
            ph.load(pos_i, pos_i[:], bass.AP(pos_t, tt * 512, [[0, 128], [1, 512]]))

        def a_norm(tt):
            tsl = slice(tt * 512, (tt + 1) * 512)
            hTc = hTs[tt % 2]
            rms_rows(x32, h, gb, junk, ss, 4)

        def a_trans(tt):
            tsl = slice(tt * 512, (tt + 1) * 512)
            hTc = hTs[tt % 2]
            transpose_rows(h, hTc, 8, 4, tps, ident)
            cp("vector", pos_f[:], pos_i[:], r=[pos_i.b], w=[pos_f.b])
            for (icol, tabs) in ((0, (sink, sinq, cosk, cosq, 1, 2, 0.125)), (3, (sinmk, sinmq, cosmk, cosmq, 4, 5, SC96))):
                sk_, sq_, ck_, cq_, c1, c2, qs = tabs
                I("vector", "tensor_scalar", ang[:], pos_f[:], cols[:, icol:icol + 1], None, ALU.mult,
                  r=[pos_f.b, cols.b], w=[ang.b])
                for (rr, off, dst) in ((r1, 0.0, sinr), (r2, 0.25, ck_)):
                    I("vector", "tensor_scalar", rr[:], ang[:], 1.0 / (2 * PI), off, ALU.mult, ALU.add,
                      r=[ang.b], w=[rr.b])
                    cp("vector", pos_i[:], rr[:], r=[rr.b], w=[pos_i.b])
                    cp("vector", rr[:], pos_i[:], r=[pos_i.b], w=[rr.b])
                    I("vector", "scalar_tensor_tensor", rr[:], rr[:], -2 * PI, ang[:], ALU.mult, ALU.add,
                      r=[rr.b, ang.b], w=[rr.b])
                    I("scalar", "activation", dst[:], rr[:], AF.Sin, scale=SSC, bias=off * 2 * PI * SSC,
                      r=[rr.b], w=[dst.b])
                I("vector", "tensor_scalar", sk_[:], sinr[:], cols[:, c1:c1 + 1], None, ALU.mult,
                  r=[sinr.b, cols.b], w=[sk_.b])
                I("vector", "tensor_scalar", sq_[:], sinr[:], cols[:, c2:c2 + 1], None, ALU.mult,
                  r=[sinr.b, cols.b], w=[sq_.b])
                I("vector", "tensor_scalar", cq_[:], ck_[:], qs, None, ALU.mult, r=[ck_.b], w=[cq_.b])

        def a_p1(tt):
            tsl = slice(tt * 512, (tt + 1) * 512)
            hTc = hTs[tt % 2]
            for i in range(12):
                pq, pqr = ph.nxt(), ph.nxt()
                proj_fm(pq, Wqk, i * 128, 128, hTc, 8)
                proj_fm(pqr, Wqkr, i * 128, 128, hTc, 8)
                cs_, sn_ = (cosq, sinq) if i < 6 else (cosk, sink)
                a1, a2, st = t1[i % 2], t2[i % 2], qk_st[i % 2]
                I("vector", "tensor_tensor", a1[:], pq[:, 0:512], cs_[:], ALU.mult, r=[pq.b, cs_.b], w=[a1.b])
                I("vector", "tensor_tensor", a2[:], pqr[:, 0:512], sn_[:], ALU.mult, r=[pqr.b, sn_.b], w=[a2.b])
                I("gpsimd", "tensor_tensor", st[:], a1[:], a2[:], ALU.add, r=[a1.b, a2.b], w=[st.b])
                dst = (qt_dsw if i < 6 else kt_dsw)[i % 6, :, tsl]
                ph.store(st, dst, st[:])

        def a_p2(tt):
            tsl = slice(tt * 512, (tt + 1) * 512)
            hTc = hTs[tt % 2]
            for j in range(4):
                for (c0, n) in ((0, 512), (512, 256)):
                    pv = ph.nxt()
                    for c in range(8):
                        mm(pv[:, 0:n], hTc[:, c, j * 128:(j + 1) * 128], Wv[:, c, c0:c0 + n], c == 0, c == 7,
                           r=[hTc.b, Wv.b], w=[pv.b])
                    cp("scalar", Vst5[:, j, c0 // 64:(c0 + n) // 64, 0:64],
                       pv.t[:, 0:n].rearrange("p (h d) -> p h d", d=64), r=[pv.b], w=[Vst.b])
            for j in range(4):
                ph.store(Vst, v_dsw[:, tt * 4 + j].rearrange("h p d -> p h d"), Vst5[:, j])
            for i in range(3):
                pc = ph.nxt()
                proj_fm(pc, Wc, i * 128, 128, hTc, 8)
                cp("vector", cT32[:, i, :], pc[:, 0:512], r=[pc.b], w=[cT32.b])
                I("scalar", "activation", sq32[:, i, :], pc[:, 0:512], AF.Square, r=[pc.b], w=[sq32.b])
            pssq, psskv = ph.nxt(), ph.nxt()
            mm(pssq[:, 0:512], ones32[:], sq32[:, 0, :], True, False, r=[ones32.b, sq32.b], w=[pssq.b])
            mm(pssq[:, 0:512], ones32[:], sq32[:, 1, :], False, True, r=[ones32.b, sq32.b], w=[pssq.b])
            mm(psskv[:, 0:512], ones32[:], sq32[:, 2, :], True, True, r=[ones32.b, sq32.b], w=[psskv.b])
            I("vector", "tensor_scalar", rstd[:, 0, :], pssq[:, 0:512], 1.0 / 256, 1e-6, ALU.mult, ALU.add,
              r=[pssq.b], w=[rstd.b])
            I("vector", "tensor_scalar", rstd[:, 1, :], psskv[:, 0:512], 1.0 / 128, 1e-6, ALU.mult, ALU.add,
              r=[psskv.b], w=[rstd.b])
            I("scalar", "activation", rstd[:], rstd[:], AF.Ln, r=[rstd.b], w=[rstd.b])
            I("scalar", "activation", rstd[:], rstd[:], AF.Exp, scale=-0.5, r=[rstd.b], w=[rstd.b])
            for c in range(2):
                I("vector", "scalar_tensor_tensor", cqn[:, c, :], cT32[:, c, :], gq[:, c, :], rstd[:, 0, :],
                  ALU.mult, ALU.mult, r=[cT32.b, gq.b, rstd.b], w=[cqn.b])
            I("vector", "scalar_tensor_tensor", ckvn[:], cT32[:, 2, :], gkv[:, 0, :], rstd[:, 1, :],
              ALU.mult, ALU.mult, r=[cT32.b, gkv.b, rstd.b], w=[ckvn.b])
            for hh in range(12):
                pm, pmr = ph.nxt(), ph.nxt()
                for c in range(2):
                    mm(pm[0:96, 0:512], Wuq[:, c, hh * 96:(hh + 1) * 96], cqn[:, c, :], c == 0, c == 1,
                       r=[Wuq.b, cqn.b], w=[pm.b])
                for c in range(2):
                    mm(pmr[0:96, 0:512], Wuqr[:, c, hh * 96:(hh + 1) * 96], cqn[:, c, :], c == 0, c == 1,
                       r=[Wuqr.b, cqn.b], w=[pmr.b])
                st, a1, a2 = Qst[hh % 2], t1[hh % 2], t2[hh % 2]
                I("scalar", "mul", st[0:64, :], pm[0:64, 0:512], SC96, r=[pm.b], w=[st.b])
                I("vector", "tensor_tensor", a1[64:96, :], pm[64:96, 0:512], cosmq[64:96, :], ALU.mult,
                  r=[pm.b, cosmq.b], w=[a1.b])
                I("vector", "tensor_tensor", a2[64:96, :], pmr[64:96, 0:512], sinmq[64:96, :], ALU.mult,
                  r=[pmr.b, sinmq.b], w=[a2.b])
                I("gpsimd", "tensor_tensor", st[64:96, :], a1[64:96, :], a2[64:96, :], ALU.add,
                  r=[a1.b, a2.b], w=[st.b])
                ph.store(st, qt_mla[hh, :, tsl], st[0:96, :])
            pk, pkr = ph.nxt(), ph.nxt()
            proj_fm(pk, Wkpe, 0, 96, hTc, 8)
            proj_fm(pkr, Wkper, 0, 96, hTc, 8)
            a1, a2 = t1[0], t2[0]
            I("vector", "tensor_tensor", a1[64:96, :], pk[64:96, 0:512], cosmk[64:96, :], ALU.mult,
              r=[pk.b, cosmk.b], w=[a1.b])
            I("vector", "tensor_tensor", a2[64:96, :], pkr[64:96, 0:512], sinmk[64:96, :], ALU.mult,
              r=[pkr.b, sinmk.b], w=[a2.b])
            I("gpsimd", "tensor_tensor", kpeT[64:96, :], a1[64:96, :], a2[64:96, :], ALU.add,
              r=[a1.b, a2.b], w=[kpeT.b])
            for hh in range(12):
                pkn = ph.nxt()
                mm(pkn[0:64, 0:512], Wukv4[:, hh, 0:64], ckvn[:], True, True, r=[Wukv.b, ckvn.b], w=[pkn.b])
                st = Kst[hh % 2]
                cp("scalar", st[0:64, :], pkn[0:64, 0:512], r=[pkn.b], w=[st.b])
                cp("gpsimd", st[64:96, :], kpeT[64:96, :], r=[kpeT.b], w=[st.b])
                ph.store(st, kt_mla[hh, :, tsl], st[0:96, :])
            for j in range(4):
                for (c0, n) in ((0, 512), (512, 256)):
                    pv = ph.nxt()
                    mm(pv[:, 0:n], ckvn[:, j * 128:(j + 1) * 128], Wvm[:, c0:c0 + n], True, True,
                       r=[ckvn.b, Wvm.b], w=[pv.b])
                    cp("scalar", Vmst5[:, j, c0 // 64:(c0 + n) // 64, 0:64],
                       pv.t[:, 0:n].rearrange("p (h d) -> p h d", d=64), r=[pv.b], w=[Vmst.b])
            for j in range(4):
                ph.store(Vmst, v_mla[:, tt * 4 + j].rearrange("h p d -> p h d"), Vmst5[:, j])

        a_loads(0)
        a_norm(0)
        a_trans(0)
        for tt in range(NT):
            nxt_ = tt + 1 < NT
            if nxt_:
                a_loads(tt + 1)
            a_p1(tt)
            if nxt_:
                a_norm(tt + 1)
            a_p2(tt)
            if nxt_:
                a_trans(tt + 1)
        ph.close()

    def attn_softmax(pname, units, ydstT, prefetch=None):
        ph = Phase(pname)
        sb, ps = ph.sb, ph.ps
        ident = sb("ident", [128, 128], BF16)
        ph.load_w(ident, ident[:], C["c_ident"])
        masks = {}
        for u in units:
            for shd in u["subs"]:
                mn = shd["mask"]
                if mn not in masks:
                    wdt = C[mn].shape[1]
                    masks[mn] = sb(mn, [128, wdt], BF16)
                    ph.load_w(masks[mn], masks[mn][:], C[mn])
        nsub = max(len(u["subs"]) for u in units)
        KT = [[sb("KT%d_%d" % (i, s_), [128, S], BF16) for s_ in range(nsub)] for i in range(2)]
        QT = [[sb("QT%d_%d" % (i, s_), [128, S], BF16) for s_ in range(nsub)] for i in range(2)]
        VH = [[sb("VH%d_%d" % (i, s_), [128, 32, 128], BF16) for s_ in range(nsub)] for i in range(2)]
        dk0 = units[0]["subs"][0]["dk"]
        for i in range(2):
            for s_ in range(nsub):
                I("gpsimd", "memset", KT[i][s_][64:128, :], 0.0, w=[KT[i][s_].b])
                I("vector", "memset", QT[i][s_][64:128, :], 0.0, w=[QT[i][s_].b])
        pts = [sb("pt%d" % i, [128, 512], BF16) for i in range(5)]
        sps = [ps("sp%d" % i, [128, 512], F32) for i in range(5)]
        accs = [ps("acc%d" % i, [128, 512], F32) for i in range(2)]
        rden = [sb("rden%d" % i, [64, 512], F32) for i in range(2)]
        yst = [sb("yst%d" % i, [64, 512], BF16) for i in range(2)]
        it = 0
        gi = 0
        pend = []

        def flush(keep):
            while len(pend) > keep:
                pend.pop(0)()

        def load_unit(ui_):
            u_ = units[ui_]
            b_ = ui_ % 2
            for si, shd in enumerate(u_["subs"]):
                dk = shd["dk"]
                ph.load(KT[b_][si], KT[b_][si][0:dk, :], shd["kt"])
                ph.load(QT[b_][si], QT[b_][si][0:dk, :], shd["qt"])
                ph.load(VH[b_][si], VH[b_][si][:], shd["v"].rearrange("j p d -> p j d"))

        load_unit(0)
        if prefetch is not None:
            prefetch(ph)
        for ui, u in enumerate(units):
            bi = ui % 2
            for G in range(8):
                if G == 1 and ui + 1 < len(units):
                    load_unit(ui + 1)
                contrib = []
                for si, shd in enumerate(u["subs"]):
                    wb = shd["wb"]
                    for j in range(0, 4 * G + 4):
                        qlo = max(4 * G, j)
                        qhi = 4 * G + 3 if wb is None else min(4 * G + 3, j + wb)
                        if qhi >= qlo:
                            contrib.append((si, j, qlo, qhi))
                acc = accs[gi % 2]
                rd, ys = rden[gi % 2], yst[gi % 2]
                gi += 1
                col = u["col"]
                for idx, (si, j, qlo, qhi) in enumerate(contrib):
                    shd = u["subs"][si]
                    dk = shd["dk"]
                    sp, pt = sps[it % 5], pts[it % 5]
                    it += 1
                    c0 = (qlo - 4 * G) * 128
                    n = (qhi - qlo + 1) * 128
                    need_mask = shd["always"] or (qlo == j)
                    kt_, qt_, vh_ = KT[bi][si], QT[bi][si], VH[bi][si]
                    mm(sp[:, c0:c0 + n], kt_[:, j * 128:(j + 1) * 128], qt_[:, qlo * 128:(qhi + 1) * 128],
                       True, not need_mask, r=[kt_.b, qt_.b], w=[sp.b])
                    if need_mask:
                        mt = masks[shd["mask"]]
                        u0 = (qlo - j) * 128
                        mm(sp[:, c0:c0 + n], ident[:], mt[:, u0:u0 + n], False, True, r=[ident.b, mt.b], w=[sp.b])
                    I("scalar", "activation", pt[:, c0:c0 + n], sp[:, c0:c0 + n], AF.Exp, r=[sp.b], w=[pt.b])
                    last = idx == len(contrib) - 1

                    def st3(acc=acc, c0=c0, n=n, vh_=vh_, j=j, pt=pt, idx=idx, last=last, rd=rd, ys=ys, col=col, G=G):
                        I("tensor", "matmul", acc[:, c0:c0 + n], vh_[:, j, :], pt[:, c0:c0 + n], start=(idx == 0),
                          stop=last, skip_group_check=True, r=[pt.b, vh_.b], w=[acc.b], inc=last)
                        if last:
                            I("vector", "reciprocal", rd[:], acc[64:128, 0:512], r=[acc.b], w=[rd.b])
                            I("vector", "tensor_tensor", ys[:], acc[0:64, 0:512], rd[:], ALU.mult,
                              r=[acc.b, rd.b], w=[ys.b])
                            ph.store(ys, ydstT[col:col + 64, G * 512:(G + 1) * 512], ys[:])
                    pend.append(st3)
                    flush(2)
        flush(0)
        ph.close()

    def vsrc(vt, hh):
        return vt.rearrange("j p (h d) -> p j h d", d=65)[:, :, hh, :]

    def phase_B_dsw():
        units = []
        wbs = (1, 4, 16)
        for jh in range(4):
            subs = []
            for g in range(3):
                hh = g * 4 + jh
                subs.append(dict(kt=kt_dsw[hh // 2, (hh % 2) * 64:(hh % 2) * 64 + 64, :],
                                 qt=qt_dsw[hh // 2, (hh % 2) * 64:(hh % 2) * 64 + 64, :],
                                 v=v_dsw[hh], dk=64, mask="c_mb%d" % g, wb=wbs[g], always=True))
            units.append(dict(subs=subs, col=jh * 64))
        attn_softmax("Bd", units, y0)

    def phase_B_mla(prefetch=None):
        units = []
        for hh in range(12):
            subs = [dict(kt=kt_mla[hh], qt=qt_mla[hh], v=v_mla[hh], dk=96, mask="c_mbc", wb=None, always=False)]
            units.append(dict(subs=subs, col=256 + hh * 64))
        attn_softmax("Bm", units, y0, prefetch=prefetch)


    PRE = {}

    def pre_alloc(stack, name, shape, dt):
        t = TB(stack.enter_context(nc.sbuf_tensor("pre_" + name, shape, dt)), "pre_" + name)
        PRE[name] = t
        return t

    def pre_load_C(ph, L, which):
        pre = "l%d_" % L
        if "Wout" in which:
            t = PRE["C%d_Wout" % L]
            ph.load_w(t, t[:], W[pre + "w_out"].rearrange("(c p) n -> p c n", p=128))
        wgv = W[pre + "ffn_w_gate"].rearrange("(c p) n -> p c n", p=128)
        wuv = W[pre + "ffn_w_up"].rearrange("(c p) n -> p c n", p=128)
        for c in range(8):
            ph.load_w(PRE["C%d_Wg" % L], PRE["C%d_Wg" % L][:, c, :], wgv[:, c, :], group=True)
            ph.load_w(PRE["C%d_Wu" % L], PRE["C%d_Wu" % L][:, c, :], wuv[:, c, :], group=True)

    def pre_load_D(ph):
        wv = W["l1_w_in"].rearrange("(c p) n -> p c n", p=128)
        t = PRE["D_Win"]
        for c in range(8):
            ph.load_w(t, t[:, c, :], wv[:, c, :], group=True)

    def phase_C(L):
        ph = Phase("C%d" % L)
        sb, ps = ph.sb, ph.ps
        pre = "l%d_" % L
        ysrc, nyc = (y0, 8) if L == 0 else (y1, 12)
        xsrc = x_in if L == 0 else x2
        xdst = x1 if L == 0 else x3
        actd = act0 if L == 0 else act1
        ident = sb("ident", [128, 128], BF16)
        ph.load_w(ident, ident[:], C["c_ident"])
        gb = sb("gb", [128, 1024], F32)
        ph.load(gb, gb[:], bcast_row(W[pre + "norm_ffn"], 1024))
        cw = sb("cw", [128, NFC, 4], F32)
        for jj in range(3):
            ph.load(cw, cw[:, :, jj:jj + 1], colvec(W[pre + "ffn_conv_w"][jj], NFC), group=True, slow=True)
        ph.load(cw, cw[:, :, 3:4], colvec(W[pre + "ffn_conv_b"], NFC), group=True, slow=True)
        if ("C%d_Wout" % L) in PRE:
            Wout = PRE["C%d_Wout" % L]
        else:
            Wout = sb("Wout", [128, nyc, 1024], BF16)
            ph.load_w(Wout, Wout[:], W[pre + "w_out"].rearrange("(c p) n -> p c n", p=128))
        Wg, Wu = PRE["C%d_Wg" % L], PRE["C%d_Wu" % L]
        yT = sb("yT", [128, nyc, 512], BF16)
        x32 = sb("x32", [128, 4, 1024], F32)
        h = sb("h", [128, 4, 1024], BF16)
        hT = sb("hT", [128, 8, 512], BF16)
        junk = sb("junk", [128, 1024], F32)
        ss = sb("ss", [128, 4], F32)
        halo = sb("halo", [128, NFC, 2], F32)
        I("vector", "memset", halo[:], 0.0, w=[halo.b])
        gsb = [sb("gsb%d" % i, [128, 514], F32) for i in range(2)]
        acc = [sb("acc%d" % i, [128, 512], F32) for i in range(2)]
        sil = [sb("sil%d" % i, [128, 512], F32) for i in range(2)]
        ast = [sb("ast%d" % i, [128, 512], BF16) for i in range(2)]
        tps = [ps("tp%d" % i, [128, 1024], BF16) for i in range(2)]
        ph.pspool(6)
        hTs = [hT, sb("hT1", [128, 8, 512], BF16)]

        def tslice(tt):
            return slice(tt * 512, (tt + 1) * 512)

        def loads(tt):
            tsl = tslice(tt)
            ph.load(yT, yT[:], ysrc[:, tsl].rearrange("(c p) t -> p c t", p=128))
            ph.load(x32, x32[:], xsrc[tsl, :].rearrange("(j p) d -> p j d", p=128))

        def front(tt):
            tsl = tslice(tt)
            for j in range(4):
                for hf in range(2):
                    po = ph.nxt()
                    for c in range(nyc):
                        mm(po[:, 0:512], yT[:, c, j * 128:(j + 1) * 128], Wout[:, c, hf * 512:(hf + 1) * 512],
                           c == 0, c == nyc - 1, r=[yT.b, Wout.b], w=[po.b])
                    I("vector", "tensor_tensor", x32[:, j, hf * 512:(hf + 1) * 512], po[:, 0:512],
                      x32[:, j, hf * 512:(hf + 1) * 512], ALU.add, r=[po.b, x32.b], w=[x32.b])
            ph.store(x32, xdst[tsl, :].rearrange("(j p) d -> p j d", p=128), x32[:])
            rms_rows(x32, h, gb, junk, ss, 4)

        def gateup(tt, fr):
            tsl = tslice(tt)
            hT_ = hTs[tt % 2]
            for f in fr:
                pg, pu = ph.nxt(), ph.nxt()
                proj_fm(pg, Wg, f * 128, 128, hT_, 8)
                proj_fm(pu, Wu, f * 128, 128, hT_, 8)
                g, a, sl_, st = gsb[f % 2], acc[f % 2], sil[f % 2], ast[f % 2]
                cp("scalar", g[:, 2:514], pg[:, 0:512], r=[pg.b], w=[g.b])
                cp("gpsimd", g[:, 0:2], halo[:, f, :], r=[halo.b], w=[g.b])
                cp("gpsimd", halo[:, f, :], g[:, 512:514], r=[g.b], w=[halo.b])
                I("vector", "tensor_scalar", a[:], g[:, 2:514], cw[:, f, 2:3], cw[:, f, 3:4], ALU.mult, ALU.add,
                  r=[g.b, cw.b], w=[a.b])
                I("vector", "scalar_tensor_tensor", a[:], g[:, 1:513], cw[:, f, 1:2], a[:], ALU.mult, ALU.add,
                  r=[g.b, cw.b, a.b], w=[a.b])
                I("vector", "scalar_tensor_tensor", a[:], g[:, 0:512], cw[:, f, 0:1], a[:], ALU.mult, ALU.add,
                  r=[g.b, cw.b, a.b], w=[a.b])
                I("scalar", "activation", sl_[:], a[:], AF.Silu, r=[a.b], w=[sl_.b])
                I("vector", "tensor_tensor", st[:], sl_[:], pu[:, 0:512], ALU.mult, r=[sl_.b, pu.b], w=[st.b])
                ph.store(st, actd[f, :, tsl], st[:])

        loads(0)
        front(0)
        transpose_rows(h, hTs[0], 8, 4, tps, ident)
        for tt in range(NT):
            nxt_ = tt + 1 < NT
            if nxt_:
                loads(tt + 1)
            gateup(tt, range(0, 11))
            if nxt_:
                front(tt + 1)
            gateup(tt, range(11, NFC))
            if nxt_:
                transpose_rows(h, hTs[(tt + 1) % 2], 8, 4, tps, ident)
        ph.close()

    def phase_Cd(L, prefetch=None):
        ph = Phase("Cd%d" % L)
        sb, ps = ph.sb, ph.ps
        pre = "l%d_" % L
        xsrc = x1 if L == 0 else x3
        actd = act0 if L == 0 else act1
        Wd = sb("Wd", [128, NFC, 1024], BF16)
        wdv = W[pre + "ffn_w_down"].rearrange("(c p) n -> p c n", p=128)
        for c in range(0, NFC, 2):
            ph.load_w(Wd, Wd[:, c:c + 2, :], wdv[:, c:c + 2, :], group=True)
        aTs = [sb("aT%d" % i, [128, NFC, 512], BF16) for i in range(2)]
        x32s = [sb("x32_%d" % i, [128, 4, 1024], F32) for i in range(2)]
        ph.pspool(6)
        if L == 1:
            gb = sb("gb", [128, 1024], F32)
            ph.load(gb, gb[:], bcast_row(W["final_norm"], 1024))
            junk = sb("junk", [128, 1024], F32)
            ss = sb("ss", [128, 4], F32)
            o32 = sb("o32", [128, 4, 1024], F32)

        def loads(tt):
            tsl = slice(tt * 512, (tt + 1) * 512)
            ph.load(aTs[tt % 2], aTs[tt % 2][:], actd[:, :, tsl].rearrange("f p t -> p f t"))
            ph.load(x32s[tt % 2], x32s[tt % 2][:], xsrc[tsl, :].rearrange("(j p) d -> p j d", p=128))

        loads(0)
        if prefetch is not None:
            prefetch(ph)
        for tt in range(NT):
            tsl = slice(tt * 512, (tt + 1) * 512)
            if tt + 1 < NT:
                loads(tt + 1)
            aT, x32 = aTs[tt % 2], x32s[tt % 2]
            for j in range(4):
                for hf in range(2):
                    po = ph.nxt()
                    for f in range(NFC):
                        mm(po[:, 0:512], aT[:, f, j * 128:(j + 1) * 128], Wd[:, f, hf * 512:(hf + 1) * 512],
                           f == 0, f == NFC - 1, r=[aT.b, Wd.b], w=[po.b])
                    I("vector", "tensor_tensor", x32[:, j, hf * 512:(hf + 1) * 512], po[:, 0:512],
                      x32[:, j, hf * 512:(hf + 1) * 512], ALU.add, r=[po.b, x32.b], w=[x32.b])
            if L == 0:
                ph.store(x32, x2[tsl, :].rearrange("(j p) d -> p j d", p=128), x32[:])
            else:
                rms_rows(x32, o32, gb, junk, ss, 4)
                ph.store(o32, out_t[tsl, :].rearrange("(j p) d -> p j d", p=128), o32[:])
        ph.close()

    def phase_D():
        ph = Phase("D")
        sb, ps = ph.sb, ph.ps
        ident = sb("ident", [128, 128], BF16)
        ph.load_w(ident, ident[:], C["c_ident"])
        tri = sb("tri", [128, 128], F32)
        ph.load(tri, tri[:], C["c_tri"])
        ones32 = sb("ones32", [128, 128], F32)
        I("vector", "memset", ones32[:], 1.0, w=[ones32.b])
        gb = sb("gb", [128, 1024], F32)
        ph.load(gb, gb[:], bcast_row(W["l1_norm_mix"], 1024))
        gn = sb("gn", [128, 1024], F32)
        ph.load(gn, gn[:], bcast_row(W["l1_ssm_norm"], 1024))
        cw = sb("cw", [128, 12, 5], F32)
        for jj in range(4):
            ph.load(cw, cw[:, :, jj:jj + 1], colvec(W["l1_ssm_conv_w"][jj], 12), group=True, slow=True)
        ph.load(cw, cw[:, :, 4:5], colvec(W["l1_ssm_conv_b"], 12), group=True, slow=True)
        dtb = sb("dtb", [128, 16], F32)
        ph.load(dtb, dtb[:], bcast_row(W["l1_ssm_dt_bias"], 16))
        Aneg = sb("Aneg", [128, 16], F32)
        ph.load(Aneg, Aneg[:], bcast_row(W["l1_ssm_a_log"], 16))
        I("scalar", "activation", Aneg[:], Aneg[:], AF.Exp, r=[Aneg.b], w=[Aneg.b])
        I("vector", "tensor_scalar", Aneg[:], Aneg[:], -1.0, None, ALU.mult, r=[Aneg.b], w=[Aneg.b])
        Dsk = sb("Dsk", [128, 16], F32)
        ph.load(Dsk, Dsk[:], bcast_row(W["l1_ssm_d"], 16))
        Win = PRE["D_Win"]
        x32 = sb("x32", [128, 4, 1024], F32)
        h = sb("h", [128, 4, 1024], BF16)
        hT = sb("hT", [128, 8, 512], BF16)
        junk = sb("junk", [128, 1024], F32)
        ss = sb("ss", [128, 4], F32)
        halo = sb("halo", [128, 12, 3], F32)
        I("vector", "memset", halo[:], 0.0, w=[halo.b])
        raw = [sb("raw0", [128, 515], F32)] * 2
        cac = [sb("cac0", [128, 512], F32)] * 2
        xbcT = sb("xbcT", [128, 12, 512], BF16)
        qkst = [sb("qkst%d" % i, [128, 512], BF16) for i in range(2)]
        zs = sb("zs", [128, 4, 1024], F32)
        dta = sb("dta", [128, 4, 16], F32)
        xs_tm2 = [sb("xs_tm%d" % i, [128, 1024], BF16) for i in range(2)]
        B_tm2 = [sb("B_tm%d" % i, [128, 256], BF16) for i in range(2)]
        da = sb("da", [128, 16], F32)
        cscol = sb("cscol", [128, 16], F32)
        Rt = sb("Rt", [128, 16, 128], F32)
        dif = sb("dif", [128, 16, 128], F32)
        cl = sb("cl", [128, 16], F32)
        dend2 = [sb("dend%d" % i, [128, 16], F32) for i in range(2)]
        ecl2 = [sb("ecl%d" % i, [128, 16], F32) for i in range(2)]
        ecs2 = [sb("ecs%d" % i, [128, 16], F32) for i in range(2)]
        Gm = sb("Gm", [128, 2, 128], F32)
        MT = sb("MT", [128, 16, 128], BF16)
        xdt = sb("xdt", [128, 16, 64], BF16)
        xdd = sb("xdd", [128, 16, 64], BF16)

        class VW:
            def __init__(self, ap, b):
                self.ap = ap
                self.b = b

            def __getitem__(self, k):
                return self.ap[k]

        MT2 = [MT, VW(h.t[:, 0:2, :].rearrange("p a (h l) -> p (a h) l", l=128), h.b)]
        xdt2 = [xdt, VW(h.t[:, 2, :].rearrange("p (h d) -> p h d", d=64), h.b)]
        xdd2 = [xdd, VW(h.t[:, 3, :].rearrange("p (h d) -> p h d", d=64), h.b)]
        ytm = sb("ytm", [128, 1024], F32)
        tmp2 = sb("tmp2", [128, 1024], F32)
        prev32 = sb("prev32", [128, 1024], F32)
        prevb = sb("prevb", [128, 1024], BF16)
        I("vector", "memset", prev32[:], 0.0, w=[prev32.b])
        I("vector", "memset", prevb[:], 0.0, w=[prevb.b])
        ss2 = sb("ss2", [128, 2], F32)
        ycst = sb("ycst", [128, 4, 1024], BF16)
        ycT = hT
        tps = [ps("tp%d" % i, [128, 1024], BF16) for i in range(2)]
        ph.pspool(6)

        def v3(ap, d):
            return ap.rearrange("p (h d) -> p h d", d=d)

        xpool = ph.pp[0:2]
        ypy = ph.pp[2:4]
        yq = ph.pp[4:6]

        for tt in range(NT):
            tsl = slice(tt * 512, (tt + 1) * 512)
            ph.load(x32, x32[:], x2[tsl, :].rearrange("(j p) d -> p j d", p=128))
            rms_rows(x32, h, gb, junk, ss, 4)
            transpose_rows(h, hT, 8, 4, tps, ident)
            for i in range(12):
                pc = ph.nxt()
                proj_fm(pc, Win, 1024 + i * 128, 128, hT, 8)
                rw, a = raw[i % 2], cac[i % 2]
                cp("scalar", rw[:, 3:515], pc[:, 0:512], r=[pc.b], w=[rw.b])
                cp("gpsimd", rw[:, 0:3], halo[:, i, :], r=[halo.b], w=[rw.b])
                cp("gpsimd", halo[:, i, :], rw[:, 512:515], r=[rw.b], w=[halo.b])
                I("vector", "tensor_scalar", a[:], rw[:, 3:515], cw[:, i, 3:4], cw[:, i, 4:5], ALU.mult, ALU.add,
                  r=[rw.b, cw.b], w=[a.b])
                for k_ in (2, 1, 0):
                    I("vector", "scalar_tensor_tensor", a[:], rw[:, k_:k_ + 512], cw[:, i, k_:k_ + 1], a[:],
                      ALU.mult, ALU.add, r=[rw.b, cw.b, a.b], w=[a.b])
                I("scalar", "activation", xbcT[:, i, :], a[:], AF.Silu, r=[a.b], w=[xbcT.b])
            for i in range(8):
                pq = ph.nxt()
                proj_fm(pq, Win, 2576 + i * 128, 128, hT, 8)
                st = qkst[i % 2]
                I("scalar", "mul", st[:], pq[:, 0:512], 0.125 if i < 4 else 1.0, r=[pq.b], w=[st.b])
                ph.store(st, (qt_sb if i < 4 else kt_sb)[i % 4, :, tsl], st[:])
            for j in range(4):
                pv = ph.nxt()
                for c in range(8):
                    mm(pv[:, 0:512], hT[:, c, j * 128:(j + 1) * 128], Win[:, c, 3600:4112], c == 0, c == 7,
                       r=[hT.b, Win.b], w=[pv.b])
                cp("scalar", ycst[:, j, 0:512], pv[:, 0:512], r=[pv.b], w=[ycst.b])
            ph.store(ycst, v_sb[tt * 4:(tt + 1) * 4].rearrange("j p f -> p j f"), ycst[:, :, 0:512])
            for j in range(4):
                for hf in range(2):
                    pz = ph.nxt()
                    for c in range(8):
                        mm(pz[:, 0:512], hT[:, c, j * 128:(j + 1) * 128], Win[:, c, hf * 512:(hf + 1) * 512],
                           c == 0, c == 7, r=[hT.b, Win.b], w=[pz.b])
                    I("scalar", "activation", zs[:, j, hf * 512:(hf + 1) * 512], pz[:, 0:512], AF.Silu,
                      r=[pz.b], w=[zs.b])
                pd = ph.nxt()
                for c in range(8):
                    mm(pd[:, 0:16], hT[:, c, j * 128:(j + 1) * 128], Win[:, c, 2560:2576], c == 0, c == 7,
                       r=[hT.b, Win.b], w=[pd.b])
                I("vector", "tensor_tensor", dta[:, j, :], pd[:, 0:16], dtb[:], ALU.add, r=[pd.b, dtb.b], w=[dta.b])
            I("scalar", "activation", dta[:], dta[:], AF.Exp, r=[dta.b], w=[dta.b])
            I("scalar", "activation", dta[:], dta[:], AF.Ln, bias=1.0, r=[dta.b], w=[dta.b])
            def Xgen(j):
                par = j % 2
                csl = slice(j * 128, (j + 1) * 128)
                xs_, B_, MT_, xdt_, xdd_ = xs_tm2[par], B_tm2[par], MT2[par], xdt2[par], xdd2[par]
                dend_, ecl_, ecs_ = dend2[par], ecl2[par], ecs2[par]
                tp = tps[0]
                for c in range(8):
                    I("tensor", "transpose", tp[:, c * 128:(c + 1) * 128], xbcT[:, c, csl], ident[:],
                      r=[xbcT.b, ident.b], w=[tp.b], inc=(c == 7))
                cp("vector", xs_[:], tp[:, 0:1024], r=[tp.b], w=[xs_.b])
                yield
                tp = tps[1]
                for c in range(2):
                    I("tensor", "transpose", tp[:, c * 128:(c + 1) * 128], xbcT[:, 8 + c, csl], ident[:],
                      r=[xbcT.b, ident.b], w=[tp.b], inc=(c == 1))
                cp("scalar", B_[:], tp[:, 0:256], r=[tp.b], w=[B_.b])
                I("vector", "tensor_tensor", da[:], dta[:, j, :], Aneg[:], ALU.mult, r=[dta.b, Aneg.b], w=[da.b])
                yield
                pcs = xpool[0]
                mm(pcs[:, 0:16], tri[:], da[:], True, True, r=[tri.b, da.b], w=[pcs.b])
                cp("vector", cscol[:], pcs[:, 0:16], r=[pcs.b], w=[cscol.b])
                I("vector", "tensor_tensor", Rt[:], tri[:].unsqueeze(1).to_broadcast([128, 16, 128]),
                  da[:].unsqueeze(2).to_broadcast([128, 16, 128]), ALU.mult, r=[tri.b, da.b], w=[Rt.b])
                yield
                for q4 in range(4):
                    pr = xpool[(q4 + 1) % 2]
                    mm(pr[:, 0:512], ones32[:], Rt.t[:, q4 * 4:(q4 + 1) * 4, :].rearrange("p h l -> p (h l)"),
                       True, True, r=[ones32.b, Rt.b], w=[pr.b])
                    pr3 = pr.t[:, 0:512].rearrange("p (h l) -> p h l", l=128)
                    I("vector", "tensor_tensor", dif[:, q4 * 4:(q4 + 1) * 4, :], pr3,
                      cscol[:, q4 * 4:(q4 + 1) * 4].unsqueeze(2).to_broadcast([128, 4, 128]), ALU.subtract,
                      r=[pr.b, cscol.b], w=[dif.b])
                    cp("vector", cl[:, q4 * 4:(q4 + 1) * 4], pr3[:, :, 127], r=[pr.b], w=[cl.b])
                    yield
                I("vector", "tensor_tensor", dif[:, 0:8, :], dif[:, 0:8, :],
                  tri[:].unsqueeze(1).to_broadcast([128, 8, 128]), ALU.mult, r=[dif.b, tri.b], w=[dif.b])
                I("gpsimd", "tensor_tensor", dif[:, 8:16, :], dif[:, 8:16, :],
                  tri[:].unsqueeze(1).to_broadcast([128, 8, 128]), ALU.mult, r=[dif.b, tri.b], w=[dif.b])
                yield
                I("scalar", "activation", dif[:], dif[:], AF.Exp, r=[dif.b], w=[dif.b])
                I("vector", "tensor_tensor", dend_[:], cl[:], cscol[:], ALU.subtract, r=[cl.b, cscol.b], w=[dend_.b])
                yield
                I("scalar", "activation", dend_[:], dend_[:], AF.Exp, r=[dend_.b], w=[dend_.b])
                I("scalar", "activation", ecl_[:], cl[:], AF.Exp, r=[cl.b], w=[ecl_.b])
                I("scalar", "activation", ecs_[:], cscol[:], AF.Exp, r=[cscol.b], w=[ecs_.b])
                yield
                pG = xpool[0]
                for g in range(2):
                    mm(pG[:, g * 128:(g + 1) * 128], xbcT[:, 8 + g, csl], xbcT[:, 10 + g, csl], True, True,
                       r=[xbcT.b], w=[pG.b])
                I("vector", "tensor_tensor", Gm[:], pG.t[:, 0:256].rearrange("p (g l) -> p g l", l=128),
                  tri[:].unsqueeze(1).to_broadcast([128, 2, 128]), ALU.mult, r=[pG.b, tri.b], w=[Gm.b])
                yield
                for g in range(2):
                    I("gpsimd" if g == 0 else "vector", "tensor_tensor", MT_[:, g * 8:(g + 1) * 8, :],
                      dif[:, g * 8:(g + 1) * 8, :], Gm[:, g, :].unsqueeze(1).to_broadcast([128, 8, 128]), ALU.mult,
                      r=[dif.b, Gm.b], w=[MT_.b])
                    yield
                I("vector", "tensor_tensor", xdt_[:], v3(xs_[:], 64),
                  dta[:, j, :].unsqueeze(2).to_broadcast([128, 16, 64]), ALU.mult, r=[xs_.b, dta.b], w=[xdt_.b])
                yield
                I("gpsimd", "tensor_tensor", xdd_[:], xdt_[:], dend_[:].unsqueeze(2).to_broadcast([128, 16, 64]),
                  ALU.mult, r=[xdt_.b, dend_.b], w=[xdd_.b])
                yield

            def Ygen(j):
                par = j % 2
                csl = slice(j * 128, (j + 1) * 128)
                xs_, B_, MT_, xdt_, xdd_ = xs_tm2[par], B_tm2[par], MT2[par], xdt2[par], xdd2[par]
                dend_, ecl_, ecs_ = dend2[par], ecl2[par], ecs2[par]
                py = ypy
                for hh in range(16):
                    mm(py[hh // 8][:, (hh % 8) * 64:(hh % 8) * 64 + 64], MT_[:, hh, :], xdt_[:, hh, :], True, True,
                       r=[MT_.b, xdt_.b], w=[py[hh // 8].b], inc=(hh % 8 == 7))
                    if hh % 8 == 7:
                        yield
                for g in range(2):
                    pyo = yq[g]
                    mm(pyo[:, 0:512], xbcT[:, 10 + g, csl], prevb[:, g * 512:(g + 1) * 512], True, True,
                       r=[xbcT.b, prevb.b], w=[pyo.b])
                    gs = slice(g * 512, (g + 1) * 512)
                    I("vector", "tensor_tensor", v3(ytm.t[:, gs], 64), v3(pyo.t[:, 0:512], 64),
                      ecs_[:, g * 8:(g + 1) * 8].unsqueeze(2).to_broadcast([128, 8, 64]), ALU.mult,
                      r=[pyo.b, ecs_.b], w=[ytm.b])
                    yield
                    I("vector", "tensor_tensor", ytm[:, gs], ytm[:, gs], py[g][:, 0:512], ALU.add,
                      r=[ytm.b, py[g].b], w=[ytm.b])
                    yield
                I("gpsimd", "tensor_tensor", v3(tmp2.t[:], 64), v3(xs_[:], 64),
                  Dsk[:].unsqueeze(2).to_broadcast([128, 16, 64]), ALU.mult, r=[xs_.b, Dsk.b], w=[tmp2.b])
                yield
                I("vector", "tensor_tensor", ytm[:], ytm[:], tmp2[:], ALU.add, r=[ytm.b, tmp2.b], w=[ytm.b])
                yield
                for g in range(2):
                    pst = yq[g]
                    gs = slice(g * 512, (g + 1) * 512)
                    mm(pst[:, 0:512], B_[:, g * 128:(g + 1) * 128],
                       xdd_[:, g * 8:(g + 1) * 8, :].rearrange("p h d -> p (h d)"), True, True,
                       r=[B_.b, xdd_.b], w=[pst.b])
                    I("gpsimd", "tensor_tensor", v3(prev32.t[:, gs], 64), v3(prev32.t[:, gs], 64),
                      ecl_[:, g * 8:(g + 1) * 8].unsqueeze(2).to_broadcast([128, 8, 64]), ALU.mult,
                      r=[prev32.b, ecl_.b], w=[prev32.b])
                    yield
                    I("vector", "tensor_tensor", prev32[:, gs], prev32[:, gs], pst[:, 0:512], ALU.add,
                      r=[prev32.b, pst.b], w=[prev32.b])
                    yield
                cp("scalar", prevb[:], prev32[:], r=[prev32.b], w=[prevb.b])
                I("vector", "tensor_tensor", ytm[:], ytm[:], zs[:, j, :], ALU.mult, r=[ytm.b, zs.b], w=[ytm.b])
                yield
                I("vector", "memset", ss2[:], 0.0, w=[ss2.b])
                for g in range(2):
                    I("scalar", "activation", junk[:, 0:512], ytm[:, g * 512:(g + 1) * 512], AF.Square,
                      accum_out=ss2[:, g:g + 1], r=[ytm.b, ss2.b], w=[junk.b, ss2.b])
                yield
                I("vector", "tensor_scalar", ss2[:], ss2[:], 1.0 / 512, 1e-6, ALU.mult, ALU.add, r=[ss2.b], w=[ss2.b])
                I("scalar", "activation", ss2[:], ss2[:], AF.Ln, r=[ss2.b], w=[ss2.b])
                I("scalar", "activation", ss2[:], ss2[:], AF.Exp, scale=-0.5, r=[ss2.b], w=[ss2.b])
                yield
                for g in range(2):
                    gs = slice(g * 512, (g + 1) * 512)
                    I("vector", "scalar_tensor_tensor", ycst[:, j, gs], ytm[:, gs], ss2[:, g:g + 1], gn[:, gs],
                      ALU.mult, ALU.mult, r=[ytm.b, ss2.b, gn.b], w=[ycst.b])
                yield

            def run_interleaved(gens):
                gens = [g for g in gens if g is not None]
                while gens:
                    for g in list(gens):
                        try:
                            next(g)
                        except StopIteration:
                            gens.remove(g)

            run_interleaved([Xgen(0)])
            for j in range(4):
                run_interleaved([Ygen(j), Xgen(j + 1) if j < 3 else None])
            transpose_rows(ycst, ycT, 8, 4, tps, ident)
            ph.store(ycT, y1[0:1024, tsl].rearrange("(c p) t -> p c t", p=128), ycT[:])
        ph.close()

    def phase_E(prefetch=None):
        ph = Phase("E")
        sb, ps = ph.sb, ph.ps
        ident = sb("ident", [128, 128], BF16)
        ph.load_w(ident, ident[:], C["c_ident"])
        mbs = sb("mbs", [128, 512], BF16)
        ph.load_w(mbs, mbs[:], C["c_mbs"])
        smk = sb("smk", [128, 128], BF16)
        ph.load_w(smk, smk[:], C["c_sm"])
        negtri = sb("negtri", [128, 128], BF16)
        ph.load_w(negtri, negtri[:], C["c_negtri"])
        negones = sb("negones", [128, 128], BF16)
        I("vector", "memset", negones[:], -1.0, w=[negones.b])
        Vall = sb("Vall", [128, 32, 512], BF16)
        for q4 in range(4):
            ph.load(Vall, Vall[:, q4 * 8:(q4 + 1) * 8, :], v_sb[q4 * 8:(q4 + 1) * 8].rearrange("j p f -> p j f"),
                    group=True)
        KT = [sb("KT%d" % i, [128, S], BF16) for i in range(2)]
        QT = [sb("QT%d" % i, [128, S], BF16) for i in range(2)]
        for i in range(2):
            I("gpsimd", "memset", KT[i][64:128, :], 0.0, w=[KT[i].b])
            I("vector", "memset", QT[i][64:128, :], 0.0, w=[QT[i].b])
        Rb = sb("Rb", [128, 512], BF16)
        yst = [sb("yst%d" % i, [64, 512], BF16) for i in range(2)]
        pzs = [ps("pz%d" % i, [128, 512], F32) for i in range(2)]
        dps = ps("dps", [128, 512], F32)

        def filler():
            mm(dps[:, 0:512], negtri[:], mbs[:, 0:512], True, True, r=[negtri.b, mbs.b], w=[dps.b], inc=False)

        p2s = [ps("p2%d" % i, [128, 512], F32) for i in range(3)]
        accs = [ps("acc%d" % i, [128, 512], F32) for i in range(2)]
        it = 0
        gi = 0
        pend1, pend2, pend3 = [], [], []

        def flush(lst, keep):
            while len(lst) > keep:
                lst.pop(0)()

        e3 = [sb("e3_%d" % i, [128, 512], F32) for i in range(6)]
        sp3 = [sb("sp3_%d" % i, [128, 512], BF16) for i in range(6)]
        wf3 = [sb("wf3_%d" % i, [128, 512], F32) for i in range(6)]
        wt3 = [sb("wt3_%d" % i, [128, 512], BF16) for i in range(6)]
        def load_head(h_):
            ph.load(KT[h_ % 2], KT[h_ % 2][0:64, :], kt_sb[h_ // 2, (h_ % 2) * 64:(h_ % 2) * 64 + 64, :])
            ph.load(QT[h_ % 2], QT[h_ % 2][0:64, :], qt_sb[h_ // 2, (h_ % 2) * 64:(h_ % 2) * 64 + 64, :])

        load_head(0)
        if prefetch is not None:
            prefetch(ph)
        for w_ in range(32):
            mm(p2s[0][:, 0:512], negtri[:], QT[0][:, 0:512], True, True, r=[KT[0].b, QT[0].b, Vall.b, negtri.b,
               mbs.b, smk.b, ident.b], w=[p2s[0].b], inc=(w_ == 31))
        for hh in range(8):
            kt_, qt_ = KT[hh % 2], QT[hh % 2]
            for G in range(8):
                if G == 1 and hh + 1 < 8:
                    load_head(hh + 1)
                acc = accs[gi % 2]
                ys = yst[gi % 2]
                gi += 1
                for j in range(4 * G + 3, -1, -1):
                    qlo, qhi = max(4 * G, j), 4 * G + 3
                    c0 = (qlo - 4 * G) * 128
                    n = (qhi - qlo + 1) * 128
                    cs_ = slice(c0, c0 + n)
                    pz, p2 = pzs[it % 2], p2s[it % 3]
                    e_, sp_, wf, wt = e3[it % 6], sp3[it % 6], wf3[it % 6], wt3[it % 6]
                    it += 1
                    kb = kt_[:, j * 128:(j + 1) * 128]
                    qc = qt_[:, qlo * 128:(qhi + 1) * 128]
                    diag = (qlo == j)
                    first = (j == 4 * G + 3)
                    mm(pz[:, cs_], kb, qc, True, True, r=[kt_.b, qt_.b], w=[pz.b])
                    filler()
                    I("scalar", "activation", e_[:, cs_], pz[:, cs_], AF.Exp, r=[pz.b], w=[e_.b])

                    def st1b(sp_=sp_, e_=e_, cs_=cs_, diag=diag, c0=c0):
                        I("scalar", "activation", sp_[:, cs_], e_[:, cs_], AF.Ln, bias=1.0, r=[e_.b], w=[sp_.b])
                        if diag:
                            I("gpsimd", "tensor_tensor", sp_[:, c0:c0 + 128], sp_[:, c0:c0 + 128], smk[:], ALU.mult,
                              r=[sp_.b, smk.b], w=[sp_.b])
                    pend1.append(st1b)
                    flush(pend1, 1)

                    def st2(p2=p2, cs_=cs_, sp_=sp_, first=first, diag=diag, n=n, wf=wf, wt=wt, e_=e_, j=j):
                        if first:
                            I("gpsimd", "memset", Rb[:], 0.0, w=[Rb.b])
                        useR = not first
                        mm(p2[:, cs_], negtri[:], sp_[:, cs_], True, not (useR or diag), r=[negtri.b, sp_.b],
                           w=[p2.b])
                        if useR:
                            mm(p2[:, cs_], negones[:], Rb[:, cs_], False, not diag, r=[negones.b, Rb.b], w=[p2.b])
                        if diag:
                            mm(p2[:, cs_], ident[:], mbs[:, 0:n], False, True, r=[ident.b, mbs.b], w=[p2.b])
                        if j > 0:
                            I("vector", "tensor_tensor", Rb[:, cs_], Rb[:, cs_], sp_[:, cs_], ALU.add,
                              r=[Rb.b, sp_.b], w=[Rb.b])
                        I("scalar", "activation", wf[:, cs_], p2[:, cs_], AF.Exp, r=[p2.b], w=[wf.b])
                        I("vector", "tensor_tensor", wt[:, cs_], wf[:, cs_], e_[:, cs_], ALU.mult,
                          r=[wf.b, e_.b], w=[wt.b])

                    def st3(acc=acc, cs_=cs_, j=j, hh=hh, wt=wt, first=first, ys=ys, G=G):
                        lastj = (j == 0)
                        v0 = min(hh * 64, 384)
                        r0 = hh * 64 - v0
                        I("tensor", "matmul", acc[:, cs_], Vall[:, j, v0:v0 + 128], wt[:, cs_],
                          start=first, stop=lastj, skip_group_check=True, r=[wt.b, Vall.b], w=[acc.b], inc=lastj)
                        filler()
                        if lastj:
                            cp("vector", ys[:], acc[r0:r0 + 64, 0:512], r=[acc.b], w=[ys.b])
                            col = 1024 + hh * 64
                            ph.store(ys, y1[col:col + 64, G * 512:(G + 1) * 512], ys[:])
                    pend2.append(st2)
                    flush(pend2, 3)
                    pend3.append(st3)
                    flush(pend3, 4)
        flush(pend1, 0)
        flush(pend2, 0)
        flush(pend3, 0)
        ph.close()

    def run_all():
        phase_A()
        phase_B_dsw()
        with ExitStack() as st0:
            pre_alloc(st0, "C0_Wout", [128, 8, 1024], BF16)
            pre_alloc(st0, "C0_Wg", [128, 8, FF], BF16)
            pre_alloc(st0, "C0_Wu", [128, 8, FF], BF16)
            phase_B_mla(prefetch=lambda ph: pre_load_C(ph, 0, ("Wout",)))
            phase_C(0)
        with ExitStack() as st1:
            pre_alloc(st1, "D_Win", [128, 8, 4112], BF16)
            phase_Cd(0, prefetch=pre_load_D)
            phase_D()
        with ExitStack() as st2:
            pre_alloc(st2, "C1_Wg", [128, 8, FF], BF16)
            pre_alloc(st2, "C1_Wu", [128, 8, FF], BF16)
            phase_E(prefetch=lambda ph: pre_load_C(ph, 1, ()))
            phase_C(1)
        phase_Cd(1)

    run_all()
    es.close()
    return nc


_CACHE = {}


def kernel(**inputs):
    stop_after = os.environ.get("K_STOP") or None
    dbg = bool(os.environ.get("K_DBG"))
    nc = build(stop_after=stop_after, dbg=dbg)
    consts = host_consts()
    in_maps = []
    for b in range(8):
        m = {"x": np.ascontiguousarray(inputs["x"][b], dtype=np.float32),
             "positions": np.ascontiguousarray(inputs["positions"][b], dtype=np.int32)}
        for n, _ in W_NAMES:
            m[n] = np.ascontiguousarray(inputs[n], dtype=np.float32)
        m.update(consts)
        in_maps.append(m)
    res = run_bass_kernel_spmd(nc, in_maps, core_ids=list(range(8)))
    if dbg:
        _CACHE["res"] = res.results
    return np.stack([res.results[b]["out"] for b in range(8)], axis=0).astype(np.float32)
```

```python
import math
import os
from contextlib import ExitStack

import numpy as np
import concourse.bass as bass
import concourse.mybir as mybir
from concourse.bass_utils import run_bass_kernel_spmd

F32 = mybir.dt.float32
BF16 = mybir.dt.bfloat16
I32 = mybir.dt.int32
AF = mybir.ActivationFunctionType
ALU = mybir.AluOpType

S = 4096
DM = 1024
NT = 8
FF = 2816
NFC = 22
NEG = -30000.0
PI = math.pi


class Buf:
    __slots__ = ("name", "w", "rs", "excl")

    def __init__(self, name):
        self.name = name
        self.w = None
        self.rs = {}
        self.excl = False


class TB:
    def __init__(self, t, name):
        self.t = t
        self.b = Buf(name)
        self.name = name

    def __getitem__(self, k):
        return self.t[k]


class Prog:
    ENGS = ("sync", "scalar", "vector", "gpsimd", "tensor")

    def __init__(self, nc, es, npool=28, npool_sw=20):
        self.nc = nc
        self.lists = {e: [] for e in self.ENGS}
        self.cnt = {e: 0 for e in self.ENGS}
        self.seen = {e: {} for e in self.ENGS}
        self.sems = {e: es.enter_context(nc.semaphore("s_" + e)) for e in self.ENGS}
        self.pool = [[es.enter_context(nc.semaphore("d%d" % i)), 0, ("d", i)] for i in range(npool + npool_sw)]
        self.npool = npool
        self.keymap = {}
        self.pool_next = 0
        self.pool_next_sw = npool
        self.ninstr = 0

    def new_phase(self):
        self.keymap = {}
        self.pool_next = 0
        self.pool_next_sw = self.npool

    def _ent(self, key, eng):
        if key not in self.keymap:
            if eng == "gpsimd":
                assert self.pool_next_sw < len(self.pool), "out of sw dma sems"
                self.keymap[key] = self.pool[self.pool_next_sw]
                self.pool_next_sw += 1
            else:
                assert self.pool_next < self.npool, "out of dma sems"
                self.keymap[key] = self.pool[self.pool_next]
                self.pool_next += 1
        return self.keymap[key]

    def _wait(self, eng, key, sem, val):
        s = self.seen[eng]
        if s.get(key, 0) >= val:
            return
        s[key] = val
        self.lists[eng].append(lambda e, sem=sem, val=val: e.wait_ge(sem, val))

    def _dep(self, eng, dep, same_ok=False):
        if dep is None:
            return
        key, seq = dep
        if isinstance(key, str):
            if key == eng and same_ok:
                return
            self._wait(eng, key, self.sems[key], seq)
        else:
            self._wait(eng, key, self.pool[key[1]][0], seq)

    def op(self, eng, fn, reads=(), writes=(), inc=True):
        for b in reads:
            if b.w is not None:
                self._dep(eng, b.w, same_ok=(eng == "tensor"))
            if b.excl:
                for re_, rs_ in b.rs.items():
                    self._dep(eng, (re_, rs_), same_ok=True)
        pe = (eng == "tensor")
        for b in writes:
            if b.w is not None:
                self._dep(eng, b.w, same_ok=pe)
            for re_, rs_ in b.rs.items():
                self._dep(eng, (re_, rs_), same_ok=pe)
        if inc:
            self.cnt[eng] += 1
            seq = self.cnt[eng]
            sem = self.sems[eng]
            self.lists[eng].append(lambda e, fn=fn, sem=sem: fn(e).then_inc(sem, 1))
        else:
            seq = self.cnt[eng] + 1
            self.lists[eng].append(fn)
        self.ninstr += 1
        for b in reads:
            if seq > b.rs.get(eng, 0):
                b.rs[eng] = seq
        for b in writes:
            b.w = (eng, seq)
            b.rs = {}
        return seq

    def dma(self, eng, out, in_, reads=(), writes=(), key=None, group=False, slow=False):
        ent = self._ent(key, eng)
        k = ent[2]
        for b in reads:
            if b.w is not None:
                self._dep(eng, b.w)
        for b in writes:
            if b.w is not None and not (group and b.w[0] == k):
                self._dep(eng, b.w)
            for re_, rs_ in b.rs.items():
                self._dep(eng, (re_, rs_))
        ent[1] += 16
        seq = ent[1]
        sem = ent[0]
        self.lists[eng].append(
            lambda e, out=out, in_=in_, sem=sem, slow=slow: (
                e.dma_start(out=out, in_=in_, allow_slow_non_contiguous=True) if slow
                else e.dma_start(out=out, in_=in_)).then_inc(sem, 16))
        self.ninstr += 1
        for b in reads:
            if seq > b.rs.get(k, 0):
                b.rs[k] = seq
        for b in writes:
            b.w = (k, seq)
            b.rs = {}
        return seq

    def barrier_all(self):
        for eng in self.ENGS:
            for other in self.ENGS:
                if other != eng and self.cnt[other] > 0:
                    self._wait(eng, other, self.sems[other], self.cnt[other])
            for sem, cnt, k in self.pool:
                if cnt > 0:
                    self._wait(eng, k, sem, cnt)

    def emit(self):
        nc = self.nc
        lists = self.lists
        with nc.Block() as block:
            @block.sync
            def _(e):
                for f in lists["sync"]:
                    f(e)

            @block.scalar
            def _(e):
                for f in lists["scalar"]:
                    f(e)

            @block.vector
            def _(e):
                for f in lists["vector"]:
                    f(e)

            @block.gpsimd
            def _(e):
                for f in lists["gpsimd"]:
                    f(e)

            @block.tensor
            def _(e):
                for f in lists["tensor"]:
                    f(e)
        self.lists = {e: [] for e in self.ENGS}


def host_consts():
    c = {}
    kk = np.arange(128)[:, None]

    def mb(width, window, dil, strict=False):
        u = np.arange(width)[None, :]
        d = u - kk
        ok = (d >= (1 if strict else 0)) & (d <= window) & (d % dil == 0)
        return np.where(ok, 0.0, NEG).astype(np.float32)

    c["c_ident"] = np.eye(128, dtype=np.float32)
    c["c_mb0"] = mb(128 + 512, 128, 1)
    c["c_mb1"] = mb(512 + 512, 512, 4)
    c["c_mb2"] = mb(2048 + 512, 2048, 16)
    c["c_mbc"] = mb(512, 1 << 30, 1)
    c["c_mbs"] = mb(512, 1 << 30, 1, strict=True)
    u = np.arange(128)[None, :]
    c["c_sm"] = (u - kk > 0).astype(np.float32)
    c["c_tri"] = (kk <= u).astype(np.float32)
    c["c_negtri"] = -(kk >= u).astype(np.float32)
    cols = np.zeros((128, 8), np.float32)
    p = np.arange(128)
    cols[:, 0] = 10000.0 ** (-(p % 32) / 32.0)
    sg = np.where((p % 64) < 32, -1.0, 1.0)
    cols[:, 1] = sg
    cols[:, 2] = sg * 0.125
    pm = p - 64
    inm = (pm >= 0) & (pm < 32)
    cols[:, 3] = np.where(inm, 10000.0 ** (-(pm % 16) / 16.0), 0.0)
    sgm = np.where(pm < 16, -1.0, 1.0)
    cols[:, 4] = np.where(inm, sgm, 0.0)
    cols[:, 5] = np.where(inm, sgm * 96 ** -0.5, 0.0)
    c["c_cols"] = cols.astype(np.float32)
    return c


W_NAMES = [
    ("l0_norm_mix", [1024]), ("l0_w_in", [1024, 2720]), ("l0_mla_q_norm", [256]), ("l0_mla_w_uq", [256, 1152]),
    ("l0_mla_kv_norm", [128]), ("l0_mla_w_ukv", [128, 1536]), ("l0_w_out", [1024, 1024]), ("l0_norm_ffn", [1024]),
    ("l0_ffn_w_gate", [1024, 2816]), ("l0_ffn_w_up", [1024, 2816]), ("l0_ffn_conv_w", [3, 2816]),
    ("l0_ffn_conv_b", [2816]), ("l0_ffn_w_down", [2816, 1024]),
    ("l1_norm_mix", [1024]), ("l1_w_in", [1024, 4112]), ("l1_ssm_conv_w", [4, 1536]), ("l1_ssm_conv_b", [1536]),
    ("l1_ssm_dt_bias", [16]), ("l1_ssm_a_log", [16]), ("l1_ssm_d", [16]), ("l1_ssm_norm", [1024]),
    ("l1_w_out", [1536, 1024]), ("l1_norm_ffn", [1024]), ("l1_ffn_w_gate", [1024, 2816]),
    ("l1_ffn_w_up", [1024, 2816]), ("l1_ffn_conv_w", [3, 2816]), ("l1_ffn_conv_b", [2816]),
    ("l1_ffn_w_down", [2816, 1024]), ("final_norm", [1024]),
]


def build(stop_after=None, dbg=False):
    nc = bass.Bass("TRN2", target_bir_lowering=False)
    skind = "ExternalOutput" if dbg else "Internal"
    x_in = nc.dram_tensor("x", [S, DM], F32, kind="ExternalInput").ap()
    pos_t = nc.dram_tensor("positions", [S], I32, kind="ExternalInput")
    W = {n: nc.dram_tensor(n, shp, F32, kind="ExternalInput").ap() for n, shp in W_NAMES}
    C = {n: nc.dram_tensor(n, list(a.shape), F32, kind="ExternalInput").ap() for n, a in host_consts().items()}
    out_t = nc.dram_tensor("out", [S, DM], F32, kind="ExternalOutput").ap()

    def scratch(name, shape, dt):
        return nc.dram_tensor(name, shape, dt, kind=skind).ap()

    qt_dsw = scratch("qt_dsw", [6, 128, S], BF16)
    kt_dsw = scratch("kt_dsw", [6, 128, S], BF16)
    v_dsw = scratch("v_dsw", [12, 32, 128, 128], BF16)
    qt_mla = scratch("qt_mla", [12, 96, S], BF16)
    kt_mla = scratch("kt_mla", [12, 96, S], BF16)
    v_mla = scratch("v_mla", [12, 32, 128, 128], BF16)
    y0 = scratch("y0", [1024, S], BF16)
    x1 = scratch("x1", [S, DM], F32)
    act0 = scratch("act0", [NFC, 128, S], BF16)
    x2 = scratch("x2", [S, DM], F32)
    y1 = scratch("y1", [1536, S], BF16)
    qt_sb = scratch("qt_sb", [4, 128, S], BF16)
    kt_sb = scratch("kt_sb", [4, 128, S], BF16)
    v_sb = scratch("v_sb", [32, 128, 512], BF16)
    x3 = scratch("x3", [S, DM], F32)
    act1 = scratch("act1", [NFC, 128, S], BF16)

    es = ExitStack()
    P = Prog(nc, es)

    def I(eng, meth, *args, r=(), w=(), inc=True, **kw):
        return P.op(eng, lambda e: getattr(e, meth)(*args, **kw), r, w, inc=inc)

    def cp(eng, out, in_, r, w):
        if eng == "scalar":
            return I(eng, "copy", out, in_, r=r, w=w)
        return I(eng, "tensor_copy", out, in_, r=r, w=w)

    def mm(out, lhsT, rhs, start, stop, r, w, inc=None):
        return I("tensor", "matmul", out, lhsT, rhs, start=start, stop=stop, r=r, w=w,
                 inc=(stop if inc is None else inc))

    def bcast_row(ap1d, n):
        return ap1d.partition_broadcast(128)

    def colvec(ap1d, nchunk):
        t = ap1d.tensor
        off = ap1d.offset
        return bass.AP(t, off, [[1, 128], [128, nchunk], [1, 1]])

    class Phase:
        def __init__(self, name):
            self.name = name
            self.st = ExitStack()
            P.new_phase()
            self.pp = []
            self.pi = 0

        def sb(self, n, s, d):
            return TB(self.st.enter_context(nc.sbuf_tensor(self.name + "_" + n, s, d)), self.name + "_" + n)

        def ps(self, n, s, d):
            t = TB(self.st.enter_context(nc.psum_tensor(self.name + "_" + n, s, d)), self.name + "_" + n)
            t.b.excl = True
            return t

        def pspool(self, k):
            self.pp = [self.ps("pp%d" % i, [128, 512], F32) for i in range(k)]

        def nxt(self):
            t = self.pp[self.pi % len(self.pp)]
            self.pi += 1
            return t

        def load_w(self, tb, out, in_, group=False):
            P.dma("gpsimd", out, in_, writes=[tb.b], key=tb.name, group=group)

        def load(self, tb, out, in_, group=False, slow=False):
            P.dma("sync", out, in_, writes=[tb.b], key=tb.name, group=group, slow=slow)

        def store(self, tb, out, in_):
            P.dma("sync", out, in_, reads=[tb.b], key=tb.name + "_st")

        def close(self):
            print("phase", self.name, "instr", P.ninstr, dict(P.cnt))
            P.barrier_all()
            P.emit()
            self.st.close()

    def rms_rows(x, h, gb, junk, ss, J, width=1024, eps=1e-6):
        I("vector", "memset", ss[:, 0:J], 0.0, w=[ss.b])
        for j in range(J):
            I("scalar", "activation", junk[:, 0:width], x[:, j, 0:width], AF.Square, accum_out=ss[:, j:j + 1],
              r=[x.b, ss.b], w=[junk.b, ss.b])
        I("vector", "tensor_scalar", ss[:, 0:J], ss[:, 0:J], 1.0 / width, eps, ALU.mult, ALU.add, r=[ss.b], w=[ss.b])
        I("scalar", "activation", ss[:, 0:J], ss[:, 0:J], AF.Ln, r=[ss.b], w=[ss.b])
        I("scalar", "activation", ss[:, 0:J], ss[:, 0:J], AF.Exp, scale=-0.5, r=[ss.b], w=[ss.b])
        for j in range(J):
            I("vector", "scalar_tensor_tensor", h[:, j, 0:width], x[:, j, 0:width], ss[:, j:j + 1], gb[:, 0:width],
              ALU.mult, ALU.mult, r=[x.b, ss.b, gb.b], w=[h.b])

    def transpose_rows(h, hT, nchunk, J, tps, ident):
        for c in range(nchunk):
            tp = tps[c % 2]
            for j in range(J):
                I("tensor", "transpose", tp[:, j * 128:(j + 1) * 128], h[:, j, c * 128:(c + 1) * 128], ident[:],
                  r=[h.b, ident.b], w=[tp.b], inc=(j == J - 1))
            cp("vector" if c % 2 == 0 else "scalar", hT[:, c, 0:J * 128], tp[:, 0:J * 128], r=[tp.b], w=[hT.b])

    def proj_fm(pt, Wt, col0, M, hT, nK, N=512):
        for c in range(nK):
            mm(pt[0:M, 0:N], Wt[:, c, col0:col0 + M], hT[:, c, 0:N], c == 0, c == nK - 1, r=[Wt.b, hT.b], w=[pt.b])

    def phase_A():
        ph = Phase("A")
        sb, ps = ph.sb, ph.ps
        ident = sb("ident", [128, 128], BF16)
        ph.load_w(ident, ident[:], C["c_ident"])
        cols = sb("cols", [128, 8], F32)
        ph.load(cols, cols[:], C["c_cols"])
        ones32 = sb("ones32", [128, 128], F32)
        I("vector", "memset", ones32[:], 1.0, w=[ones32.b])
        gb = sb("gb", [128, 1024], F32)
        ph.load(gb, gb[:], bcast_row(W["l0_norm_mix"], 1024))
        gq = sb("gq", [128, 2, 1], F32)
        ph.load(gq, gq[:], colvec(W["l0_mla_q_norm"], 2), slow=True)
        gkv = sb("gkv", [128, 1, 1], F32)
        ph.load(gkv, gkv[:], colvec(W["l0_mla_kv_norm"], 1), slow=True)
        win = W["l0_w_in"].rearrange("(c p) n -> p c n", p=128)
        Wqk = sb("Wqk", [128, 8, 1536], BF16)
        for c in range(8):
            ph.load_w(Wqk, Wqk[:, c, :], win[:, c, 0:1536], group=True)
        Wqkr = sb("Wqkr", [128, 8, 1536], BF16)
        v_src = Wqk.t[:].rearrange("p c (h d) -> p (c h) d", d=64)
        v_dst = Wqkr.t[:].rearrange("p c (h d) -> p (c h) d", d=64)
        cp("vector", v_dst[:, :, 0:32], v_src[:, :, 32:64], r=[Wqk.b], w=[Wqkr.b])
        cp("gpsimd", v_dst[:, :, 32:64], v_src[:, :, 0:32], r=[Wqk.b], w=[Wqkr.b])
        Wv = sb("Wv", [128, 8, 768], BF16)
        ph.load_w(Wv, Wv[:], win[:, :, 1536:2304])
        Wc = sb("Wc", [128, 8, 384], BF16)
        ph.load_w(Wc, Wc[:], win[:, :, 2304:2688])
        Wkpe = sb("Wkpe", [128, 8, 96], BF16)
        I("vector", "memset", Wkpe[:], 0.0, w=[Wkpe.b])
        ph.load_w(Wkpe, Wkpe[:, :, 64:96], win[:, :, 2688:2720])
        Wkper = sb("Wkper", [128, 8, 96], BF16)
        I("vector", "memset", Wkper[:], 0.0, w=[Wkper.b])
        cp("vector", Wkper[:, :, 64:80], Wkpe[:, :, 80:96], r=[Wkpe.b], w=[Wkper.b])
        cp("vector", Wkper[:, :, 80:96], Wkpe[:, :, 64:80], r=[Wkpe.b], w=[Wkper.b])
        Wuq = sb("Wuq", [128, 2, 1152], BF16)
        ph.load_w(Wuq, Wuq[:], W["l0_mla_w_uq"].rearrange("(c p) n -> p c n", p=128))
        Wuqr = sb("Wuqr", [128, 2, 1152], BF16)
        u_src = Wuq.t[:].rearrange("p c (h d) -> p (c h) d", d=96)
        u_dst = Wuqr.t[:].rearrange("p c (h d) -> p (c h) d", d=96)
        cp("vector", u_dst[:, :, 0:64], u_src[:, :, 0:64], r=[Wuq.b], w=[Wuqr.b])
        cp("vector", u_dst[:, :, 64:80], u_src[:, :, 80:96], r=[Wuq.b], w=[Wuqr.b])
        cp("vector", u_dst[:, :, 80:96], u_src[:, :, 64:80], r=[Wuq.b], w=[Wuqr.b])
        Wukv = sb("Wukv", [128, 1536], BF16)
        ph.load_w(Wukv, Wukv[:], W["l0_mla_w_ukv"])
        Wukv4 = Wukv.t[:].rearrange("p (h d) -> p h d", d=128)
        Wvm = sb("Wvm", [128, 768], BF16)
        cp("gpsimd", Wvm.t[:].rearrange("p (h d) -> p h d", d=64), Wukv4[:, :, 64:128], r=[Wukv.b], w=[Wvm.b])

        x32 = sb("x32", [128, 4, 1024], F32)
        h = sb("h", [128, 4, 1024], BF16)
        hT = sb("hT", [128, 8, 512], BF16)
        junk = sb("junk", [128, 1024], F32)
        ss = sb("ss", [128, 4], F32)
        pos_i = sb("pos_i", [128, 512], I32)
        pos_f = sb("pos_f", [128, 512], F32)
        cosk = sb("cosk", [128, 512], F32)
        sink = sb("sink", [128, 512], F32)
        cosq = sb("cosq", [128, 512], F32)
        sinq = sb("sinq", [128, 512], F32)
        cosmk = sb("cosmk", [128, 512], F32)
        sinmk = sb("sinmk", [128, 512], F32)
        cosmq = sb("cosmq", [128, 512], F32)
        sinmq = sb("sinmq", [128, 512], F32)
        t1 = [sb("t1_%d" % i, [128, 512], F32) for i in range(2)]
        t2 = [sb("t2_%d" % i, [128, 512], F32) for i in range(2)]
        ang, r1, r2, sinr = t1[0], t1[1], t2[0], t2[1]
        qk_st = [sb("qkst%d" % i, [128, 512], BF16) for i in range(2)]
        Vst = sb("Vst", [128, 4, 1536], BF16)
        Vmst = Vst
        Vst5 = Vst.t[:].rearrange("p j (h d) -> p j h d", d=128)
        Vmst5 = Vmst.t[:].rearrange("p j (h d) -> p j h d", d=128)
        I("vector", "memset", Vst[:], 1.0, w=[Vst.b])
        cT32 = sb("cT32", [128, 3, 512], F32)
        sq32 = sb("sq32", [128, 3, 512], F32)
        rstd = sb("rstd", [128, 2, 512], F32)
        cqn = sb("cqn", [128, 2, 512], BF16)
        ckvn = sb("ckvn", [128, 512], BF16)
        kpeT = sb("kpeT", [128, 512], BF16)
        Qst = [sb("Qst%d" % i, [128, 512], BF16) for i in range(2)]
        Kst = [sb("Kst%d" % i, [128, 512], BF16) for i in range(2)]
        tps = [ps("tp%d" % i, [128, 1024], BF16) for i in range(2)]
        ph.pspool(6)
        SC96 = 96 ** -0.5
        SSC = 1.0 - 2e-4

        hTs = [hT, sb("hT1", [128, 8, 512], BF16)]

        def a_loads(tt):
            tsl = slice(tt * 512, (tt + 1) * 512)
            hTc = hTs[tt % 2]
            ph.load(x32, x32[:], x_in[tsl, :].rearrange("(j p) d -> p j d", p=128))
            ph.load(pos_i, pos_i[:], bass.AP(pos_t, tt * 512, [[0, 128], [1, 512]]))

        def a_norm(tt):
            tsl = slice(tt * 512, (tt + 1) * 512)
            hTc = hTs[tt % 2]
            rms_rows(x32, h, gb, junk, ss, 4)

        def a_trans(tt):
            tsl = slice(tt * 512, (tt + 1) * 512)
            hTc = hTs[tt % 2]
            transpose_rows(h, hTc, 8, 4, tps, ident)
            cp("vector", pos_f[:], pos_i[:], r=[pos_i.b], w=[pos_f.b])
            for (icol, tabs) in ((0, (sink, sinq, cosk, cosq, 1, 2, 0.125)), (3, (sinmk, sinmq, cosmk, cosmq, 4, 5, SC96))):
                sk_, sq_, ck_, cq_, c1, c2, qs = tabs
                I("vector", "tensor_scalar", ang[:], pos_f[:], cols[:, icol:icol + 1], None, ALU.mult,
                  r=[pos_f.b, cols.b], w=[ang.b])
                for (rr, off, dst) in ((r1, 0.0, sinr), (r2, 0.25, ck_)):
                    I("vector", "tensor_scalar", rr[:], ang[:], 1.0 / (2 * PI), off, ALU.mult, ALU.add,
                      r=[ang.b], w=[rr.b])
                    cp("vector", pos_i[:], rr[:], r=[rr.b], w=[pos_i.b])
                    cp("vector", rr[:], pos_i[:], r=[pos_i.b], w=[rr.b])
                    I("vector", "scalar_tensor_tensor", rr[:], rr[:], -2 * PI, ang[:], ALU.mult, ALU.add,
                      r=[rr.b, ang.b], w=[rr.b])
                    I("scalar", "activation", dst[:], rr[:], AF.Sin, scale=SSC, bias=off * 2 * PI * SSC,
                      r=[rr.b], w=[dst.b])
                I("vector", "tensor_scalar", sk_[:], sinr[:], cols[:, c1:c1 + 1], None, ALU.mult,
                  r=[sinr.b, cols.b], w=[sk_.b])
                I("vector", "tensor_scalar", sq_[:], sinr[:], cols[:, c2:c2 + 1], None, ALU.mult,
                  r=[sinr.b, cols.b], w=[sq_.b])
                I("vector", "tensor_scalar", cq_[:], ck_[:], qs, None, ALU.mult, r=[ck_.b], w=[cq_.b])

        def a_p1(tt):
            tsl = slice(tt * 512, (tt + 1) * 512)
            hTc = hTs[tt % 2]
            for i in range(12):
                pq, pqr = ph.nxt(), ph.nxt()
                proj_fm(pq, Wqk, i * 128, 128, hTc, 8)
                proj_fm(pqr, Wqkr, i * 128, 128, hTc, 8)
                cs_, sn_ = (cosq, sinq) if i < 6 else (cosk, sink)
                a1, a2, st = t1[i % 2], t2[i % 2], qk_st[i % 2]
                I("vector", "tensor_tensor", a1[:], pq[:, 0:512], cs_[:], ALU.mult, r=[pq.b, cs_.b], w=[a1.b])
                I("vector", "tensor_tensor", a2[:], pqr[:, 0:512], sn_[:], ALU.mult, r=[pqr.b, sn_.b], w=[a2.b])
                I("gpsimd", "tensor_tensor", st[:], a1[:], a2[:], ALU.add, r=[a1.b, a2.b], w=[st.b])
                dst = (qt_dsw if i < 6 else kt_dsw)[i % 6, :, tsl]
                ph.store(st, dst, st[:])

        def a_p2(tt):
            tsl = slice(tt * 512, (tt + 1) * 512)
            hTc = hTs[tt % 2]
            for j in range(4):
                for (c0, n) in ((0, 512), (512, 256)):
                    pv = ph.nxt()
                    for c in range(8):
                        mm(pv[:, 0:n], hTc[:, c, j * 128:(j + 1) * 128], Wv[:, c, c0:c0 + n], c == 0, c == 7,
                           r=[hTc.b, Wv.b], w=[pv.b])
                    cp("scalar", Vst5[:, j, c0 // 64:(c0 + n) // 64, 0:64],
                       pv.t[:, 0:n].rearrange("p (h d) -> p h d", d=64), r=[pv.b], w=[Vst.b])
            for j in range(4):
                ph.store(Vst, v_dsw[:, tt * 4 + j].rearrange("h p d -> p h d"), Vst5[:, j])
            for i in range(3):
                pc = ph.nxt()
                proj_fm(pc, Wc, i * 128, 128, hTc, 8)
                cp("vector", cT32[:, i, :], pc[:, 0:512], r=[pc.b], w=[cT32.b])
                I("scalar", "activation", sq32[:, i, :], pc[:, 0:512], AF.Square, r=[pc.b], w=[sq32.b])
            pssq, psskv = ph.nxt(), ph.nxt()
            mm(pssq[:, 0:512], ones32[:], sq32[:, 0, :], True, False, r=[ones32.b, sq32.b], w=[pssq.b])
            mm(pssq[:, 0:512], ones32[:], sq32[:, 1, :], False, True, r=[ones32.b, sq32.b], w=[pssq.b])
            mm(psskv[:, 0:512], ones32[:], sq32[:, 2, :], True, True, r=[ones32.b, sq32.b], w=[psskv.b])
            I("vector", "tensor_scalar", rstd[:, 0, :], pssq[:, 0:512], 1.0 / 256, 1e-6, ALU.mult, ALU.add,
              r=[pssq.b], w=[rstd.b])
            I("vector", "tensor_scalar", rstd[:, 1, :], psskv[:, 0:512], 1.0 / 128, 1e-6, ALU.mult, ALU.add,
              r=[psskv.b], w=[rstd.b])
            I("scalar", "activation", rstd[:], rstd[:], AF.Ln, r=[rstd.b], w=[rstd.b])
            I("scalar", "activation", rstd[:], rstd[:], AF.Exp, scale=-0.5, r=[rstd.b], w=[rstd.b])
            for c in range(2):
                I("vector", "scalar_tensor_tensor", cqn[:, c, :], cT32[:, c, :], gq[:, c, :], rstd[:, 0, :],
                  ALU.mult, ALU.mult, r=[cT32.b, gq.b, rstd.b], w=[cqn.b])
            I("vector", "scalar_tensor_tensor", ckvn[:], cT32[:, 2, :], gkv[:, 0, :], rstd[:, 1, :],
              ALU.mult, ALU.mult, r=[cT32.b, gkv.b, rstd.b], w=[ckvn.b])
            for hh in range(12):
                pm, pmr = ph.nxt(), ph.nxt()
                for c in range(2):
                    mm(pm[0:96, 0:512], Wuq[:, c, hh * 96:(hh + 1) * 96], cqn[:, c, :], c == 0, c == 1,
                       r=[Wuq.b, cqn.b], w=[pm.b])
                for c in range(2):
                    mm(pmr[0:96, 0:512], Wuqr[:, c, hh * 96:(hh + 1) * 96], cqn[:, c, :], c == 0, c == 1,
                       r=[Wuqr.b, cqn.b], w=[pmr.b])
                st, a1, a2 = Qst[hh % 2], t1[hh % 2], t2[hh % 2]
                I("scalar", "mul", st[0:64, :], pm[0:64, 0:512], SC96, r=[pm.b], w=[st.b])
                I("vector", "tensor_tensor", a1[64:96, :], pm[64:96, 0:512], cosmq[64:96, :], ALU.mult,
                  r=[pm.b, cosmq.b], w=[a1.b])
                I("vector", "tensor_tensor", a2[64:96, :], pmr[64:96, 0:512], sinmq[64:96, :], ALU.mult,
                  r=[pmr.b, sinmq.b], w=[a2.b])
                I("gpsimd", "tensor_tensor", st[64:96, :], a1[64:96, :], a2[64:96, :], ALU.add,
                  r=[a1.b, a2.b], w=[st.b])
                ph.store(st, qt_mla[hh, :, tsl], st[0:96, :])
            pk, pkr = ph.nxt(), ph.nxt()
            proj_fm(pk, Wkpe, 0, 96, hTc, 8)
            proj_fm(pkr, Wkper, 0, 96, hTc, 8)
            a1, a2 = t1[0], t2[0]
            I("vector", "tensor_tensor", a1[64:96, :], pk[64:96, 0:512], cosmk[64:96, :], ALU.mult,
              r=[pk.b, cosmk.b], w=[a1.b])
            I("vector", "tensor_tensor", a2[64:96, :], pkr[64:96, 0:512], sinmk[64:96, :], ALU.mult,
              r=[pkr.b, sinmk.b], w=[a2.b])
            I("gpsimd", "tensor_tensor", kpeT[64:96, :], a1[64:96, :], a2[64:96, :], ALU.add,
              r=[a1.b, a2.b], w=[kpeT.b])
            for hh in range(12):
                pkn = ph.nxt()
                mm(pkn[0:64, 0:512], Wukv4[:, hh, 0:64], ckvn[:], True, True, r=[Wukv.b, ckvn.b], w=[pkn.b])
                st = Kst[hh % 2]
                cp("scalar", st[0:64, :], pkn[0:64, 0:512], r=[pkn.b], w=[st.b])
                cp("gpsimd", st[64:96, :], kpeT[64:96, :], r=[kpeT.b], w=[st.b])
                ph.store(st, kt_mla[hh, :, tsl], st[0:96, :])
            for j in range(4):
                for (c0, n) in ((0, 512), (512, 256)):
                    pv = ph.nxt()
                    mm(pv[:, 0:n], ckvn[:, j * 128:(j + 1) * 128], Wvm[:, c0:c0 + n], True, True,
                       r=[ckvn.b, Wvm.b], w=[pv.b])
                    cp("scalar", Vmst5[:, j, c0 // 64:(c0 + n) // 64, 0:64],
                       pv.t[:, 0:n].rearrange("p (h d) -> p h d", d=64), r=[pv.b], w=[Vmst.b])
            for j in range(4):
                ph.store(Vmst, v_mla[:, tt * 4 + j].rearrange("h p d -> p h d"), Vmst5[:, j])

        a_loads(0)
        a_norm(0)
        a_trans(0)
        for tt in range(NT):
            nxt_ = tt + 1 < NT
            if nxt_:
                a_loads(tt + 1)
            a_p1(tt)
            if nxt_:
                a_norm(tt + 1)
            a_p2(tt)
            if nxt_:
                a_trans(tt + 1)
        ph.close()

    def attn_softmax(pname, units, ydstT, prefetch=None):
        ph = Phase(pname)
        sb, ps = ph.sb, ph.ps
        ident = sb("ident", [128, 128], BF16)
        ph.load_w(ident, ident[:], C["c_ident"])
        masks = {}
        for u in units:
            for shd in u["subs"]:
                mn = shd["mask"]
                if mn not in masks:
                    wdt = C[mn].shape[1]
                    masks[mn] = sb(mn, [128, wdt], BF16)
                    ph.load_w(masks[mn], masks[mn][:], C[mn])
        nsub = max(len(u["subs"]) for u in units)
        KT = [[sb("KT%d_%d" % (i, s_), [128, S], BF16) for s_ in range(nsub)] for i in range(2)]
        QT = [[sb("QT%d_%d" % (i, s_), [128, S], BF16) for s_ in range(nsub)] for i in range(2)]
        VH = [[sb("VH%d_%d" % (i, s_), [128, 32, 128], BF16) for s_ in range(nsub)] for i in range(2)]
        dk0 = units[0]["subs"][0]["dk"]
        for i in range(2):
            for s_ in range(nsub):
                I("gpsimd", "memset", KT[i][s_][64:128, :], 0.0, w=[KT[i][s_].b])
                I("vector", "memset", QT[i][s_][64:128, :], 0.0, w=[QT[i][s_].b])
        pts = [sb("pt%d" % i, [128, 512], BF16) for i in range(5)]
        sps = [ps("sp%d" % i, [128, 512], F32) for i in range(5)]
        accs = [ps("acc%d" % i, [128, 512], F32) for i in range(2)]
        rden = [sb("rden%d" % i, [64, 512], F32) for i in range(2)]
        yst = [sb("yst%d" % i, [64, 512], BF16) for i in range(2)]
        it = 0
        gi = 0
        pend = []

        def flush(keep):
            while len(pend) > keep:
                pend.pop(0)()

        def load_unit(ui_):
            u_ = units[ui_]
            b_ = ui_ % 2
            for si, shd in enumerate(u_["subs"]):
                dk = shd["dk"]
                ph.load(KT[b_][si], KT[b_][si][0:dk, :], shd["kt"])
                ph.load(QT[b_][si], QT[b_][si][0:dk, :], shd["qt"])
                ph.load(VH[b_][si], VH[b_][si][:], shd["v"].rearrange("j p d -> p j d"))

        load_unit(0)
        if prefetch is not None:
            prefetch(ph)
        for ui, u in enumerate(units):
            bi = ui % 2
            for G in range(8):
                if G == 1 and ui + 1 < len(units):
                    load_unit(ui + 1)
                contrib = []
                for si, shd in enumerate(u["subs"]):
                    wb = shd["wb"]
                    for j in range(0, 4 * G + 4):
                        qlo = max(4 * G, j)
                        qhi = 4 * G + 3 if wb is None else min(4 * G + 3, j + wb)
                        if qhi >= qlo:
                            contrib.append((si, j, qlo, qhi))
                acc = accs[gi % 2]
                rd, ys = rden[gi % 2], yst[gi % 2]
                gi += 1
                col = u["col"]
                for idx, (si, j, qlo, qhi) in enumerate(contrib):
                    shd = u["subs"][si]
                    dk = shd["dk"]
                    sp, pt = sps[it % 5], pts[it % 5]
                    it += 1
                    c0 = (qlo - 4 * G) * 128
                    n = (qhi - qlo + 1) * 128
                    need_mask = shd["always"] or (qlo == j)
                    kt_, qt_, vh_ = KT[bi][si], QT[bi][si], VH[bi][si]
                    mm(sp[:, c0:c0 + n], kt_[:, j * 128:(j + 1) * 128], qt_[:, qlo * 128:(qhi + 1) * 128],
                       True, not need_mask, r=[kt_.b, qt_.b], w=[sp.b])
                    if need_mask:
                        mt = masks[shd["mask"]]
                        u0 = (qlo - j) * 128
                        mm(sp[:, c0:c0 + n], ident[:], mt[:, u0:u0 + n], False, True, r=[ident.b, mt.b], w=[sp.b])
                    I("scalar", "activation", pt[:, c0:c0 + n], sp[:, c0:c0 + n], AF.Exp, r=[sp.b], w=[pt.b])
                    last = idx == len(contrib) - 1

                    def st3(acc=acc, c0=c0, n=n, vh_=vh_, j=j, pt=pt, idx=idx, last=last, rd=rd, ys=ys, col=col, G=G):
                        I("tensor", "matmul", acc[:, c0:c0 + n], vh_[:, j, :], pt[:, c0:c0 + n], start=(idx == 0),
                          stop=last, skip_group_check=True, r=[pt.b, vh_.b], w=[acc.b], inc=last)
                        if last:
                            I("vector", "reciprocal", rd[:], acc[64:128, 0:512], r=[acc.b], w=[rd.b])
                            I("vector", "tensor_tensor", ys[:], acc[0:64, 0:512], rd[:], ALU.mult,
                              r=[acc.b, rd.b], w=[ys.b])
                            ph.store(ys, ydstT[col:col + 64, G * 512:(G + 1) * 512], ys[:])
                    pend.append(st3)
                    flush(2)
        flush(0)
        ph.close()

    def vsrc(vt, hh):
        return vt.rearrange("j p (h d) -> p j h d", d=65)[:, :, hh, :]

    def phase_B_dsw():
        units = []
        wbs = (1, 4, 16)
        for jh in range(4):
            subs = []
            for g in range(3):
                hh = g * 4 + jh
                subs.append(dict(kt=kt_dsw[hh // 2, (hh % 2) * 64:(hh % 2) * 64 + 64, :],
                                 qt=qt_dsw[hh // 2, (hh % 2) * 64:(hh % 2) * 64 + 64, :],
                                 v=v_dsw[hh], dk=64, mask="c_mb%d" % g, wb=wbs[g], always=True))
            units.append(dict(subs=subs, col=jh * 64))
        attn_softmax("Bd", units, y0)

    def phase_B_mla(prefetch=None):
        units = []
        for hh in range(12):
            subs = [dict(kt=kt_mla[hh], qt=qt_mla[hh], v=v_mla[hh], dk=96, mask="c_mbc", wb=None, always=False)]
            units.append(dict(subs=subs, col=256 + hh * 64))
        attn_softmax("Bm", units, y0, prefetch=prefetch)


    PRE = {}

    def pre_alloc(stack, name, shape, dt):
        t = TB(stack.enter_context(nc.sbuf_tensor("pre_" + name, shape, dt)), "pre_" + name)
        PRE[name] = t
        return t

    def pre_load_C(ph, L, which):
        pre = "l%d_" % L
        if "Wout" in which:
            t = PRE["C%d_Wout" % L]
            ph.load_w(t, t[:], W[pre + "w_out"].rearrange("(c p) n -> p c n", p=128))
        wgv = W[pre + "ffn_w_gate"].rearrange("(c p) n -> p c n", p=128)
        wuv = W[pre + "ffn_w_up"].rearrange("(c p) n -> p c n", p=128)
        for c in range(8):
            ph.load_w(PRE["C%d_Wg" % L], PRE["C%d_Wg" % L][:, c, :], wgv[:, c, :], group=True)
            ph.load_w(PRE["C%d_Wu" % L], PRE["C%d_Wu" % L][:, c, :], wuv[:, c, :], group=True)

    def pre_load_D(ph):
        wv = W["l1_w_in"].rearrange("(c p) n -> p c n", p=128)
        t = PRE["D_Win"]
        for c in range(8):
            ph.load_w(t, t[:, c, :], wv[:, c, :], group=True)

    def phase_C(L):
        ph = Phase("C%d" % L)
        sb, ps = ph.sb, ph.ps
        pre = "l%d_" % L
        ysrc, nyc = (y0, 8) if L == 0 else (y1, 12)
        xsrc = x_in if L == 0 else x2
        xdst = x1 if L == 0 else x3
        actd = act0 if L == 0 else act1
        ident = sb("ident", [128, 128], BF16)
        ph.load_w(ident, ident[:], C["c_ident"])
        gb = sb("gb", [128, 1024], F32)
        ph.load(gb, gb[:], bcast_row(W[pre + "norm_ffn"], 1024))
        cw = sb("cw", [128, NFC, 4], F32)
        for jj in range(3):
            ph.load(cw, cw[:, :, jj:jj + 1], colvec(W[pre + "ffn_conv_w"][jj], NFC), group=True, slow=True)
        ph.load(cw, cw[:, :, 3:4], colvec(W[pre + "ffn_conv_b"], NFC), group=True, slow=True)
        if ("C%d_Wout" % L) in PRE:
            Wout = PRE["C%d_Wout" % L]
        else:
            Wout = sb("Wout", [128, nyc, 1024], BF16)
            ph.load_w(Wout, Wout[:], W[pre + "w_out"].rearrange("(c p) n -> p c n", p=128))
        Wg, Wu = PRE["C%d_Wg" % L], PRE["C%d_Wu" % L]
        yT = sb("yT", [128, nyc, 512], BF16)
        x32 = sb("x32", [128, 4, 1024], F32)
        h = sb("h", [128, 4, 1024], BF16)
        hT = sb("hT", [128, 8, 512], BF16)
        junk = sb("junk", [128, 1024], F32)
        ss = sb("ss", [128, 4], F32)
        halo = sb("halo", [128, NFC, 2], F32)
        I("vector", "memset", halo[:], 0.0, w=[halo.b])
        gsb = [sb("gsb%d" % i, [128, 514], F32) for i in range(2)]
        acc = [sb("acc%d" % i, [128, 512], F32) for i in range(2)]
        sil = [sb("sil%d" % i, [128, 512], F32) for i in range(2)]
        ast = [sb("ast%d" % i, [128, 512], BF16) for i in range(2)]
        tps = [ps("tp%d" % i, [128, 1024], BF16) for i in range(2)]
        ph.pspool(6)
        hTs = [hT, sb("hT1", [128, 8, 512], BF16)]

        def tslice(tt):
            return slice(tt * 512, (tt + 1) * 512)

        def loads(tt):
            tsl = tslice(tt)
            ph.load(yT, yT[:], ysrc[:, tsl].rearrange("(c p) t -> p c t", p=128))
            ph.load(x32, x32[:], xsrc[tsl, :].rearrange("(j p) d -> p j d", p=128))

        def front(tt):
            tsl = tslice(tt)
            for j in range(4):
                for hf in range(2):
                    po = ph.nxt()
                    for c in range(nyc):
                        mm(po[:, 0:512], yT[:, c, j * 128:(j + 1) * 128], Wout[:, c, hf * 512:(hf + 1) * 512],
                           c == 0, c == nyc - 1, r=[yT.b, Wout.b], w=[po.b])
                    I("vector", "tensor_tensor", x32[:, j, hf * 512:(hf + 1) * 512], po[:, 0:512],
                      x32[:, j, hf * 512:(hf + 1) * 512], ALU.add, r=[po.b, x32.b], w=[x32.b])
            ph.store(x32, xdst[tsl, :].rearrange("(j p) d -> p j d", p=128), x32[:])
            rms_rows(x32, h, gb, junk, ss, 4)

        def gateup(tt, fr):
            tsl = tslice(tt)
            hT_ = hTs[tt % 2]
            for f in fr:
                pg, pu = ph.nxt(), ph.nxt()
                proj_fm(pg, Wg, f * 128, 128, hT_, 8)
                proj_fm(pu, Wu, f * 128, 128, hT_, 8)
                g, a, sl_, st = gsb[f % 2], acc[f % 2], sil[f % 2], ast[f % 2]
                cp("scalar", g[:, 2:514], pg[:, 0:512], r=[pg.b], w=[g.b])
                cp("gpsimd", g[:, 0:2], halo[:, f, :], r=[halo.b], w=[g.b])
                cp("gpsimd", halo[:, f, :], g[:, 512:514], r=[g.b], w=[halo.b])
                I("vector", "tensor_scalar", a[:], g[:, 2:514], cw[:, f, 2:3], cw[:, f, 3:4], ALU.mult, ALU.add,
                  r=[g.b, cw.b], w=[a.b])
                I("vector", "scalar_tensor_tensor", a[:], g[:, 1:513], cw[:, f, 1:2], a[:], ALU.mult, ALU.add,
                  r=[g.b, cw.b, a.b], w=[a.b])
                I("vector", "scalar_tensor_tensor", a[:], g[:, 0:512], cw[:, f, 0:1], a[:], ALU.mult, ALU.add,
                  r=[g.b, cw.b, a.b], w=[a.b])
                I("scalar", "activation", sl_[:], a[:], AF.Silu, r=[a.b], w=[sl_.b])
                I("vector", "tensor_tensor", st[:], sl_[:], pu[:, 0:512], ALU.mult, r=[sl_.b, pu.b], w=[st.b])
                ph.store(st, actd[f, :, tsl], st[:])

        loads(0)
        front(0)
        transpose_rows(h, hTs[0], 8, 4, tps, ident)
        for tt in range(NT):
            nxt_ = tt + 1 < NT
            if nxt_:
                loads(tt + 1)
            gateup(tt, range(0, 11))
            if nxt_:
                front(tt + 1)
            gateup(tt, range(11, NFC))
            if nxt_:
                transpose_rows(h, hTs[(tt + 1) % 2], 8, 4, tps, ident)
        ph.close()

    def phase_Cd(L, prefetch=None):
        ph = Phase("Cd%d" % L)
        sb, ps = ph.sb, ph.ps
        pre = "l%d_" % L
        xsrc = x1 if L == 0 else x3
        actd = act0 if L == 0 else act1
        Wd = sb("Wd", [128, NFC, 1024], BF16)
        wdv = W[pre + "ffn_w_down"].rearrange("(c p) n -> p c n", p=128)
        for c in range(0, NFC, 2):
            ph.load_w(Wd, Wd[:, c:c + 2, :], wdv[:, c:c + 2, :], group=True)
        aTs = [sb("aT%d" % i, [128, NFC, 512], BF16) for i in range(2)]
        x32s = [sb("x32_%d" % i, [128, 4, 1024], F32) for i in range(2)]
        ph.pspool(6)
        if L == 1:
            gb = sb("gb", [128, 1024], F32)
            ph.load(gb, gb[:], bcast_row(W["final_norm"], 1024))
            junk = sb("junk", [128, 1024], F32)
            ss = sb("ss", [128, 4], F32)
            o32 = sb("o32", [128, 4, 1024], F32)

        def loads(tt):
            tsl = slice(tt * 512, (tt + 1) * 512)
            ph.load(aTs[tt % 2], aTs[tt % 2][:], actd[:, :, tsl].rearrange("f p t -> p f t"))
            ph.load(x32s[tt % 2], x32s[tt % 2][:], xsrc[tsl, :].rearrange("(j p) d -> p j d", p=128))

        loads(0)
        if prefetch is not None:
            prefetch(ph)
        for tt in range(NT):
            tsl = slice(tt * 512, (tt + 1) * 512)
            if tt + 1 < NT:
                loads(tt + 1)
            aT, x32 = aTs[tt % 2], x32s[tt % 2]
            for j in range(4):
                for hf in range(2):
                    po = ph.nxt()
                    for f in range(NFC):
                        mm(po[:, 0:512], aT[:, f, j * 128:(j + 1) * 128], Wd[:, f, hf * 512:(hf + 1) * 512],
                           f == 0, f == NFC - 1, r=[aT.b, Wd.b], w=[po.b])
                    I("vector", "tensor_tensor", x32[:, j, hf * 512:(hf + 1) * 512], po[:, 0:512],
                      x32[:, j, hf * 512:(hf + 1) * 512], ALU.add, r=[po.b, x32.b], w=[x32.b])
            if L == 0:
                ph.store(x32, x2[tsl, :].rearrange("(j p) d -> p j d", p=128), x32[:])
            else:
                rms_rows(x32, o32, gb, junk, ss, 4)
                ph.store(o32, out_t[tsl, :].rearrange("(j p) d -> p j d", p=128), o32[:])
        ph.close()

    def phase_D():
        ph = Phase("D")
        sb, ps = ph.sb, ph.ps
        ident = sb("ident", [128, 128], BF16)
        ph.load_w(ident, ident[:], C["c_ident"])
        tri = sb("tri", [128, 128], F32)
        ph.load(tri, tri[:], C["c_tri"])
        ones32 = sb("ones32", [128, 128], F32)
        I("vector", "memset", ones32[:], 1.0, w=[ones32.b])
        gb = sb("gb", [128, 1024], F32)
        ph.load(gb, gb[:], bcast_row(W["l1_norm_mix"], 1024))
        gn = sb("gn", [128, 1024], F32)
        ph.load(gn, gn[:], bcast_row(W["l1_ssm_norm"], 1024))
        cw = sb("cw", [128, 12, 5], F32)
        for jj in range(4):
            ph.load(cw, cw[:, :, jj:jj + 1], colvec(W["l1_ssm_conv_w"][jj], 12), group=True, slow=True)
        ph.load(cw, cw[:, :, 4:5], colvec(W["l1_ssm_conv_b"], 12), group=True, slow=True)
        dtb = sb("dtb", [128, 16], F32)
        ph.load(dtb, dtb[:], bcast_row(W["l1_ssm_dt_bias"], 16))
        Aneg = sb("Aneg", [128, 16], F32)
        ph.load(Aneg, Aneg[:], bcast_row(W["l1_ssm_a_log"], 16))
        I("scalar", "activation", Aneg[:], Aneg[:], AF.Exp, r=[Aneg.b], w=[Aneg.b])
        I("vector", "tensor_scalar", Aneg[:], Aneg[:], -1.0, None, ALU.mult, r=[Aneg.b], w=[Aneg.b])
        Dsk = sb("Dsk", [128, 16], F32)
        ph.load(Dsk, Dsk[:], bcast_row(W["l1_ssm_d"], 16))
        Win = PRE["D_Win"]
        x32 = sb("x32", [128, 4, 1024], F32)
        h = sb("h", [128, 4, 1024], BF16)
        hT = sb("hT", [128, 8, 512], BF16)
        ss = sb("ss", [128, 4], F32)
        halo = sb("halo", [128, 12, 3], F32)
        I("vector", "memset", halo[:], 0.0, w=[halo.b])
        raw = [sb("raw%d" % i, [128, 515], F32) for i in range(2)]
        cac = [sb("cac%d" % i, [128, 512], F32) for i in range(2)]
        xbcT = sb("xbcT", [128, 12, 512], BF16)
        qkst = [sb("qkst%d" % i, [128, 512], BF16) for i in range(2)]
        zs = sb("zs", [128, 4, 1024], F32)
        dta = sb("dta", [128, 4, 16], F32)
        xs_tm2 = [sb("xs_tm%d" % i, [128, 1024], BF16) for i in range(2)]
        B_tm2 = [sb("B_tm%d" % i, [128, 256], BF16) for i in range(2)]
        da = sb("da", [128, 16], F32)
        cscol = sb("cscol", [128, 16], F32)
        Rt = sb("Rt", [128, 16, 128], F32)
        dif = sb("dif", [128, 16, 128], F32)
        cl = sb("cl", [128, 16], F32)
        dend2 = [sb("dend%d" % i, [128, 16], F32) for i in range(2)]
        ecl2 = [sb("ecl%d" % i, [128, 16], F32) for i in range(2)]
        ecs2 = [sb("ecs%d" % i, [128, 16], F32) for i in range(2)]
        Gm = sb("Gm", [128, 2, 128], F32)
        MT = sb("MT", [128, 16, 128], BF16)
        xdt = sb("xdt", [128, 16, 64], BF16)
        xdd = sb("xdd", [128, 16, 64], BF16)

        class VW:
            def __init__(self, ap, b):
                self.ap = ap
                self.b = b

            def __getitem__(self, k):
                return self.ap[k]

        MT2 = [MT, VW(h.t[:, 0:2, :].rearrange("p a (h l) -> p (a h) l", l=128), h.b)]
        xdt2 = [xdt, VW(h.t[:, 2, :].rearrange("p (h d) -> p h d", d=64), h.b)]
        xdd2 = [xdd, VW(h.t[:, 3, :].rearrange("p (h d) -> p h d", d=64), h.b)]
        ytm = sb("ytm", [128, 1024], F32)
        tmp2 = sb("tmp2", [128, 1024], F32)
        junk = tmp2
        prev32 = sb("prev32", [128, 1024], F32)
        prevb = sb("prevb", [128, 1024], BF16)
        I("vector", "memset", prev32[:], 0.0, w=[prev32.b])
        I("vector", "memset", prevb[:], 0.0, w=[prevb.b])
        ss2 = sb("ss2", [128, 2], F32)
        ycst = sb("ycst", [128, 4, 1024], BF16)
        ycT = hT
        tps = [ps("tp%d" % i, [128, 1024], BF16) for i in range(2)]
        ph.pspool(6)

        def v3(ap, d):
            return ap.rearrange("p (h d) -> p h d", d=d)

        xpool = ph.pp[0:2]
        ypy = ph.pp[2:4]
        yq = ph.pp[4:6]

        for tt in range(NT):
            tsl = slice(tt * 512, (tt + 1) * 512)
            ph.load(x32, x32[:], x2[tsl, :].rearrange("(j p) d -> p j d", p=128))
            rms_rows(x32, h, gb, junk, ss, 4)
            transpose_rows(h, hT, 8, 4, tps, ident)
            for i in range(12):
                pc = ph.nxt()
                proj_fm(pc, Win, 1024 + i * 128, 128, hT, 8)
                rw, a = raw[i % 2], cac[i % 2]
                cp("scalar", rw[:, 3:515], pc[:, 0:512], r=[pc.b], w=[rw.b])
                cp("gpsimd", rw[:, 0:3], halo[:, i, :], r=[halo.b], w=[rw.b])
                cp("gpsimd", halo[:, i, :], rw[:, 512:515], r=[rw.b], w=[halo.b])
                I("vector", "tensor_scalar", a[:], rw[:, 3:515], cw[:, i, 3:4], cw[:, i, 4:5], ALU.mult, ALU.add,
                  r=[rw.b, cw.b], w=[a.b])
                for k_ in (2, 1, 0):
                    I("vector", "scalar_tensor_tensor", a[:], rw[:, k_:k_ + 512], cw[:, i, k_:k_ + 1], a[:],
                      ALU.mult, ALU.add, r=[rw.b, cw.b, a.b], w=[a.b])
                I("scalar", "activation", xbcT[:, i, :], a[:], AF.Silu, r=[a.b], w=[xbcT.b])
            for i in range(8):
                pq = ph.nxt()
                proj_fm(pq, Win, 2576 + i * 128, 128, hT, 8)
                st = qkst[i % 2]
                I("scalar", "mul", st[:], pq[:, 0:512], 0.125 if i < 4 else 1.0, r=[pq.b], w=[st.b])
                ph.store(st, (qt_sb if i < 4 else kt_sb)[i % 4, :, tsl], st[:])
            for j in range(4):
                pv = ph.nxt()
                for c in range(8):
                    mm(pv[:, 0:512], hT[:, c, j * 128:(j + 1) * 128], Win[:, c, 3600:4112], c == 0, c == 7,
                       r=[hT.b, Win.b], w=[pv.b])
                cp("scalar", ycst[:, j, 0:512], pv[:, 0:512], r=[pv.b], w=[ycst.b])
            ph.store(ycst, v_sb[tt * 4:(tt + 1) * 4].rearrange("j p f -> p j f"), ycst[:, :, 0:512])
            for j in range(4):
                for hf in range(2):
                    pz = ph.nxt()
                    for c in range(8):
                        mm(pz[:, 0:512], hT[:, c, j * 128:(j + 1) * 128], Win[:, c, hf * 512:(hf + 1) * 512],
                           c == 0, c == 7, r=[hT.b, Win.b], w=[pz.b])
                    I("scalar", "activation", zs[:, j, hf * 512:(hf + 1) * 512], pz[:, 0:512], AF.Silu,
                      r=[pz.b], w=[zs.b])
                pd = ph.nxt()
                for c in range(8):
                    mm(pd[:, 0:16], hT[:, c, j * 128:(j + 1) * 128], Win[:, c, 2560:2576], c == 0, c == 7,
                       r=[hT.b, Win.b], w=[pd.b])
                I("vector", "tensor_tensor", dta[:, j, :], pd[:, 0:16], dtb[:], ALU.add, r=[pd.b, dtb.b], w=[dta.b])
            I("scalar", "activation", dta[:], dta[:], AF.Exp, r=[dta.b], w=[dta.b])
            I("scalar", "activation", dta[:], dta[:], AF.Ln, bias=1.0, r=[dta.b], w=[dta.b])
            def Xgen(j):
                par = j % 2
                csl = slice(j * 128, (j + 1) * 128)
                xs_, B_, MT_, xdt_, xdd_ = xs_tm2[par], B_tm2[par], MT2[par], xdt2[par], xdd2[par]
                dend_, ecl_, ecs_ = dend2[par], ecl2[par], ecs2[par]
                tp = tps[0]
                for c in range(8):
                    I("tensor", "transpose", tp[:, c * 128:(c + 1) * 128], xbcT[:, c, csl], ident[:],
                      r=[xbcT.b, ident.b], w=[tp.b], inc=(c == 7))
                cp("vector", xs_[:], tp[:, 0:1024], r=[tp.b], w=[xs_.b])
                yield
                tp = tps[1]
                for c in range(2):
                    I("tensor", "transpose", tp[:, c * 128:(c + 1) * 128], xbcT[:, 8 + c, csl], ident[:],
                      r=[xbcT.b, ident.b], w=[tp.b], inc=(c == 1))
                cp("scalar", B_[:], tp[:, 0:256], r=[tp.b], w=[B_.b])
                I("vector", "tensor_tensor", da[:], dta[:, j, :], Aneg[:], ALU.mult, r=[dta.b, Aneg.b], w=[da.b])
                yield
                pcs = xpool[0]
                mm(pcs[:, 0:16], tri[:], da[:], True, True, r=[tri.b, da.b], w=[pcs.b])
                cp("vector", cscol[:], pcs[:, 0:16], r=[pcs.b], w=[cscol.b])
                I("vector", "tensor_tensor", Rt[:], tri[:].unsqueeze(1).to_broadcast([128, 16, 128]),
                  da[:].unsqueeze(2).to_broadcast([128, 16, 128]), ALU.mult, r=[tri.b, da.b], w=[Rt.b])
                yield
                for q4 in range(4):
                    pr = xpool[(q4 + 1) % 2]
                    mm(pr[:, 0:512], ones32[:], Rt.t[:, q4 * 4:(q4 + 1) * 4, :].rearrange("p h l -> p (h l)"),
                       True, True, r=[ones32.b, Rt.b], w=[pr.b])
                    pr3 = pr.t[:, 0:512].rearrange("p (h l) -> p h l", l=128)
                    I("vector", "tensor_tensor", dif[:, q4 * 4:(q4 + 1) * 4, :], pr3,
                      cscol[:, q4 * 4:(q4 + 1) * 4].unsqueeze(2).to_broadcast([128, 4, 128]), ALU.subtract,
                      r=[pr.b, cscol.b], w=[dif.b])
                    cp("vector", cl[:, q4 * 4:(q4 + 1) * 4], pr3[:, :, 127], r=[pr.b], w=[cl.b])
                    yield
                I("vector", "tensor_tensor", dif[:, 0:8, :], dif[:, 0:8, :],
                  tri[:].unsqueeze(1).to_broadcast([128, 8, 128]), ALU.mult, r=[dif.b, tri.b], w=[dif.b])
                I("gpsimd", "tensor_tensor", dif[:, 8:16, :], dif[:, 8:16, :],
                  tri[:].unsqueeze(1).to_broadcast([128, 8, 128]), ALU.mult, r=[dif.b, tri.b], w=[dif.b])
                yield
                I("scalar", "activation", dif[:], dif[:], AF.Exp, r=[dif.b], w=[dif.b])
                I("vector", "tensor_tensor", dend_[:], cl[:], cscol[:], ALU.subtract, r=[cl.b, cscol.b], w=[dend_.b])
                yield
                I("scalar", "activation", dend_[:], dend_[:], AF.Exp, r=[dend_.b], w=[dend_.b])
                I("scalar", "activation", ecl_[:], cl[:], AF.Exp, r=[cl.b], w=[ecl_.b])
                I("scalar", "activation", ecs_[:], cscol[:], AF.Exp, r=[cscol.b], w=[ecs_.b])
                yield
                pG = xpool[0]
                for g in range(2):
                    mm(pG[:, g * 128:(g + 1) * 128], xbcT[:, 8 + g, csl], xbcT[:, 10 + g, csl], True, True,
                       r=[xbcT.b], w=[pG.b])
                I("vector", "tensor_tensor", Gm[:], pG.t[:, 0:256].rearrange("p (g l) -> p g l", l=128),
                  tri[:].unsqueeze(1).to_broadcast([128, 2, 128]), ALU.mult, r=[pG.b, tri.b], w=[Gm.b])
                yield
                for g in range(2):
                    I("gpsimd" if g == 0 else "vector", "tensor_tensor", MT_[:, g * 8:(g + 1) * 8, :],
                      dif[:, g * 8:(g + 1) * 8, :], Gm[:, g, :].unsqueeze(1).to_broadcast([128, 8, 128]), ALU.mult,
                      r=[dif.b, Gm.b], w=[MT_.b])
                    yield
                I("vector", "tensor_tensor", xdt_[:], v3(xs_[:], 64),
                  dta[:, j, :].unsqueeze(2).to_broadcast([128, 16, 64]), ALU.mult, r=[xs_.b, dta.b], w=[xdt_.b])
                yield
                I("gpsimd", "tensor_tensor", xdd_[:], xdt_[:], dend_[:].unsqueeze(2).to_broadcast([128, 16, 64]),
                  ALU.mult, r=[xdt_.b, dend_.b], w=[xdd_.b])
                yield

            def Ygen(j):
                par = j % 2
                csl = slice(j * 128, (j + 1) * 128)
                xs_, B_, MT_, xdt_, xdd_ = xs_tm2[par], B_tm2[par], MT2[par], xdt2[par], xdd2[par]
                dend_, ecl_, ecs_ = dend2[par], ecl2[par], ecs2[par]
                py = ypy
                for hh in range(16):
                    mm(py[hh // 8][:, (hh % 8) * 64:(hh % 8) * 64 + 64], MT_[:, hh, :], xdt_[:, hh, :], True, True,
                       r=[MT_.b, xdt_.b], w=[py[hh // 8].b], inc=(hh % 8 == 7))
                    if hh % 8 == 7:
                        yield
                for g in range(2):
                    pyo = yq[g]
                    mm(pyo[:, 0:512], xbcT[:, 10 + g, csl], prevb[:, g * 512:(g + 1) * 512], True, True,
                       r=[xbcT.b, prevb.b], w=[pyo.b])
                    gs = slice(g * 512, (g + 1) * 512)
                    I("vector", "tensor_tensor", v3(ytm.t[:, gs], 64), v3(pyo.t[:, 0:512], 64),
                      ecs_[:, g * 8:(g + 1) * 8].unsqueeze(2).to_broadcast([128, 8, 64]), ALU.mult,
                      r=[pyo.b, ecs_.b], w=[ytm.b])
                    yield
                    I("vector", "tensor_tensor", ytm[:, gs], ytm[:, gs], py[g][:, 0:512], ALU.add,
                      r=[ytm.b, py[g].b], w=[ytm.b])
                    yield
                I("gpsimd", "tensor_tensor", v3(tmp2.t[:], 64), v3(xs_[:], 64),
                  Dsk[:].unsqueeze(2).to_broadcast([128, 16, 64]), ALU.mult, r=[xs_.b, Dsk.b], w=[tmp2.b])
                yield
                I("vector", "tensor_tensor", ytm[:], ytm[:], tmp2[:], ALU.add, r=[ytm.b, tmp2.b], w=[ytm.b])
                yield
                for g in range(2):
                    pst = yq[g]
                    gs = slice(g * 512, (g + 1) * 512)
                    mm(pst[:, 0:512], B_[:, g * 128:(g + 1) * 128],
                       xdd_[:, g * 8:(g + 1) * 8, :].rearrange("p h d -> p (h d)"), True, True,
                       r=[B_.b, xdd_.b], w=[pst.b])
                    I("gpsimd", "tensor_tensor", v3(prev32.t[:, gs], 64), v3(prev32.t[:, gs], 64),
                      ecl_[:, g * 8:(g + 1) * 8].unsqueeze(2).to_broadcast([128, 8, 64]), ALU.mult,
                      r=[prev32.b, ecl_.b], w=[prev32.b])
                    yield
                    I("vector", "tensor_tensor", prev32[:, gs], prev32[:, gs], pst[:, 0:512], ALU.add,
                      r=[prev32.b, pst.b], w=[prev32.b])
                    yield
                cp("scalar", prevb[:], prev32[:], r=[prev32.b], w=[prevb.b])
                I("vector", "tensor_tensor", ytm[:], ytm[:], zs[:, j, :], ALU.mult, r=[ytm.b, zs.b], w=[ytm.b])
                yield
                I("vector", "memset", ss2[:], 0.0, w=[ss2.b])
                for g in range(2):
                    I("scalar", "activation", junk[:, 0:512], ytm[:, g * 512:(g + 1) * 512], AF.Square,
                      accum_out=ss2[:, g:g + 1], r=[ytm.b, ss2.b], w=[junk.b, ss2.b])
                yield
                I("vector", "tensor_scalar", ss2[:], ss2[:], 1.0 / 512, 1e-6, ALU.mult, ALU.add, r=[ss2.b], w=[ss2.b])
                I("scalar", "activation", ss2[:], ss2[:], AF.Ln, r=[ss2.b], w=[ss2.b])
                I("scalar", "activation", ss2[:], ss2[:], AF.Exp, scale=-0.5, r=[ss2.b], w=[ss2.b])
                yield
                for g in range(2):
                    gs = slice(g * 512, (g + 1) * 512)
                    I("vector", "scalar_tensor_tensor", ycst[:, j, gs], ytm[:, gs], ss2[:, g:g + 1], gn[:, gs],
                      ALU.mult, ALU.mult, r=[ytm.b, ss2.b, gn.b], w=[ycst.b])
                yield

            def run_interleaved(gens):
                gens = [g for g in gens if g is not None]
                while gens:
                    for g in list(gens):
                        try:
                            next(g)
                        except StopIteration:
                            gens.remove(g)

            run_interleaved([Xgen(0)])
            for j in range(4):
                run_interleaved([Ygen(j), Xgen(j + 1) if j < 3 else None])
            transpose_rows(ycst, ycT, 8, 4, tps, ident)
            ph.store(ycT, y1[0:1024, tsl].rearrange("(c p) t -> p c t", p=128), ycT[:])
        ph.close()

    def phase_E(prefetch=None):
        ph = Phase("E")
        sb, ps = ph.sb, ph.ps
        ident = sb("ident", [128, 128], BF16)
        ph.load_w(ident, ident[:], C["c_ident"])
        mbs = sb("mbs", [128, 512], BF16)
        ph.load_w(mbs, mbs[:], C["c_mbs"])
        smk = sb("smk", [128, 128], BF16)
        ph.load_w(smk, smk[:], C["c_sm"])
        negtri = sb("negtri", [128, 128], BF16)
        ph.load_w(negtri, negtri[:], C["c_negtri"])
        negones = sb("negones", [128, 128], BF16)
        I("vector", "memset", negones[:], -1.0, w=[negones.b])
        Vall = sb("Vall", [128, 32, 512], BF16)
        for q4 in range(4):
            ph.load(Vall, Vall[:, q4 * 8:(q4 + 1) * 8, :], v_sb[q4 * 8:(q4 + 1) * 8].rearrange("j p f -> p j f"),
                    group=True)
        KT = [sb("KT%d" % i, [128, S], BF16) for i in range(2)]
        QT = [sb("QT%d" % i, [128, S], BF16) for i in range(2)]
        for i in range(2):
            I("gpsimd", "memset", KT[i][64:128, :], 0.0, w=[KT[i].b])
            I("vector", "memset", QT[i][64:128, :], 0.0, w=[QT[i].b])
        Rb = sb("Rb", [128, 512], BF16)
        yst = [sb("yst%d" % i, [64, 512], BF16) for i in range(2)]
        pzs = [ps("pz%d" % i, [128, 512], F32) for i in range(2)]
        dps = ps("dps", [128, 512], F32)

        def filler():
            mm(dps[:, 0:512], negtri[:], mbs[:, 0:512], True, True, r=[negtri.b, mbs.b], w=[dps.b], inc=False)

        p2s = [ps("p2%d" % i, [128, 512], F32) for i in range(3)]
        accs = [ps("acc%d" % i, [128, 512], F32) for i in range(2)]
        it = 0
        gi = 0
        pend1, pend2, pend3 = [], [], []

        def flush(lst, keep):
            while len(lst) > keep:
                lst.pop(0)()

        e3 = [sb("e3_%d" % i, [128, 512], F32) for i in range(6)]
        sp3 = [sb("sp3_%d" % i, [128, 512], BF16) for i in range(6)]
        wf3 = [sb("wf3_%d" % i, [128, 512], F32) for i in range(6)]
        wt3 = [sb("wt3_%d" % i, [128, 512], BF16) for i in range(6)]
        def load_head(h_):
            ph.load(KT[h_ % 2], KT[h_ % 2][0:64, :], kt_sb[h_ // 2, (h_ % 2) * 64:(h_ % 2) * 64 + 64, :])
            ph.load(QT[h_ % 2], QT[h_ % 2][0:64, :], qt_sb[h_ // 2, (h_ % 2) * 64:(h_ % 2) * 64 + 64, :])

        load_head(0)
        if prefetch is not None:
            prefetch(ph)
        for w_ in range(32):
            mm(p2s[0][:, 0:512], negtri[:], QT[0][:, 0:512], True, True, r=[KT[0].b, QT[0].b, Vall.b, negtri.b,
               mbs.b, smk.b, ident.b], w=[p2s[0].b], inc=(w_ == 31))
        for hh in range(8):
            kt_, qt_ = KT[hh % 2], QT[hh % 2]
            for G in range(8):
                if G == 1 and hh + 1 < 8:
                    load_head(hh + 1)
                acc = accs[gi % 2]
                ys = yst[gi % 2]
                gi += 1
                for j in range(4 * G + 3, -1, -1):
                    qlo, qhi = max(4 * G, j), 4 * G + 3
                    c0 = (qlo - 4 * G) * 128
                    n = (qhi - qlo + 1) * 128
                    cs_ = slice(c0, c0 + n)
                    pz, p2 = pzs[it % 2], p2s[it % 3]
                    e_, sp_, wf, wt = e3[it % 6], sp3[it % 6], wf3[it % 6], wt3[it % 6]
                    it += 1
                    kb = kt_[:, j * 128:(j + 1) * 128]
                    qc = qt_[:, qlo * 128:(qhi + 1) * 128]
                    diag = (qlo == j)
                    first = (j == 4 * G + 3)
                    mm(pz[:, cs_], kb, qc, True, True, r=[kt_.b, qt_.b], w=[pz.b])
                    filler()
                    I("scalar", "activation", e_[:, cs_], pz[:, cs_], AF.Exp, r=[pz.b], w=[e_.b])

                    def st1b(sp_=sp_, e_=e_, cs_=cs_, diag=diag, c0=c0):
                        I("scalar", "activation", sp_[:, cs_], e_[:, cs_], AF.Ln, bias=1.0, r=[e_.b], w=[sp_.b])
                        if diag:
                            I("gpsimd", "tensor_tensor", sp_[:, c0:c0 + 128], sp_[:, c0:c0 + 128], smk[:], ALU.mult,
                              r=[sp_.b, smk.b], w=[sp_.b])
                    pend1.append(st1b)
                    flush(pend1, 1)

                    def st2(p2=p2, cs_=cs_, sp_=sp_, first=first, diag=diag, n=n, wf=wf, wt=wt, e_=e_, j=j):
                        if first:
                            I("gpsimd", "memset", Rb[:], 0.0, w=[Rb.b])
                        useR = not first
                        mm(p2[:, cs_], negtri[:], sp_[:, cs_], True, not (useR or diag), r=[negtri.b, sp_.b],
                           w=[p2.b])
                        if useR:
                            mm(p2[:, cs_], negones[:], Rb[:, cs_], False, not diag, r=[negones.b, Rb.b], w=[p2.b])
                        if diag:
                            mm(p2[:, cs_], ident[:], mbs[:, 0:n], False, True, r=[ident.b, mbs.b], w=[p2.b])
                        if j > 0:
                            I("vector", "tensor_tensor", Rb[:, cs_], Rb[:, cs_], sp_[:, cs_], ALU.add,
                              r=[Rb.b, sp_.b], w=[Rb.b])
                        I("scalar", "activation", wf[:, cs_], p2[:, cs_], AF.Exp, r=[p2.b], w=[wf.b])
                        I("vector", "tensor_tensor", wt[:, cs_], wf[:, cs_], e_[:, cs_], ALU.mult,
                          r=[wf.b, e_.b], w=[wt.b])

                    def st3(acc=acc, cs_=cs_, j=j, hh=hh, wt=wt, first=first, ys=ys, G=G):
                        lastj = (j == 0)
                        v0 = min(hh * 64, 384)
                        r0 = hh * 64 - v0
                        I("tensor", "matmul", acc[:, cs_], Vall[:, j, v0:v0 + 128], wt[:, cs_],
                          start=first, stop=lastj, skip_group_check=True, r=[wt.b, Vall.b], w=[acc.b], inc=lastj)
                        filler()
                        if lastj:
                            cp("vector", ys[:], acc[r0:r0 + 64, 0:512], r=[acc.b], w=[ys.b])
                            col = 1024 + hh * 64
                            ph.store(ys, y1[col:col + 64, G * 512:(G + 1) * 512], ys[:])
                    pend2.append(st2)
                    flush(pend2, 3)
                    pend3.append(st3)
                    flush(pend3, 4)
        flush(pend1, 0)
        flush(pend2, 0)
        flush(pend3, 0)
        ph.close()

    def run_all():
        phase_A()
        phase_B_dsw()
        with ExitStack() as st0:
            pre_alloc(st0, "C0_Wout", [128, 8, 1024], BF16)
            pre_alloc(st0, "C0_Wg", [128, 8, FF], BF16)
            pre_alloc(st0, "C0_Wu", [128, 8, FF], BF16)
            phase_B_mla(prefetch=lambda ph: pre_load_C(ph, 0, ("Wout",)))
            phase_C(0)
        with ExitStack() as st1:
            pre_alloc(st1, "D_Win", [128, 8, 4112], BF16)
            phase_Cd(0, prefetch=pre_load_D)
            phase_D()
        with ExitStack() as st2:
            pre_alloc(st2, "C1_Wg", [128, 8, FF], BF16)
            pre_alloc(st2, "C1_Wu", [128, 8, FF], BF16)
            phase_E(prefetch=lambda ph: pre_load_C(ph, 1, ()))
            phase_C(1)
        phase_Cd(1)

    run_all()
    es.close()
    return nc


_CACHE = {}


def kernel(**inputs):
    stop_after = os.environ.get("K_STOP") or None
    dbg = bool(os.environ.get("K_DBG"))
    nc = build(stop_after=stop_after, dbg=dbg)
    consts = host_consts()
    in_maps = []
    for b in range(8):
        m = {"x": np.ascontiguousarray(inputs["x"][b], dtype=np.float32),
             "positions": np.ascontiguousarray(inputs["positions"][b], dtype=np.int32)}
        for n, _ in W_NAMES:
            m[n] = np.ascontiguousarray(inputs[n], dtype=np.float32)
        m.update(consts)
        in_maps.append(m)
    res = run_bass_kernel_spmd(nc, in_maps, core_ids=list(range(8)))
    if dbg:
        _CACHE["res"] = res.results
    return np.stack([res.results[b]["out"] for b in range(8)], axis=0).astype(np.float32)
```

```python
import math
import os
from contextlib import ExitStack

import numpy as np
import concourse.bass as bass
import concourse.mybir as mybir
from concourse.bass_utils import run_bass_kernel_spmd

F32 = mybir.dt.float32
BF16 = mybir.dt.bfloat16
I32 = mybir.dt.int32
AF = mybir.ActivationFunctionType
ALU = mybir.AluOpType

S = 4096
DM = 1024
NT = 8
FF = 2816
NFC = 22
NEG = -30000.0
PI = math.pi


class Buf:
    __slots__ = ("name", "w", "rs", "excl")

    def __init__(self, name):
        self.name = name
        self.w = None
        self.rs = {}
        self.excl = False


class TB:
    def __init__(self, t, name):
        self.t = t
        self.b = Buf(name)
        self.name = name

    def __getitem__(self, k):
        return self.t[k]


class Prog:
    ENGS = ("sync", "scalar", "vector", "gpsimd", "tensor")

    def __init__(self, nc, es, npool=28, npool_sw=20):
        self.nc = nc
        self.lists = {e: [] for e in self.ENGS}
        self.cnt = {e: 0 for e in self.ENGS}
        self.seen = {e: {} for e in self.ENGS}
        self.sems = {e: es.enter_context(nc.semaphore("s_" + e)) for e in self.ENGS}
        self.pool = [[es.enter_context(nc.semaphore("d%d" % i)), 0, ("d", i)] for i in range(npool + npool_sw)]
        self.npool = npool
        self.keymap = {}
        self.pool_next = 0
        self.pool_next_sw = npool
        self.ninstr = 0

    def new_phase(self):
        self.keymap = {}
        self.pool_next = 0
        self.pool_next_sw = self.npool

    def _ent(self, key, eng):
        if key not in self.keymap:
            if eng == "gpsimd":
                assert self.pool_next_sw < len(self.pool), "out of sw dma sems"
                self.keymap[key] = self.pool[self.pool_next_sw]
                self.pool_next_sw += 1
            else:
                assert self.pool_next < self.npool, "out of dma sems"
                self.keymap[key] = self.pool[self.pool_next]
                self.pool_next += 1
        return self.keymap[key]

    def _wait(self, eng, key, sem, val):
        s = self.seen[eng]
        if s.get(key, 0) >= val:
            return
        s[key] = val
        self.lists[eng].append(lambda e, sem=sem, val=val: e.wait_ge(sem, val))

    def _dep(self, eng, dep, same_ok=False):
        if dep is None:
            return
        key, seq = dep
        if isinstance(key, str):
            if key == eng and same_ok:
                return
            self._wait(eng, key, self.sems[key], seq)
        else:
            self._wait(eng, key, self.pool[key[1]][0], seq)

    def op(self, eng, fn, reads=(), writes=(), inc=True):
        for b in reads:
            if b.w is not None:
                self._dep(eng, b.w, same_ok=(eng == "tensor"))
            if b.excl:
                for re_, rs_ in b.rs.items():
                    self._dep(eng, (re_, rs_), same_ok=True)
        pe = (eng == "tensor")
        for b in writes:
            if b.w is not None:
                self._dep(eng, b.w, same_ok=pe)
            for re_, rs_ in b.rs.items():
                self._dep(eng, (re_, rs_), same_ok=pe)
        if inc:
            self.cnt[eng] += 1
            seq = self.cnt[eng]
            sem = self.sems[eng]
            self.lists[eng].append(lambda e, fn=fn, sem=sem: fn(e).then_inc(sem, 1))
        else:
            seq = self.cnt[eng] + 1
            self.lists[eng].append(fn)
        self.ninstr += 1
        for b in reads:
            if seq > b.rs.get(eng, 0):
                b.rs[eng] = seq
        for b in writes:
            b.w = (eng, seq)
            b.rs = {}
        return seq

    def dma(self, eng, out, in_, reads=(), writes=(), key=None, group=False, slow=False):
        ent = self._ent(key, eng)
        k = ent[2]
        for b in reads:
            if b.w is not None:
                self._dep(eng, b.w)
        for b in writes:
            if b.w is not None and not (group and b.w[0] == k):
                self._dep(eng, b.w)
            for re_, rs_ in b.rs.items():
                self._dep(eng, (re_, rs_))
        ent[1] += 16
        seq = ent[1]
        sem = ent[0]
        self.lists[eng].append(
            lambda e, out=out, in_=in_, sem=sem, slow=slow: (
                e.dma_start(out=out, in_=in_, allow_slow_non_contiguous=True) if slow
                else e.dma_start(out=out, in_=in_)).then_inc(sem, 16))
        self.ninstr += 1
        for b in reads:
            if seq > b.rs.get(k, 0):
                b.rs[k] = seq
        for b in writes:
            b.w = (k, seq)
            b.rs = {}
        return seq

    def barrier_all(self):
        for eng in self.ENGS:
            for other in self.ENGS:
                if other != eng and self.cnt[other] > 0:
                    self._wait(eng, other, self.sems[other], self.cnt[other])
            for sem, cnt, k in self.pool:
                if cnt > 0:
                    self._wait(eng, k, sem, cnt)

    def emit(self):
        nc = self.nc
        lists = self.lists
        with nc.Block() as block:
            @block.sync
            def _(e):
                for f in lists["sync"]:
                    f(e)

            @block.scalar
            def _(e):
                for f in lists["scalar"]:
                    f(e)

            @block.vector
            def _(e):
                for f in lists["vector"]:
                    f(e)

            @block.gpsimd
            def _(e):
                for f in lists["gpsimd"]:
                    f(e)

            @block.tensor
            def _(e):
                for f in lists["tensor"]:
                    f(e)
        self.lists = {e: [] for e in self.ENGS}


def host_consts():
    c = {}
    kk = np.arange(128)[:, None]

    def mb(width, window, dil, strict=False):
        u = np.arange(width)[None, :]
        d = u - kk
        ok = (d >= (1 if strict else 0)) & (d <= window) & (d % dil == 0)
        return np.where(ok, 0.0, NEG).astype(np.float32)

    c["c_ident"] = np.eye(128, dtype=np.float32)
    c["c_mb0"] = mb(128 + 512, 128, 1)
    c["c_mb1"] = mb(512 + 512, 512, 4)
    c["c_mb2"] = mb(2048 + 512, 2048, 16)
    c["c_mbc"] = mb(512, 1 << 30, 1)
    c["c_mbs"] = mb(512, 1 << 30, 1, strict=True)
    u = np.arange(128)[None, :]
    c["c_sm"] = (u - kk > 0).astype(np.float32)
    c["c_tri"] = (kk <= u).astype(np.float32)
    c["c_negtri"] = -(kk >= u).astype(np.float32)
    cols = np.zeros((128, 8), np.float32)
    p = np.arange(128)
    cols[:, 0] = 10000.0 ** (-(p % 32) / 32.0)
    sg = np.where((p % 64) < 32, -1.0, 1.0)
    cols[:, 1] = sg
    cols[:, 2] = sg * 0.125
    pm = p - 64
    inm = (pm >= 0) & (pm < 32)
    cols[:, 3] = np.where(inm, 10000.0 ** (-(pm % 16) / 16.0), 0.0)
    sgm = np.where(pm < 16, -1.0, 1.0)
    cols[:, 4] = np.where(inm, sgm, 0.0)
    cols[:, 5] = np.where(inm, sgm * 96 ** -0.5, 0.0)
    c["c_cols"] = cols.astype(np.float32)
    return c


W_NAMES = [
    ("l0_norm_mix", [1024]), ("l0_w_in", [1024, 2720]), ("l0_mla_q_norm", [256]), ("l0_mla_w_uq", [256, 1152]),
    ("l0_mla_kv_norm", [128]), ("l0_mla_w_ukv", [128, 1536]), ("l0_w_out", [1024, 1024]), ("l0_norm_ffn", [1024]),
    ("l0_ffn_w_gate", [1024, 2816]), ("l0_ffn_w_up", [1024, 2816]), ("l0_ffn_conv_w", [3, 2816]),
    ("l0_ffn_conv_b", [2816]), ("l0_ffn_w_down", [2816, 1024]),
    ("l1_norm_mix", [1024]), ("l1_w_in", [1024, 4112]), ("l1_ssm_conv_w", [4, 1536]), ("l1_ssm_conv_b", [1536]),
    ("l1_ssm_dt_bias", [16]), ("l1_ssm_a_log", [16]), ("l1_ssm_d", [16]), ("l1_ssm_norm", [1024]),
    ("l1_w_out", [1536, 1024]), ("l1_norm_ffn", [1024]), ("l1_ffn_w_gate", [1024, 2816]),
    ("l1_ffn_w_up", [1024, 2816]), ("l1_ffn_conv_w", [3, 2816]), ("l1_ffn_conv_b", [2816]),
    ("l1_ffn_w_down", [2816, 1024]), ("final_norm", [1024]),
]


def build(stop_after=None, dbg=False):
    nc = bass.Bass("TRN2", target_bir_lowering=False)
    skind = "ExternalOutput" if dbg else "Internal"
    x_in = nc.dram_tensor("x", [S, DM], F32, kind="ExternalInput").ap()
    pos_t = nc.dram_tensor("positions", [S], I32, kind="ExternalInput")
    W = {n: nc.dram_tensor(n, shp, F32, kind="ExternalInput").ap() for n, shp in W_NAMES}
    C = {n: nc.dram_tensor(n, list(a.shape), F32, kind="ExternalInput").ap() for n, a in host_consts().items()}
    out_t = nc.dram_tensor("out", [S, DM], F32, kind="ExternalOutput").ap()

    def scratch(name, shape, dt):
        return nc.dram_tensor(name, shape, dt, kind=skind).ap()

    qt_dsw = scratch("qt_dsw", [6, 128, S], BF16)
    kt_dsw = scratch("kt_dsw", [6, 128, S], BF16)
    v_dsw = scratch("v_dsw", [12, 32, 128, 128], BF16)
    qt_mla = scratch("qt_mla", [12, 96, S], BF16)
    kt_mla = scratch("kt_mla", [12, 96, S], BF16)
    v_mla = scratch("v_mla", [12, 32, 128, 128], BF16)
    y0 = scratch("y0", [1024, S], BF16)
    x1 = scratch("x1", [S, DM], F32)
    act0 = scratch("act0", [NFC, 128, S], BF16)
    x2 = scratch("x2", [S, DM], F32)
    y1 = scratch("y1", [1536, S], BF16)
    qt_sb = scratch("qt_sb", [4, 128, S], BF16)
    kt_sb = scratch("kt_sb", [4, 128, S], BF16)
    v_sb = scratch("v_sb", [32, 128, 512], BF16)
    x3 = scratch("x3", [S, DM], F32)
    act1 = scratch("act1", [NFC, 128, S], BF16)

    es = ExitStack()
    P = Prog(nc, es)

    def I(eng, meth, *args, r=(), w=(), inc=True, **kw):
        return P.op(eng, lambda e: getattr(e, meth)(*args, **kw), r, w, inc=inc)

    def cp(eng, out, in_, r, w):
        if eng == "scalar":
            return I(eng, "copy", out, in_, r=r, w=w)
        return I(eng, "tensor_copy", out, in_, r=r, w=w)

    def mm(out, lhsT, rhs, start, stop, r, w, inc=None):
        return I("tensor", "matmul", out, lhsT, rhs, start=start, stop=stop, r=r, w=w,
                 inc=(stop if inc is None else inc))

    def bcast_row(ap1d, n):
        return ap1d.partition_broadcast(128)

    def colvec(ap1d, nchunk):
        t = ap1d.tensor
        off = ap1d.offset
        return bass.AP(t, off, [[1, 128], [128, nchunk], [1, 1]])

    class Phase:
        def __init__(self, name):
            self.name = name
            self.st = ExitStack()
            P.new_phase()
            self.pp = []
            self.pi = 0

        def sb(self, n, s, d):
            return TB(self.st.enter_context(nc.sbuf_tensor(self.name + "_" + n, s, d)), self.name + "_" + n)

        def ps(self, n, s, d):
            t = TB(self.st.enter_context(nc.psum_tensor(self.name + "_" + n, s, d)), self.name + "_" + n)
            t.b.excl = True
            return t

        def pspool(self, k):
            self.pp = [self.ps("pp%d" % i, [128, 512], F32) for i in range(k)]

        def nxt(self):
            t = self.pp[self.pi % len(self.pp)]
            self.pi += 1
            return t

        def load_w(self, tb, out, in_, group=False):
            P.dma("gpsimd", out, in_, writes=[tb.b], key=tb.name, group=group)

        def load(self, tb, out, in_, group=False, slow=False):
            P.dma("sync", out, in_, writes=[tb.b], key=tb.name, group=group, slow=slow)

        def store(self, tb, out, in_):
            P.dma("sync", out, in_, reads=[tb.b], key=tb.name + "_st")

        def close(self):
            print("phase", self.name, "instr", P.ninstr, dict(P.cnt))
            P.barrier_all()
            P.emit()
            self.st.close()

    def rms_rows(x, h, gb, junk, ss, J, width=1024, eps=1e-6):
        I("vector", "memset", ss[:, 0:J], 0.0, w=[ss.b])
        for j in range(J):
            I("scalar", "activation", junk[:, 0:width], x[:, j, 0:width], AF.Square, accum_out=ss[:, j:j + 1],
              r=[x.b, ss.b], w=[junk.b, ss.b])
        I("vector", "tensor_scalar", ss[:, 0:J], ss[:, 0:J], 1.0 / width, eps, ALU.mult, ALU.add, r=[ss.b], w=[ss.b])
        I("scalar", "activation", ss[:, 0:J], ss[:, 0:J], AF.Ln, r=[ss.b], w=[ss.b])
        I("scalar", "activation", ss[:, 0:J], ss[:, 0:J], AF.Exp, scale=-0.5, r=[ss.b], w=[ss.b])
        for j in range(J):
            I("vector", "scalar_tensor_tensor", h[:, j, 0:width], x[:, j, 0:width], ss[:, j:j + 1], gb[:, 0:width],
              ALU.mult, ALU.mult, r=[x.b, ss.b, gb.b], w=[h.b])

    def transpose_rows(h, hT, nchunk, J, tps, ident):
        for c in range(nchunk):
            tp = tps[c % 2]
            for j in range(J):
                I("tensor", "transpose", tp[:, j * 128:(j + 1) * 128], h[:, j, c * 128:(c + 1) * 128], ident[:],
                  r=[h.b, ident.b], w=[tp.b], inc=(j == J - 1))
            cp("vector" if c % 2 == 0 else "scalar", hT[:, c, 0:J * 128], tp[:, 0:J * 128], r=[tp.b], w=[hT.b])

    def proj_fm(pt, Wt, col0, M, hT, nK, N=512):
        for c in range(nK):
            mm(pt[0:M, 0:N], Wt[:, c, col0:col0 + M], hT[:, c, 0:N], c == 0, c == nK - 1, r=[Wt.b, hT.b], w=[pt.b])

    def phase_A():
        ph = Phase("A")
        sb, ps = ph.sb, ph.ps
        ident = sb("ident", [128, 128], BF16)
        ph.load_w(ident, ident[:], C["c_ident"])
        cols = sb("cols", [128, 8], F32)
        ph.load(cols, cols[:], C["c_cols"])
        ones32 = sb("ones32", [128, 128], F32)
        I("vector", "memset", ones32[:], 1.0, w=[ones32.b])
        gb = sb("gb", [128, 1024], F32)
        ph.load(gb, gb[:], bcast_row(W["l0_norm_mix"], 1024))
        gq = sb("gq", [128, 2, 1], F32)
        ph.load(gq, gq[:], colvec(W["l0_mla_q_norm"], 2), slow=True)
        gkv = sb("gkv", [128, 1, 1], F32)
        ph.load(gkv, gkv[:], colvec(W["l0_mla_kv_norm"], 1), slow=True)
        win = W["l0_w_in"].rearrange("(c p) n -> p c n", p=128)
        Wqk = sb("Wqk", [128, 8, 1536], BF16)
        for c in range(8):
            ph.load_w(Wqk, Wqk[:, c, :], win[:, c, 0:1536], group=True)
        Wqkr = sb("Wqkr", [128, 8, 1536], BF16)
        v_src = Wqk.t[:].rearrange("p c (h d) -> p (c h) d", d=64)
        v_dst = Wqkr.t[:].rearrange("p c (h d) -> p (c h) d", d=64)
        cp("vector", v_dst[:, :, 0:32], v_src[:, :, 32:64], r=[Wqk.b], w=[Wqkr.b])
        cp("gpsimd", v_dst[:, :, 32:64], v_src[:, :, 0:32], r=[Wqk.b], w=[Wqkr.b])
        Wv = sb("Wv", [128, 8, 768], BF16)
        ph.load_w(Wv, Wv[:], win[:, :, 1536:2304])
        Wc = sb("Wc", [128, 8, 384], BF16)
        ph.load_w(Wc, Wc[:], win[:, :, 2304:2688])
        Wkpe = sb("Wkpe", [128, 8, 96], BF16)
        I("vector", "memset", Wkpe[:], 0.0, w=[Wkpe.b])
        ph.load_w(Wkpe, Wkpe[:, :, 64:96], win[:, :, 2688:2720])
        Wkper = sb("Wkper", [128, 8, 96], BF16)
        I("vector", "memset", Wkper[:], 0.0, w=[Wkper.b])
        cp("vector", Wkper[:, :, 64:80], Wkpe[:, :, 80:96], r=[Wkpe.b], w=[Wkper.b])
        cp("vector", Wkper[:, :, 80:96], Wkpe[:, :, 64:80], r=[Wkpe.b], w=[Wkper.b])
        Wuq = sb("Wuq", [128, 2, 1152], BF16)
        ph.load_w(Wuq, Wuq[:], W["l0_mla_w_uq"].rearrange("(c p) n -> p c n", p=128))
        Wuqr = sb("Wuqr", [128, 2, 1152], BF16)
        u_src = Wuq.t[:].rearrange("p c (h d) -> p (c h) d", d=96)
        u_dst = Wuqr.t[:].rearrange("p c (h d) -> p (c h) d", d=96)
        cp("vector", u_dst[:, :, 0:64], u_src[:, :, 0:64], r=[Wuq.b], w=[Wuqr.b])
        cp("vector", u_dst[:, :, 64:80], u_src[:, :, 80:96], r=[Wuq.b], w=[Wuqr.b])
        cp("vector", u_dst[:, :, 80:96], u_src[:, :, 64:80], r=[Wuq.b], w=[Wuqr.b])
        Wukv = sb("Wukv", [128, 1536], BF16)
        ph.load_w(Wukv, Wukv[:], W["l0_mla_w_ukv"])
        Wukv4 = Wukv.t[:].rearrange("p (h d) -> p h d", d=128)
        Wvm = sb("Wvm", [128, 768], BF16)
        cp("gpsimd", Wvm.t[:].rearrange("p (h d) -> p h d", d=64), Wukv4[:, :, 64:128], r=[Wukv.b], w=[Wvm.b])

        x32 = sb("x32", [128, 4, 1024], F32)
        h = sb("h", [128, 4, 1024], BF16)
        hT = sb("hT", [128, 8, 512], BF16)
        junk = sb("junk", [128, 1024], F32)
        ss = sb("ss", [128, 4], F32)
        pos_i = sb("pos_i", [128, 512], I32)
        pos_f = sb("pos_f", [128, 512], F32)
        cosk = sb("cosk", [128, 512], F32)
        sink = sb("sink", [128, 512], F32)
        cosq = sb("cosq", [128, 512], F32)
        sinq = sb("sinq", [128, 512], F32)
        cosmk = sb("cosmk", [128, 512], F32)
        sinmk = sb("sinmk", [128, 512], F32)
        cosmq = sb("cosmq", [128, 512], F32)
        sinmq = sb("sinmq", [128, 512], F32)
        t1 = [sb("t1_%d" % i, [128, 512], F32) for i in range(2)]
        t2 = [sb("t2_%d" % i, [128, 512], F32) for i in range(2)]
        ang, r1, r2, sinr = t1[0], t1[1], t2[0], t2[1]
        qk_st = [sb("qkst%d" % i, [128, 512], BF16) for i in range(2)]
        Vst = sb("Vst", [128, 4, 1536], BF16)
        Vmst = Vst
        Vst5 = Vst.t[:].rearrange("p j (h d) -> p j h d", d=128)
        Vmst5 = Vmst.t[:].rearrange("p j (h d) -> p j h d", d=128)
        I("vector", "memset", Vst[:], 1.0, w=[Vst.b])
        cT32 = sb("cT32", [128, 3, 512], F32)
        sq32 = sb("sq32", [128, 3, 512], F32)
        rstd = sb("rstd", [128, 2, 512], F32)
        cqn = sb("cqn", [128, 2, 512], BF16)
        ckvn = sb("ckvn", [128, 512], BF16)
        kpeT = sb("kpeT", [128, 512], BF16)
        Qst = [sb("Qst%d" % i, [128, 512], BF16) for i in range(2)]
        Kst = [sb("Kst%d" % i, [128, 512], BF16) for i in range(2)]
        tps = [ps("tp%d" % i, [128, 1024], BF16) for i in range(2)]
        ph.pspool(6)
        SC96 = 96 ** -0.5
        SSC = 1.0 - 2e-4

        hTs = [hT, sb("hT1", [128, 8, 512], BF16)]

        def a_loads(tt):
            tsl = slice(tt * 512, (tt + 1) * 512)
            hTc = hTs[tt % 2]
            ph.load(x32, x32[:], x_in[tsl, :].rearrange("(j p) d -> p j d", p=128))
            ph.load(pos_i, pos_i[:], bass.AP(pos_t, tt * 512, [[0, 128], [1, 512]]))

        def a_norm(tt):
            tsl = slice(tt * 512, (tt + 1) * 512)
            hTc = hTs[tt % 2]
            rms_rows(x32, h, gb, junk, ss, 4)

        def a_trans(tt):
            tsl = slice(tt * 512, (tt + 1) * 512)
            hTc = hTs[tt % 2]
            transpose_rows(h, hTc, 8, 4, tps, ident)
            cp("vector", pos_f[:], pos_i[:], r=[pos_i.b], w=[pos_f.b])
            for (icol, tabs) in ((0, (sink, sinq, cosk, cosq, 1, 2, 0.125)), (3, (sinmk, sinmq, cosmk, cosmq, 4, 5, SC96))):
                sk_, sq_, ck_, cq_, c1, c2, qs = tabs
                I("vector", "tensor_scalar", ang[:], pos_f[:], cols[:, icol:icol + 1], None, ALU.mult,
                  r=[pos_f.b, cols.b], w=[ang.b])
                for (rr, off, dst) in ((r1, 0.0, sinr), (r2, 0.25, ck_)):
                    I("vector", "tensor_scalar", rr[:], ang[:], 1.0 / (2 * PI), off, ALU.mult, ALU.add,
                      r=[ang.b], w=[rr.b])
                    cp("vector", pos_i[:], rr[:], r=[rr.b], w=[pos_i.b])
                    cp("vector", rr[:], pos_i[:], r=[pos_i.b], w=[rr.b])
                    I("vector", "scalar_tensor_tensor", rr[:], rr[:], -2 * PI, ang[:], ALU.mult, ALU.add,
                      r=[rr.b, ang.b], w=[rr.b])
                    I("scalar", "activation", dst[:], rr[:], AF.Sin, scale=SSC, bias=off * 2 * PI * SSC,
                      r=[rr.b], w=[dst.b])
                I("vector", "tensor_scalar", sk_[:], sinr[:], cols[:, c1:c1 + 1], None, ALU.mult,
                  r=[sinr.b, cols.b], w=[sk_.b])
                I("vector", "tensor_scalar", sq_[:], sinr[:], cols[:, c2:c2 + 1], None, ALU.mult,
                  r=[sinr.b, cols.b], w=[sq_.b])
                I("vector", "tensor_scalar", cq_[:], ck_[:], qs, None, ALU.mult, r=[ck_.b], w=[cq_.b])

        def a_p1(tt):
            tsl = slice(tt * 512, (tt + 1) * 512)
            hTc = hTs[tt % 2]
            for i in range(12):
                pq, pqr = ph.nxt(), ph.nxt()
                proj_fm(pq, Wqk, i * 128, 128, hTc, 8)
                proj_fm(pqr, Wqkr, i * 128, 128, hTc, 8)
                cs_, sn_ = (cosq, sinq) if i < 6 else (cosk, sink)
                a1, a2, st = t1[i % 2], t2[i % 2], qk_st[i % 2]
                I("vector", "tensor_tensor", a1[:], pq[:, 0:512], cs_[:], ALU.mult, r=[pq.b, cs_.b], w=[a1.b])
                I("vector", "tensor_tensor", a2[:], pqr[:, 0:512], sn_[:], ALU.mult, r=[pqr.b, sn_.b], w=[a2.b])
                I("gpsimd", "tensor_tensor", st[:], a1[:], a2[:], ALU.add, r=[a1.b, a2.b], w=[st.b])
                dst = (qt_dsw if i < 6 else kt_dsw)[i % 6, :, tsl]
                ph.store(st, dst, st[:])

        def a_p2(tt):
            tsl = slice(tt * 512, (tt + 1) * 512)
            hTc = hTs[tt % 2]
            for j in range(4):
                for (c0, n) in ((0, 512), (512, 256)):
                    pv = ph.nxt()
                    for c in range(8):
                        mm(pv[:, 0:n], hTc[:, c, j * 128:(j + 1) * 128], Wv[:, c, c0:c0 + n], c == 0, c == 7,
                           r=[hTc.b, Wv.b], w=[pv.b])
                    cp("scalar", Vst5[:, j, c0 // 64:(c0 + n) // 64, 0:64],
                       pv.t[:, 0:n].rearrange("p (h d) -> p h d", d=64), r=[pv.b], w=[Vst.b])
            for j in range(4):
                ph.store(Vst, v_dsw[:, tt * 4 + j].rearrange("h p d -> p h d"), Vst5[:, j])
            for i in range(3):
                pc = ph.nxt()
                proj_fm(pc, Wc, i * 128, 128, hTc, 8)
                cp("vector", cT32[:, i, :], pc[:, 0:512], r=[pc.b], w=[cT32.b])
                I("scalar", "activation", sq32[:, i, :], pc[:, 0:512], AF.Square, r=[pc.b], w=[sq32.b])
            pssq, psskv = ph.nxt(), ph.nxt()
            mm(pssq[:, 0:512], ones32[:], sq32[:, 0, :], True, False, r=[ones32.b, sq32.b], w=[pssq.b])
            mm(pssq[:, 0:512], ones32[:], sq32[:, 1, :], False, True, r=[ones32.b, sq32.b], w=[pssq.b])
            mm(psskv[:, 0:512], ones32[:], sq32[:, 2, :], True, True, r=[ones32.b, sq32.b], w=[psskv.b])
            I("vector", "tensor_scalar", rstd[:, 0, :], pssq[:, 0:512], 1.0 / 256, 1e-6, ALU.mult, ALU.add,
              r=[pssq.b], w=[rstd.b])
            I("vector", "tensor_scalar", rstd[:, 1, :], psskv[:, 0:512], 1.0 / 128, 1e-6, ALU.mult, ALU.add,
              r=[psskv.b], w=[rstd.b])
            I("scalar", "activation", rstd[:], rstd[:], AF.Ln, r=[rstd.b], w=[rstd.b])
            I("scalar", "activation", rstd[:], rstd[:], AF.Exp, scale=-0.5, r=[rstd.b], w=[rstd.b])
            for c in range(2):
                I("vector", "scalar_tensor_tensor", cqn[:, c, :], cT32[:, c, :], gq[:, c, :], rstd[:, 0, :],
                  ALU.mult, ALU.mult, r=[cT32.b, gq.b, rstd.b], w=[cqn.b])
            I("vector", "scalar_tensor_tensor", ckvn[:], cT32[:, 2, :], gkv[:, 0, :], rstd[:, 1, :],
              ALU.mult, ALU.mult, r=[cT32.b, gkv.b, rstd.b], w=[ckvn.b])
            for hh in range(12):
                pm, pmr = ph.nxt(), ph.nxt()
                for c in range(2):
                    mm(pm[0:96, 0:512], Wuq[:, c, hh * 96:(hh + 1) * 96], cqn[:, c, :], c == 0, c == 1,
                       r=[Wuq.b, cqn.b], w=[pm.b])
                for c in range(2):
                    mm(pmr[0:96, 0:512], Wuqr[:, c, hh * 96:(hh + 1) * 96], cqn[:, c, :], c == 0, c == 1,
                       r=[Wuqr.b, cqn.b], w=[pmr.b])
                st, a1, a2 = Qst[hh % 2], t1[hh % 2], t2[hh % 2]
                I("scalar", "mul", st[0:64, :], pm[0:64, 0:512], SC96, r=[pm.b], w=[st.b])
                I("vector", "tensor_tensor", a1[64:96, :], pm[64:96, 0:512], cosmq[64:96, :], ALU.mult,
                  r=[pm.b, cosmq.b], w=[a1.b])
                I("vector", "tensor_tensor", a2[64:96, :], pmr[64:96, 0:512], sinmq[64:96, :], ALU.mult,
                  r=[pmr.b, sinmq.b], w=[a2.b])
                I("vector", "tensor_tensor", st[64:96, :], a1[64:96, :], a2[64:96, :], ALU.add,
                  r=[a1.b, a2.b], w=[st.b])
                ph.store(st, qt_mla[hh, :, tsl], st[0:96, :])
            pk, pkr = ph.nxt(), ph.nxt()
            proj_fm(pk, Wkpe, 0, 96, hTc, 8)
            proj_fm(pkr, Wkper, 0, 96, hTc, 8)
            a1, a2 = t1[0], t2[0]
            I("vector", "tensor_tensor", a1[64:96, :], pk[64:96, 0:512], cosmk[64:96, :], ALU.mult,
              r=[pk.b, cosmk.b], w=[a1.b])
            I("vector", "tensor_tensor", a2[64:96, :], pkr[64:96, 0:512], sinmk[64:96, :], ALU.mult,
              r=[pkr.b, sinmk.b], w=[a2.b])
            I("vector", "tensor_tensor", kpeT[64:96, :], a1[64:96, :], a2[64:96, :], ALU.add,
              r=[a1.b, a2.b], w=[kpeT.b])
            for hh in range(12):
                pkn = ph.nxt()
                mm(pkn[0:64, 0:512], Wukv4[:, hh, 0:64], ckvn[:], True, True, r=[Wukv.b, ckvn.b], w=[pkn.b])
                st = Kst[hh % 2]
                cp("scalar", st[0:64, :], pkn[0:64, 0:512], r=[pkn.b], w=[st.b])
                cp("gpsimd" if hh % 3 == 0 else ("scalar" if hh % 3 == 1 else "vector"), st[64:96, :], kpeT[64:96, :],
                   r=[kpeT.b], w=[st.b])
                ph.store(st, kt_mla[hh, :, tsl], st[0:96, :])
            for j in range(4):
                for (c0, n) in ((0, 512), (512, 256)):
                    pv = ph.nxt()
                    mm(pv[:, 0:n], ckvn[:, j * 128:(j + 1) * 128], Wvm[:, c0:c0 + n], True, True,
                       r=[ckvn.b, Wvm.b], w=[pv.b])
                    cp("scalar", Vmst5[:, j, c0 // 64:(c0 + n) // 64, 0:64],
                       pv.t[:, 0:n].rearrange("p (h d) -> p h d", d=64), r=[pv.b], w=[Vmst.b])
            for j in range(4):
                ph.store(Vmst, v_mla[:, tt * 4 + j].rearrange("h p d -> p h d"), Vmst5[:, j])

        a_loads(0)
        a_norm(0)
        a_trans(0)
        for tt in range(NT):
            nxt_ = tt + 1 < NT
            if nxt_:
                a_loads(tt + 1)
            a_p1(tt)
            if nxt_:
                a_norm(tt + 1)
            a_p2(tt)
            if nxt_:
                a_trans(tt + 1)
        ph.close()

    def attn_softmax(pname, units, ydstT, prefetch=None):
        ph = Phase(pname)
        sb, ps = ph.sb, ph.ps
        ident = sb("ident", [128, 128], BF16)
        ph.load_w(ident, ident[:], C["c_ident"])
        masks = {}
        for u in units:
            for shd in u["subs"]:
                mn = shd["mask"]
                if mn not in masks:
                    wdt = C[mn].shape[1]
                    masks[mn] = sb(mn, [128, wdt], BF16)
                    ph.load_w(masks[mn], masks[mn][:], C[mn])
        nsub = max(len(u["subs"]) for u in units)
        KT = [[sb("KT%d_%d" % (i, s_), [128, S], BF16) for s_ in range(nsub)] for i in range(2)]
        QT = [[sb("QT%d_%d" % (i, s_), [128, S], BF16) for s_ in range(nsub)] for i in range(2)]
        VH = [[sb("VH%d_%d" % (i, s_), [128, 32, 128], BF16) for s_ in range(nsub)] for i in range(2)]
        dk0 = units[0]["subs"][0]["dk"]
        for i in range(2):
            for s_ in range(nsub):
                I("gpsimd", "memset", KT[i][s_][64:128, :], 0.0, w=[KT[i][s_].b])
                I("vector", "memset", QT[i][s_][64:128, :], 0.0, w=[QT[i][s_].b])
        pts = [sb("pt%d" % i, [128, 512], BF16) for i in range(5)]
        sps = [ps("sp%d" % i, [128, 512], F32) for i in range(5)]
        accs = [ps("acc%d" % i, [128, 512], F32) for i in range(2)]
        rden = [sb("rden%d" % i, [64, 512], F32) for i in range(2)]
        yst = [sb("yst%d" % i, [64, 512], BF16) for i in range(2)]
        it = 0
        gi = 0
        pend = []

        def flush(keep):
            while len(pend) > keep:
                pend.pop(0)()

        def load_unit(ui_):
            u_ = units[ui_]
            b_ = ui_ % 2
            for si, shd in enumerate(u_["subs"]):
                dk = shd["dk"]
                ph.load(KT[b_][si], KT[b_][si][0:dk, :], shd["kt"])
                ph.load(QT[b_][si], QT[b_][si][0:dk, :], shd["qt"])
                ph.load(VH[b_][si], VH[b_][si][:], shd["v"].rearrange("j p d -> p j d"))

        load_unit(0)
        if prefetch is not None:
            prefetch(ph)
        for ui, u in enumerate(units):
            bi = ui % 2
            for G in range(8):
                if G == 1 and ui + 1 < len(units):
                    load_unit(ui + 1)
                contrib = []
                for si, shd in enumerate(u["subs"]):
                    wb = shd["wb"]
                    for j in range(0, 4 * G + 4):
                        qlo = max(4 * G, j)
                        qhi = 4 * G + 3 if wb is None else min(4 * G + 3, j + wb)
                        if qhi >= qlo:
                            contrib.append((si, j, qlo, qhi))
                acc = accs[gi % 2]
                rd, ys = rden[gi % 2], yst[gi % 2]
                gi += 1
                col = u["col"]
                for idx, (si, j, qlo, qhi) in enumerate(contrib):
                    shd = u["subs"][si]
                    dk = shd["dk"]
                    sp, pt = sps[it % 5], pts[it % 5]
                    it += 1
                    c0 = (qlo - 4 * G) * 128
                    n = (qhi - qlo + 1) * 128
                    need_mask = shd["always"] or (qlo == j)
                    kt_, qt_, vh_ = KT[bi][si], QT[bi][si], VH[bi][si]
                    mm(sp[:, c0:c0 + n], kt_[:, j * 128:(j + 1) * 128], qt_[:, qlo * 128:(qhi + 1) * 128],
                       True, not need_mask, r=[kt_.b, qt_.b], w=[sp.b])
                    if need_mask:
                        mt = masks[shd["mask"]]
                        u0 = (qlo - j) * 128
                        mm(sp[:, c0:c0 + n], ident[:], mt[:, u0:u0 + n], False, True, r=[ident.b, mt.b], w=[sp.b])
                    I("scalar", "activation", pt[:, c0:c0 + n], sp[:, c0:c0 + n], AF.Exp, r=[sp.b], w=[pt.b])
                    last = idx == len(contrib) - 1

                    def st3(acc=acc, c0=c0, n=n, vh_=vh_, j=j, pt=pt, idx=idx, last=last, rd=rd, ys=ys, col=col, G=G):
                        I("tensor", "matmul", acc[:, c0:c0 + n], vh_[:, j, :], pt[:, c0:c0 + n], start=(idx == 0),
                          stop=last, skip_group_check=True, r=[pt.b, vh_.b], w=[acc.b], inc=last)
                        if last:
                            I("vector", "reciprocal", rd[:], acc[64:128, 0:512], r=[acc.b], w=[rd.b])
                            I("vector", "tensor_tensor", ys[:], acc[0:64, 0:512], rd[:], ALU.mult,
                              r=[acc.b, rd.b], w=[ys.b])
                            ph.store(ys, ydstT[col:col + 64, G * 512:(G + 1) * 512], ys[:])
                    pend.append(st3)
                    flush(2)
        flush(0)
        ph.close()

    def vsrc(vt, hh):
        return vt.rearrange("j p (h d) -> p j h d", d=65)[:, :, hh, :]

    def phase_B_dsw():
        units = []
        wbs = (1, 4, 16)
        for jh in range(4):
            subs = []
            for g in range(3):
                hh = g * 4 + jh
                subs.append(dict(kt=kt_dsw[hh // 2, (hh % 2) * 64:(hh % 2) * 64 + 64, :],
                                 qt=qt_dsw[hh // 2, (hh % 2) * 64:(hh % 2) * 64 + 64, :],
                                 v=v_dsw[hh], dk=64, mask="c_mb%d" % g, wb=wbs[g], always=True))
            units.append(dict(subs=subs, col=jh * 64))
        attn_softmax("Bd", units, y0)

    def phase_B_mla(prefetch=None):
        units = []
        for hh in range(12):
            subs = [dict(kt=kt_mla[hh], qt=qt_mla[hh], v=v_mla[hh], dk=96, mask="c_mbc", wb=None, always=False)]
            units.append(dict(subs=subs, col=256 + hh * 64))
        attn_softmax("Bm", units, y0, prefetch=prefetch)


    PRE = {}

    def pre_alloc(stack, name, shape, dt):
        t = TB(stack.enter_context(nc.sbuf_tensor("pre_" + name, shape, dt)), "pre_" + name)
        PRE[name] = t
        return t

    def pre_load_C(ph, L, which):
        pre = "l%d_" % L
        if "Wout" in which:
            t = PRE["C%d_Wout" % L]
            ph.load_w(t, t[:], W[pre + "w_out"].rearrange("(c p) n -> p c n", p=128))
        wgv = W[pre + "ffn_w_gate"].rearrange("(c p) n -> p c n", p=128)
        wuv = W[pre + "ffn_w_up"].rearrange("(c p) n -> p c n", p=128)
        for c in range(8):
            ph.load_w(PRE["C%d_Wg" % L], PRE["C%d_Wg" % L][:, c, :], wgv[:, c, :], group=True)
            ph.load_w(PRE["C%d_Wu" % L], PRE["C%d_Wu" % L][:, c, :], wuv[:, c, :], group=True)

    def pre_load_D(ph):
        wv = W["l1_w_in"].rearrange("(c p) n -> p c n", p=128)
        t = PRE["D_Win"]
        for c in range(8):
            ph.load_w(t, t[:, c, :], wv[:, c, :], group=True)

    def phase_C(L):
        ph = Phase("C%d" % L)
        sb, ps = ph.sb, ph.ps
        pre = "l%d_" % L
        ysrc, nyc = (y0, 8) if L == 0 else (y1, 12)
        xsrc = x_in if L == 0 else x2
        xdst = x1 if L == 0 else x3
        actd = act0 if L == 0 else act1
        ident = sb("ident", [128, 128], BF16)
        ph.load_w(ident, ident[:], C["c_ident"])
        gb = sb("gb", [128, 1024], F32)
        ph.load(gb, gb[:], bcast_row(W[pre + "norm_ffn"], 1024))
        cw = sb("cw", [128, NFC, 4], F32)
        for jj in range(3):
            ph.load(cw, cw[:, :, jj:jj + 1], colvec(W[pre + "ffn_conv_w"][jj], NFC), group=True, slow=True)
        ph.load(cw, cw[:, :, 3:4], colvec(W[pre + "ffn_conv_b"], NFC), group=True, slow=True)
        if ("C%d_Wout" % L) in PRE:
            Wout = PRE["C%d_Wout" % L]
        else:
            Wout = sb("Wout", [128, nyc, 1024], BF16)
            ph.load_w(Wout, Wout[:], W[pre + "w_out"].rearrange("(c p) n -> p c n", p=128))
        Wg, Wu = PRE["C%d_Wg" % L], PRE["C%d_Wu" % L]
        yT = sb("yT", [128, nyc, 512], BF16)
        x32 = sb("x32", [128, 4, 1024], F32)
        h = sb("h", [128, 4, 1024], BF16)
        hT = sb("hT", [128, 8, 512], BF16)
        junk = sb("junk", [128, 1024], F32)
        ss = sb("ss", [128, 4], F32)
        halo = sb("halo", [128, NFC, 2], F32)
        I("vector", "memset", halo[:], 0.0, w=[halo.b])
        gsb = [sb("gsb%d" % i, [128, 514], F32) for i in range(2)]
        acc = [sb("acc%d" % i, [128, 512], F32) for i in range(2)]
        sil = [sb("sil%d" % i, [128, 512], F32) for i in range(2)]
        ast = [sb("ast%d" % i, [128, 512], BF16) for i in range(2)]
        tps = [ps("tp%d" % i, [128, 1024], BF16) for i in range(2)]
        ph.pspool(6)
        hTs = [hT, sb("hT1", [128, 8, 512], BF16)]

        def tslice(tt):
            return slice(tt * 512, (tt + 1) * 512)

        def loads(tt):
            tsl = tslice(tt)
            ph.load(yT, yT[:], ysrc[:, tsl].rearrange("(c p) t -> p c t", p=128))
            ph.load(x32, x32[:], xsrc[tsl, :].rearrange("(j p) d -> p j d", p=128))

        def front(tt):
            tsl = tslice(tt)
            for j in range(4):
                for hf in range(2):
                    po = ph.nxt()
                    for c in range(nyc):
                        mm(po[:, 0:512], yT[:, c, j * 128:(j + 1) * 128], Wout[:, c, hf * 512:(hf + 1) * 512],
                           c == 0, c == nyc - 1, r=[yT.b, Wout.b], w=[po.b])
                    I("vector", "tensor_tensor", x32[:, j, hf * 512:(hf + 1) * 512], po[:, 0:512],
                      x32[:, j, hf * 512:(hf + 1) * 512], ALU.add, r=[po.b, x32.b], w=[x32.b])
            ph.store(x32, xdst[tsl, :].rearrange("(j p) d -> p j d", p=128), x32[:])
            rms_rows(x32, h, gb, junk, ss, 4)

        def gateup(tt, fr):
            tsl = tslice(tt)
            hT_ = hTs[tt % 2]
            for f in fr:
                pg, pu = ph.nxt(), ph.nxt()
                proj_fm(pg, Wg, f * 128, 128, hT_, 8)
                proj_fm(pu, Wu, f * 128, 128, hT_, 8)
                g, a, sl_, st = gsb[f % 2], acc[f % 2], sil[f % 2], ast[f % 2]
                cp("scalar", g[:, 2:514], pg[:, 0:512], r=[pg.b], w=[g.b])
                cp("gpsimd", g[:, 0:2], halo[:, f, :], r=[halo.b], w=[g.b])
                cp("gpsimd", halo[:, f, :], g[:, 512:514], r=[g.b], w=[halo.b])
                I("vector", "tensor_scalar", a[:], g[:, 2:514], cw[:, f, 2:3], cw[:, f, 3:4], ALU.mult, ALU.add,
                  r=[g.b, cw.b], w=[a.b])
                I("vector", "scalar_tensor_tensor", a[:], g[:, 1:513], cw[:, f, 1:2], a[:], ALU.mult, ALU.add,
                  r=[g.b, cw.b, a.b], w=[a.b])
                I("vector", "scalar_tensor_tensor", a[:], g[:, 0:512], cw[:, f, 0:1], a[:], ALU.mult, ALU.add,
                  r=[g.b, cw.b, a.b], w=[a.b])
                I("scalar", "activation", sl_[:], a[:], AF.Silu, r=[a.b], w=[sl_.b])
                I("vector", "tensor_tensor", st[:], sl_[:], pu[:, 0:512], ALU.mult, r=[sl_.b, pu.b], w=[st.b])
                ph.store(st, actd[f, :, tsl], st[:])

        loads(0)
        front(0)
        transpose_rows(h, hTs[0], 8, 4, tps, ident)
        for tt in range(NT):
            nxt_ = tt + 1 < NT
            if nxt_:
                loads(tt + 1)
            gateup(tt, range(0, 11))
            if nxt_:
                front(tt + 1)
            gateup(tt, range(11, NFC))
            if nxt_:
                transpose_rows(h, hTs[(tt + 1) % 2], 8, 4, tps, ident)
        ph.close()

    def phase_Cd(L, prefetch=None):
        ph = Phase("Cd%d" % L)
        sb, ps = ph.sb, ph.ps
        pre = "l%d_" % L
        xsrc = x1 if L == 0 else x3
        actd = act0 if L == 0 else act1
        Wd = sb("Wd", [128, NFC, 1024], BF16)
        wdv = W[pre + "ffn_w_down"].rearrange("(c p) n -> p c n", p=128)
        for c in range(0, NFC, 2):
            ph.load_w(Wd, Wd[:, c:c + 2, :], wdv[:, c:c + 2, :], group=True)
        aTs = [sb("aT%d" % i, [128, NFC, 512], BF16) for i in range(2)]
        x32s = [sb("x32_%d" % i, [128, 4, 1024], F32) for i in range(2)]
        ph.pspool(6)
        if L == 1:
            gb = sb("gb", [128, 1024], F32)
            ph.load(gb, gb[:], bcast_row(W["final_norm"], 1024))
            junk = sb("junk", [128, 1024], F32)
            ss = sb("ss", [128, 4], F32)
            o32 = sb("o32", [128, 4, 1024], F32)

        def loads(tt):
            tsl = slice(tt * 512, (tt + 1) * 512)
            ph.load(aTs[tt % 2], aTs[tt % 2][:], actd[:, :, tsl].rearrange("f p t -> p f t"))
            ph.load(x32s[tt % 2], x32s[tt % 2][:], xsrc[tsl, :].rearrange("(j p) d -> p j d", p=128))

        loads(0)
        if prefetch is not None:
            prefetch(ph)
        for tt in range(NT):
            tsl = slice(tt * 512, (tt + 1) * 512)
            if tt + 1 < NT:
                loads(tt + 1)
            aT, x32 = aTs[tt % 2], x32s[tt % 2]
            for j in range(4):
                for hf in range(2):
                    po = ph.nxt()
                    for f in range(NFC):
                        mm(po[:, 0:512], aT[:, f, j * 128:(j + 1) * 128], Wd[:, f, hf * 512:(hf + 1) * 512],
                           f == 0, f == NFC - 1, r=[aT.b, Wd.b], w=[po.b])
                    I("vector", "tensor_tensor", x32[:, j, hf * 512:(hf + 1) * 512], po[:, 0:512],
                      x32[:, j, hf * 512:(hf + 1) * 512], ALU.add, r=[po.b, x32.b], w=[x32.b])
            if L == 0:
                ph.store(x32, x2[tsl, :].rearrange("(j p) d -> p j d", p=128), x32[:])
            else:
                rms_rows(x32, o32, gb, junk, ss, 4)
                ph.store(o32, out_t[tsl, :].rearrange("(j p) d -> p j d", p=128), o32[:])
        ph.close()

    def phase_D():
        ph = Phase("D")
        sb, ps = ph.sb, ph.ps
        ident = sb("ident", [128, 128], BF16)
        ph.load_w(ident, ident[:], C["c_ident"])
        tri = sb("tri", [128, 128], F32)
        ph.load(tri, tri[:], C["c_tri"])
        ones32 = sb("ones32", [128, 128], F32)
        I("vector", "memset", ones32[:], 1.0, w=[ones32.b])
        gb = sb("gb", [128, 1024], F32)
        ph.load(gb, gb[:], bcast_row(W["l1_norm_mix"], 1024))
        gn = sb("gn", [128, 1024], F32)
        ph.load(gn, gn[:], bcast_row(W["l1_ssm_norm"], 1024))
        cw = sb("cw", [128, 12, 5], F32)
        for jj in range(4):
            ph.load(cw, cw[:, :, jj:jj + 1], colvec(W["l1_ssm_conv_w"][jj], 12), group=True, slow=True)
        ph.load(cw, cw[:, :, 4:5], colvec(W["l1_ssm_conv_b"], 12), group=True, slow=True)
        dtb = sb("dtb", [128, 16], F32)
        ph.load(dtb, dtb[:], bcast_row(W["l1_ssm_dt_bias"], 16))
        Aneg = sb("Aneg", [128, 16], F32)
        ph.load(Aneg, Aneg[:], bcast_row(W["l1_ssm_a_log"], 16))
        I("scalar", "activation", Aneg[:], Aneg[:], AF.Exp, r=[Aneg.b], w=[Aneg.b])
        I("vector", "tensor_scalar", Aneg[:], Aneg[:], -1.0, None, ALU.mult, r=[Aneg.b], w=[Aneg.b])
        Dsk = sb("Dsk", [128, 16], F32)
        ph.load(Dsk, Dsk[:], bcast_row(W["l1_ssm_d"], 16))
        Win = PRE["D_Win"]
        x32 = sb("x32", [128, 4, 1024], F32)
        h = sb("h", [128, 4, 1024], BF16)
        hT = sb("hT", [128, 8, 512], BF16)
        ss = sb("ss", [128, 4], F32)
        halo = sb("halo", [128, 12, 3], F32)
        I("vector", "memset", halo[:], 0.0, w=[halo.b])
        raw = [sb("raw%d" % i, [128, 515], F32) for i in range(2)]
        cac = [sb("cac%d" % i, [128, 512], F32) for i in range(2)]
        xbcT = sb("xbcT", [128, 12, 512], BF16)
        qkst = [sb("qkst%d" % i, [128, 512], BF16) for i in range(2)]
        zs = sb("zs", [128, 4, 1024], F32)
        dta = sb("dta", [128, 4, 16], F32)
        xs_tm2 = [sb("xs_tm%d" % i, [128, 1024], BF16) for i in range(2)]
        B_tm2 = [sb("B_tm%d" % i, [128, 256], BF16) for i in range(2)]
        da = sb("da", [128, 16], F32)
        cscol = sb("cscol", [128, 16], F32)
        Rt = sb("Rt", [128, 16, 128], F32)
        dif = sb("dif", [128, 16, 128], F32)
        cl = sb("cl", [128, 16], F32)
        dend2 = [sb("dend%d" % i, [128, 16], F32) for i in range(2)]
        ecl2 = [sb("ecl%d" % i, [128, 16], F32) for i in range(2)]
        ecs2 = [sb("ecs%d" % i, [128, 16], F32) for i in range(2)]
        Gm = sb("Gm", [128, 2, 128], F32)
        MT = sb("MT", [128, 16, 128], BF16)
        xdt = sb("xdt", [128, 16, 64], BF16)
        xdd = sb("xdd", [128, 16, 64], BF16)

        class VW:
            def __init__(self, ap, b):
                self.ap = ap
                self.b = b

            def __getitem__(self, k):
                return self.ap[k]

        MT2 = [MT, VW(h.t[:, 0:2, :].rearrange("p a (h l) -> p (a h) l", l=128), h.b)]
        xdt2 = [xdt, VW(h.t[:, 2, :].rearrange("p (h d) -> p h d", d=64), h.b)]
        xdd2 = [xdd, VW(h.t[:, 3, :].rearrange("p (h d) -> p h d", d=64), h.b)]
        ytm = sb("ytm", [128, 1024], F32)
        tmp2 = sb("tmp2", [128, 1024], F32)
        junk = tmp2
        prev32 = sb("prev32", [128, 1024], F32)
        prevb = sb("prevb", [128, 1024], BF16)
        I("vector", "memset", prev32[:], 0.0, w=[prev32.b])
        I("vector", "memset", prevb[:], 0.0, w=[prevb.b])
        ss2 = sb("ss2", [128, 2], F32)
        ycst = sb("ycst", [128, 4, 1024], BF16)
        ycT = hT
        tps = [ps("tp%d" % i, [128, 1024], BF16) for i in range(2)]
        ph.pspool(6)

        def v3(ap, d):
            return ap.rearrange("p (h d) -> p h d", d=d)

        xpool = ph.pp[0:2]
        ypy = ph.pp[2:4]
        yq = ph.pp[4:6]

        for tt in range(NT):
            tsl = slice(tt * 512, (tt + 1) * 512)
            ph.load(x32, x32[:], x2[tsl, :].rearrange("(j p) d -> p j d", p=128))
            rms_rows(x32, h, gb, junk, ss, 4)
            transpose_rows(h, hT, 8, 4, tps, ident)
            for i in range(12):
                pc = ph.nxt()
                proj_fm(pc, Win, 1024 + i * 128, 128, hT, 8)
                rw, a = raw[i % 2], cac[i % 2]
                cp("scalar", rw[:, 3:515], pc[:, 0:512], r=[pc.b], w=[rw.b])
                cp("gpsimd", rw[:, 0:3], halo[:, i, :], r=[halo.b], w=[rw.b])
                cp("gpsimd", halo[:, i, :], rw[:, 512:515], r=[rw.b], w=[halo.b])
                I("scalar", "activation", a[:], pc[:, 0:512], AF.Identity, scale=cw[:, i, 3:4], bias=cw[:, i, 4:5],
                  r=[pc.b, cw.b], w=[a.b])
                for k_ in (2, 1, 0):
                    I("vector", "scalar_tensor_tensor", a[:], rw[:, k_:k_ + 512], cw[:, i, k_:k_ + 1], a[:],
                      ALU.mult, ALU.add, r=[rw.b, cw.b, a.b], w=[a.b])
                I("scalar", "activation", xbcT[:, i, :], a[:], AF.Silu, r=[a.b], w=[xbcT.b])
            for i in range(8):
                pq = ph.nxt()
                proj_fm(pq, Win, 2576 + i * 128, 128, hT, 8)
                st = qkst[i % 2]
                I("scalar", "mul", st[:], pq[:, 0:512], 0.125 if i < 4 else 1.0, r=[pq.b], w=[st.b])
                ph.store(st, (qt_sb if i < 4 else kt_sb)[i % 4, :, tsl], st[:])
            for j in range(4):
                pv = ph.nxt()
                for c in range(8):
                    mm(pv[:, 0:512], hT[:, c, j * 128:(j + 1) * 128], Win[:, c, 3600:4112], c == 0, c == 7,
                       r=[hT.b, Win.b], w=[pv.b])
                cp("scalar", ycst[:, j, 0:512], pv[:, 0:512], r=[pv.b], w=[ycst.b])
            ph.store(ycst, v_sb[tt * 4:(tt + 1) * 4].rearrange("j p f -> p j f"), ycst[:, :, 0:512])
            for j in range(4):
                for hf in range(2):
                    pz = ph.nxt()
                    for c in range(8):
                        mm(pz[:, 0:512], hT[:, c, j * 128:(j + 1) * 128], Win[:, c, hf * 512:(hf + 1) * 512],
                           c == 0, c == 7, r=[hT.b, Win.b], w=[pz.b])
                    I("scalar", "activation", zs[:, j, hf * 512:(hf + 1) * 512], pz[:, 0:512], AF.Silu,
                      r=[pz.b], w=[zs.b])
                pd = ph.nxt()
                for c in range(8):
                    mm(pd[:, 0:16], hT[:, c, j * 128:(j + 1) * 128], Win[:, c, 2560:2576], c == 0, c == 7,
                       r=[hT.b, Win.b], w=[pd.b])
                I("vector", "tensor_tensor", dta[:, j, :], pd[:, 0:16], dtb[:], ALU.add, r=[pd.b, dtb.b], w=[dta.b])
            I("scalar", "activation", dta[:], dta[:], AF.Exp, r=[dta.b], w=[dta.b])
            I("scalar", "activation", dta[:], dta[:], AF.Ln, bias=1.0, r=[dta.b], w=[dta.b])
            def Xgen(j):
                par = j % 2
                csl = slice(j * 128, (j + 1) * 128)
                xs_, B_, MT_, xdt_, xdd_ = xs_tm2[par], B_tm2[par], MT2[par], xdt2[par], xdd2[par]
                dend_, ecl_, ecs_ = dend2[par], ecl2[par], ecs2[par]
                tp = tps[0]
                for c in range(8):
                    I("tensor", "transpose", tp[:, c * 128:(c + 1) * 128], xbcT[:, c, csl], ident[:],
                      r=[xbcT.b, ident.b], w=[tp.b], inc=(c == 7))
                cp("vector", xs_[:], tp[:, 0:1024], r=[tp.b], w=[xs_.b])
                yield
                tp = tps[1]
                for c in range(2):
                    I("tensor", "transpose", tp[:, c * 128:(c + 1) * 128], xbcT[:, 8 + c, csl], ident[:],
                      r=[xbcT.b, ident.b], w=[tp.b], inc=(c == 1))
                cp("scalar", B_[:], tp[:, 0:256], r=[tp.b], w=[B_.b])
                I("vector", "tensor_tensor", da[:], dta[:, j, :], Aneg[:], ALU.mult, r=[dta.b, Aneg.b], w=[da.b])
                yield
                pcs = xpool[0]
                mm(pcs[:, 0:16], tri[:], da[:], True, True, r=[tri.b, da.b], w=[pcs.b])
                cp("vector", cscol[:], pcs[:, 0:16], r=[pcs.b], w=[cscol.b])
                I("vector", "tensor_tensor", Rt[:], tri[:].unsqueeze(1).to_broadcast([128, 16, 128]),
                  da[:].unsqueeze(2).to_broadcast([128, 16, 128]), ALU.mult, r=[tri.b, da.b], w=[Rt.b])
                yield
                for q4 in range(4):
                    pr = xpool[(q4 + 1) % 2]
                    mm(pr[:, 0:512], ones32[:], Rt.t[:, q4 * 4:(q4 + 1) * 4, :].rearrange("p h l -> p (h l)"),
                       True, True, r=[ones32.b, Rt.b], w=[pr.b])
                    pr3 = pr.t[:, 0:512].rearrange("p (h l) -> p h l", l=128)
                    I("vector", "tensor_tensor", dif[:, q4 * 4:(q4 + 1) * 4, :], pr3,
                      cscol[:, q4 * 4:(q4 + 1) * 4].unsqueeze(2).to_broadcast([128, 4, 128]), ALU.subtract,
                      r=[pr.b, cscol.b], w=[dif.b])
                    cp("vector", cl[:, q4 * 4:(q4 + 1) * 4], pr3[:, :, 127], r=[pr.b], w=[cl.b])
                    yield
                I("vector", "tensor_tensor", dif[:, 0:8, :], dif[:, 0:8, :],
                  tri[:].unsqueeze(1).to_broadcast([128, 8, 128]), ALU.mult, r=[dif.b, tri.b], w=[dif.b])
                I("gpsimd", "tensor_tensor", dif[:, 8:16, :], dif[:, 8:16, :],
                  tri[:].unsqueeze(1).to_broadcast([128, 8, 128]), ALU.mult, r=[dif.b, tri.b], w=[dif.b])
                yield
                I("scalar", "activation", dif[:], dif[:], AF.Exp, r=[dif.b], w=[dif.b])
                I("vector", "tensor_tensor", dend_[:], cl[:], cscol[:], ALU.subtract, r=[cl.b, cscol.b], w=[dend_.b])
                yield
                I("scalar", "activation", dend_[:], dend_[:], AF.Exp, r=[dend_.b], w=[dend_.b])
                I("scalar", "activation", ecl_[:], cl[:], AF.Exp, r=[cl.b], w=[ecl_.b])
                I("scalar", "activation", ecs_[:], cscol[:], AF.Exp, r=[cscol.b], w=[ecs_.b])
                yield
                pG = xpool[0]
                for g in range(2):
                    mm(pG[:, g * 128:(g + 1) * 128], xbcT[:, 8 + g, csl], xbcT[:, 10 + g, csl], True, True,
                       r=[xbcT.b], w=[pG.b])
                I("vector", "tensor_tensor", Gm[:], pG.t[:, 0:256].rearrange("p (g l) -> p g l", l=128),
                  tri[:].unsqueeze(1).to_broadcast([128, 2, 128]), ALU.mult, r=[pG.b, tri.b], w=[Gm.b])
                yield
                for g in range(2):
                    I("gpsimd" if g == 0 else "vector", "tensor_tensor", MT_[:, g * 8:(g + 1) * 8, :],
                      dif[:, g * 8:(g + 1) * 8, :], Gm[:, g, :].unsqueeze(1).to_broadcast([128, 8, 128]), ALU.mult,
                      r=[dif.b, Gm.b], w=[MT_.b])
                    yield
                I("vector", "tensor_tensor", xdt_[:], v3(xs_[:], 64),
                  dta[:, j, :].unsqueeze(2).to_broadcast([128, 16, 64]), ALU.mult, r=[xs_.b, dta.b], w=[xdt_.b])
                yield
                I("gpsimd", "tensor_tensor", xdd_[:], xdt_[:], dend_[:].unsqueeze(2).to_broadcast([128, 16, 64]),
                  ALU.mult, r=[xdt_.b, dend_.b], w=[xdd_.b])
                yield

            def Ygen(j):
                par = j % 2
                csl = slice(j * 128, (j + 1) * 128)
                xs_, B_, MT_, xdt_, xdd_ = xs_tm2[par], B_tm2[par], MT2[par], xdt2[par], xdd2[par]
                dend_, ecl_, ecs_ = dend2[par], ecl2[par], ecs2[par]
                py = ypy
                for hh in range(16):
                    mm(py[hh // 8][:, (hh % 8) * 64:(hh % 8) * 64 + 64], MT_[:, hh, :], xdt_[:, hh, :], True, True,
                       r=[MT_.b, xdt_.b], w=[py[hh // 8].b], inc=(hh % 8 == 7))
                    if hh % 8 == 7:
                        yield
                for g in range(2):
                    pyo = yq[g]
                    mm(pyo[:, 0:512], xbcT[:, 10 + g, csl], prevb[:, g * 512:(g + 1) * 512], True, True,
                       r=[xbcT.b, prevb.b], w=[pyo.b])
                    gs = slice(g * 512, (g + 1) * 512)
                    I("vector", "tensor_tensor", v3(ytm.t[:, gs], 64), v3(pyo.t[:, 0:512], 64),
                      ecs_[:, g * 8:(g + 1) * 8].unsqueeze(2).to_broadcast([128, 8, 64]), ALU.mult,
                      r=[pyo.b, ecs_.b], w=[ytm.b])
                    yield
                    I("vector", "tensor_tensor", ytm[:, gs], ytm[:, gs], py[g][:, 0:512], ALU.add,
                      r=[ytm.b, py[g].b], w=[ytm.b])
                    yield
                I("gpsimd", "tensor_tensor", v3(tmp2.t[:], 64), v3(xs_[:], 64),
                  Dsk[:].unsqueeze(2).to_broadcast([128, 16, 64]), ALU.mult, r=[xs_.b, Dsk.b], w=[tmp2.b])
                yield
                I("vector", "tensor_tensor", ytm[:], ytm[:], tmp2[:], ALU.add, r=[ytm.b, tmp2.b], w=[ytm.b])
                yield
                for g in range(2):
                    pst = yq[g]
                    gs = slice(g * 512, (g + 1) * 512)
                    mm(pst[:, 0:512], B_[:, g * 128:(g + 1) * 128],
                       xdd_[:, g * 8:(g + 1) * 8, :].rearrange("p h d -> p (h d)"), True, True,
                       r=[B_.b, xdd_.b], w=[pst.b])
                    I("gpsimd", "tensor_tensor", v3(prev32.t[:, gs], 64), v3(prev32.t[:, gs], 64),
                      ecl_[:, g * 8:(g + 1) * 8].unsqueeze(2).to_broadcast([128, 8, 64]), ALU.mult,
                      r=[prev32.b, ecl_.b], w=[prev32.b])
                    yield
                    I("vector", "tensor_tensor", prev32[:, gs], prev32[:, gs], pst[:, 0:512], ALU.add,
                      r=[prev32.b, pst.b], w=[prev32.b])
                    yield
                cp("scalar", prevb[:], prev32[:], r=[prev32.b], w=[prevb.b])
                I("vector", "tensor_tensor", ytm[:], ytm[:], zs[:, j, :], ALU.mult, r=[ytm.b, zs.b], w=[ytm.b])
                yield
                I("vector", "memset", ss2[:], 0.0, w=[ss2.b])
                for g in range(2):
                    I("scalar", "activation", junk[:, 0:512], ytm[:, g * 512:(g + 1) * 512], AF.Square,
                      accum_out=ss2[:, g:g + 1], r=[ytm.b, ss2.b], w=[junk.b, ss2.b])
                yield
                I("vector", "tensor_scalar", ss2[:], ss2[:], 1.0 / 512, 1e-6, ALU.mult, ALU.add, r=[ss2.b], w=[ss2.b])
                I("scalar", "activation", ss2[:], ss2[:], AF.Ln, r=[ss2.b], w=[ss2.b])
                I("scalar", "activation", ss2[:], ss2[:], AF.Exp, scale=-0.5, r=[ss2.b], w=[ss2.b])
                yield
                for g in range(2):
                    gs = slice(g * 512, (g + 1) * 512)
                    I("vector", "scalar_tensor_tensor", ycst[:, j, gs], ytm[:, gs], ss2[:, g:g + 1], gn[:, gs],
                      ALU.mult, ALU.mult, r=[ytm.b, ss2.b, gn.b], w=[ycst.b])
                yield

            def run_interleaved(gens):
                gens = [g for g in gens if g is not None]
                while gens:
                    for g in list(gens):
                        try:
                            next(g)
                        except StopIteration:
                            gens.remove(g)

            run_interleaved([Xgen(0)])
            for j in range(4):
                run_interleaved([Ygen(j), Xgen(j + 1) if j < 3 else None])
            transpose_rows(ycst, ycT, 8, 4, tps, ident)
            ph.store(ycT, y1[0:1024, tsl].rearrange("(c p) t -> p c t", p=128), ycT[:])
        ph.close()

    def phase_E(prefetch=None):
        ph = Phase("E")
        sb, ps = ph.sb, ph.ps
        ident = sb("ident", [128, 128], BF16)
        ph.load_w(ident, ident[:], C["c_ident"])
        mbs = sb("mbs", [128, 512], BF16)
        ph.load_w(mbs, mbs[:], C["c_mbs"])
        smk = sb("smk", [128, 128], BF16)
        ph.load_w(smk, smk[:], C["c_sm"])
        negtri = sb("negtri", [128, 128], BF16)
        ph.load_w(negtri, negtri[:], C["c_negtri"])
        negones = sb("negones", [128, 128], BF16)
        I("vector", "memset", negones[:], -1.0, w=[negones.b])
        Vall = sb("Vall", [128, 32, 512], BF16)
        for q4 in range(4):
            ph.load(Vall, Vall[:, q4 * 8:(q4 + 1) * 8, :], v_sb[q4 * 8:(q4 + 1) * 8].rearrange("j p f -> p j f"),
                    group=True)
        KT = [sb("KT%d" % i, [128, S], BF16) for i in range(2)]
        QT = [sb("QT%d" % i, [128, S], BF16) for i in range(2)]
        for i in range(2):
            I("gpsimd", "memset", KT[i][64:128, :], 0.0, w=[KT[i].b])
            I("vector", "memset", QT[i][64:128, :], 0.0, w=[QT[i].b])
        Rb = sb("Rb", [128, 512], BF16)
        yst = [sb("yst%d" % i, [64, 512], BF16) for i in range(2)]
        pzs = [ps("pz%d" % i, [128, 512], F32) for i in range(2)]
        dps = ps("dps", [128, 512], F32)

        def filler():
            mm(dps[:, 0:512], negtri[:], mbs[:, 0:512], True, True, r=[negtri.b, mbs.b], w=[dps.b], inc=False)

        p2s = [ps("p2%d" % i, [128, 512], F32) for i in range(3)]
        accs = [ps("acc%d" % i, [128, 512], F32) for i in range(2)]
        it = 0
        gi = 0
        pend1, pend2, pend3 = [], [], []

        def flush(lst, keep):
            while len(lst) > keep:
                lst.pop(0)()

        e3 = [sb("e3_%d" % i, [128, 512], F32) for i in range(6)]
        sp3 = [sb("sp3_%d" % i, [128, 512], BF16) for i in range(6)]
        wf3 = [sb("wf3_%d" % i, [128, 512], F32) for i in range(6)]
        wt3 = [sb("wt3_%d" % i, [128, 512], BF16) for i in range(6)]
        def load_head(h_):
            ph.load(KT[h_ % 2], KT[h_ % 2][0:64, :], kt_sb[h_ // 2, (h_ % 2) * 64:(h_ % 2) * 64 + 64, :])
            ph.load(QT[h_ % 2], QT[h_ % 2][0:64, :], qt_sb[h_ // 2, (h_ % 2) * 64:(h_ % 2) * 64 + 64, :])

        load_head(0)
        if prefetch is not None:
            prefetch(ph)
        for w_ in range(32):
            mm(p2s[0][:, 0:512], negtri[:], QT[0][:, 0:512], True, True, r=[KT[0].b, QT[0].b, Vall.b, negtri.b,
               mbs.b, smk.b, ident.b], w=[p2s[0].b], inc=(w_ == 31))
        for hh in range(8):
            kt_, qt_ = KT[hh % 2], QT[hh % 2]
            for G in range(8):
                if G == 1 and hh + 1 < 8:
                    load_head(hh + 1)
                acc = accs[gi % 2]
                ys = yst[gi % 2]
                gi += 1
                for j in range(4 * G + 3, -1, -1):
                    qlo, qhi = max(4 * G, j), 4 * G + 3
                    c0 = (qlo - 4 * G) * 128
                    n = (qhi - qlo + 1) * 128
                    cs_ = slice(c0, c0 + n)
                    pz, p2 = pzs[it % 2], p2s[it % 3]
                    e_, sp_, wf, wt = e3[it % 6], sp3[it % 6], wf3[it % 6], wt3[it % 6]
                    it += 1
                    kb = kt_[:, j * 128:(j + 1) * 128]
                    qc = qt_[:, qlo * 128:(qhi + 1) * 128]
                    diag = (qlo == j)
                    first = (j == 4 * G + 3)
                    mm(pz[:, cs_], kb, qc, True, True, r=[kt_.b, qt_.b], w=[pz.b])
                    filler()
                    I("scalar", "activation", e_[:, cs_], pz[:, cs_], AF.Exp, r=[pz.b], w=[e_.b])

                    def st1b(sp_=sp_, e_=e_, cs_=cs_, diag=diag, c0=c0):
                        I("scalar", "activation", sp_[:, cs_], e_[:, cs_], AF.Ln, bias=1.0, r=[e_.b], w=[sp_.b])
                        if diag:
                            I("gpsimd", "tensor_tensor", sp_[:, c0:c0 + 128], sp_[:, c0:c0 + 128], smk[:], ALU.mult,
                              r=[sp_.b, smk.b], w=[sp_.b])
                    pend1.append(st1b)
                    flush(pend1, 1)

                    def st2(p2=p2, cs_=cs_, sp_=sp_, first=first, diag=diag, n=n, wf=wf, wt=wt, e_=e_, j=j):
                        if first:
                            I("gpsimd", "memset", Rb[:], 0.0, w=[Rb.b])
                        useR = not first
                        mm(p2[:, cs_], negtri[:], sp_[:, cs_], True, not (useR or diag), r=[negtri.b, sp_.b],
                           w=[p2.b])
                        if useR:
                            mm(p2[:, cs_], negones[:], Rb[:, cs_], False, not diag, r=[negones.b, Rb.b], w=[p2.b])
                        if diag:
                            mm(p2[:, cs_], ident[:], mbs[:, 0:n], False, True, r=[ident.b, mbs.b], w=[p2.b])
                        if j > 0:
                            I("vector", "tensor_tensor", Rb[:, cs_], Rb[:, cs_], sp_[:, cs_], ALU.add,
                              r=[Rb.b, sp_.b], w=[Rb.b])
                        I("scalar", "activation", wf[:, cs_], p2[:, cs_], AF.Exp, r=[p2.b], w=[wf.b])
                        I("vector", "tensor_tensor", wt[:, cs_], wf[:, cs_], e_[:, cs_], ALU.mult,
                          r=[wf.b, e_.b], w=[wt.b])

                    def st3(acc=acc, cs_=cs_, j=j, hh=hh, wt=wt, first=first, ys=ys, G=G):
                        lastj = (j == 0)
                        v0 = min(hh * 64, 384)
                        r0 = hh * 64 - v0
                        I("tensor", "matmul", acc[:, cs_], Vall[:, j, v0:v0 + 128], wt[:, cs_],
                          start=first, stop=lastj, skip_group_check=True, r=[wt.b, Vall.b], w=[acc.b], inc=lastj)
                        filler()
                        if lastj:
                            cp("vector", ys[:], acc[r0:r0 + 64, 0:512], r=[acc.b], w=[ys.b])
                            col = 1024 + hh * 64
                            ph.store(ys, y1[col:col + 64, G * 512:(G + 1) * 512], ys[:])
                    pend2.append(st2)
                    flush(pend2, 3)
                    pend3.append(st3)
                    flush(pend3, 4)
        flush(pend1, 0)
        flush(pend2, 0)
        flush(pend3, 0)
        ph.close()

    def run_all():
        phase_A()
        phase_B_dsw()
        with ExitStack() as st0:
            pre_alloc(st0, "C0_Wout", [128, 8, 1024], BF16)
            pre_alloc(st0, "C0_Wg", [128, 8, FF], BF16)
            pre_alloc(st0, "C0_Wu", [128, 8, FF], BF16)
            phase_B_mla(prefetch=lambda ph: pre_load_C(ph, 0, ("Wout",)))
            phase_C(0)
        with ExitStack() as st1:
            pre_alloc(st1, "D_Win", [128, 8, 4112], BF16)
            phase_Cd(0, prefetch=pre_load_D)
            phase_D()
        with ExitStack() as st2:
            pre_alloc(st2, "C1_Wg", [128, 8, FF], BF16)
            pre_alloc(st2, "C1_Wu", [128, 8, FF], BF16)
            phase_E(prefetch=lambda ph: pre_load_C(ph, 1, ()))
            phase_C(1)
        phase_Cd(1)

    run_all()
    es.close()
    return nc


_CACHE = {}


def kernel(**inputs):
    stop_after = os.environ.get("K_STOP") or None
    dbg = bool(os.environ.get("K_DBG"))
    nc = build(stop_after=stop_after, dbg=dbg)
    consts = host_consts()
    in_maps = []
    for b in range(8):
        m = {"x": np.ascontiguousarray(inputs["x"][b], dtype=np.float32),
             "positions": np.ascontiguousarray(inputs["positions"][b], dtype=np.int32)}
        for n, _ in W_NAMES:
            m[n] = np.ascontiguousarray(inputs[n], dtype=np.float32)
        m.update(consts)
        in_maps.append(m)
    res = run_bass_kernel_spmd(nc, in_maps, core_ids=list(range(8)))
    if dbg:
        _CACHE["res"] = res.results
    return np.stack([res.results[b]["out"] for b in range(8)], axis=0).astype(np.float32)
```
